# Optimizing a Trainium2 kernel written in Bass

```python
import math
import jax, jax.numpy as jnp
from jax import lax
import numpy as np

D_MODEL = 2048
BATCH = 2
SEQ = 4096
DEPTH = 4

N_META = 16
ATTN_HEADS = 16
ATTN_HEAD_DIM = 64
ATTN_WIDTH = ATTN_HEADS * ATTN_HEAD_DIM
Q_BLOCK = 128
CONV_WIDTH = D_MODEL - ATTN_WIDTH
CONV_K = 3
PROJ_WIDTH = 3 * ATTN_WIDTH + ATTN_HEADS + 3 * CONV_WIDTH
S5_GROUP = 16
S5_GROUPS = D_MODEL // S5_GROUP
S5_STATE = 64
S5_MIN_DECAY = 1e-4
S5_DT_MIN = 1e-3
S5_DT_MAX = 1e-1
FFN_HIDDEN = int(math.ceil(8 * D_MODEL / 3 / 256) * 256)
NORM_EPS = 1e-6
N_EVEN = (DEPTH + 1) // 2
N_ODD = DEPTH // 2

kernel_name = "fox_shortconv_s5_hybrid_trunk"


def rms_norm(x, g):
    xf = x.astype(jnp.float32)
    y = xf * lax.rsqrt(jnp.mean(xf * xf, axis=-1, keepdims=True) + NORM_EPS)
    return (y * g.astype(jnp.float32)).astype(x.dtype)


def fox_attention(q, k, v, log_f):
    L = q.shape[1]
    seq_real = L - N_META
    c = jnp.transpose(jnp.cumsum(log_f, axis=1), (0, 2, 1))
    scale = ATTN_HEAD_DIM ** -0.5
    blocks = [(0, N_META)] + [(N_META + i * Q_BLOCK, Q_BLOCK) for i in range(seq_real // Q_BLOCK)]
    outs = []
    for start, size in blocks:
        end = start + size
        qb = q[:, start:end]
        kb = k[:, :end]
        vb = v[:, :end]
        s = jnp.einsum('bqhd,bkhd->bhqk', qb, kb).astype(jnp.float32) * scale
        s = s + c[:, :, start:end][..., None] - c[:, :, None, :end]
        qpos = jnp.arange(start, end)[:, None]
        kpos = jnp.arange(end)[None, :]
        s = jnp.where(kpos <= qpos, s, -jnp.inf)
        p = jax.nn.softmax(s, axis=-1).astype(v.dtype)
        outs.append(jnp.einsum('bhqk,bkhd->bqhd', p, vb))
    return jnp.concatenate(outs, axis=1)


def attn_conv_mixer(h, w_in, b_f, conv_w, conv_b, w_o):
    Bsz, L, _ = h.shape
    proj = h @ w_in
    cuts = [ATTN_WIDTH, 2 * ATTN_WIDTH, 3 * ATTN_WIDTH, 3 * ATTN_WIDTH + ATTN_HEADS,
            3 * ATTN_WIDTH + ATTN_HEADS + CONV_WIDTH, 3 * ATTN_WIDTH + ATTN_HEADS + 2 * CONV_WIDTH]
    q, k, v, fg, gate_b, gate_c, xc = jnp.split(proj, cuts, axis=-1)
    log_f = jax.nn.log_sigmoid((fg + b_f).astype(jnp.float32))
    hshape = (Bsz, L, ATTN_HEADS, ATTN_HEAD_DIM)
    attn = fox_attention(q.reshape(hshape), k.reshape(hshape), v.reshape(hshape), log_f)
    attn = attn.reshape(Bsz, L, ATTN_WIDTH)
    z = gate_c * xc
    zp = jnp.pad(z, ((0, 0), (CONV_K - 1, 0), (0, 0)))
    conv = sum(conv_w[j] * zp[:, j:j + L] for j in range(CONV_K)) + conv_b
    conv_out = gate_b * conv
    return jnp.concatenate([attn, conv_out], axis=-1) @ w_o


def _complex_affine_combine(e1, e2):
    a1r, a1i, b1r, b1i = e1
    a2r, a2i, b2r, b2i = e2
    ar = a2r * a1r - a2i * a1i
    ai = a2r * a1i + a2i * a1r
    br = a2r * b1r - a2i * b1i + b2r
    bi = a2r * b1i + a2i * b1r + b2i
    return (ar, ai, br, bi)


def s5_mixer(u, a_re, a_im, log_step, b_re, b_im, c_re, c_im, d_skip, w_glu1, w_glu2):
    Bsz, L, _ = u.shape
    f32 = jnp.float32
    lam_re = jnp.minimum(a_re.astype(f32), -S5_MIN_DECAY)
    lam_im = a_im.astype(f32)
    delta = jnp.exp(log_step.astype(f32))[:, None]
    mag = jnp.exp(lam_re * delta)
    ang = lam_im * delta
    lb_re = mag * jnp.cos(ang)
    lb_im = mag * jnp.sin(ang)
    den = lam_re * lam_re + lam_im * lam_im
    nr = lb_re - 1.0
    ni = lb_im
    coef_re = (nr * lam_re + ni * lam_im) / den
    coef_im = (ni * lam_re - nr * lam_im) / den
    br_ = b_re.astype(f32)
    bi_ = b_im.astype(f32)
    bb_re = coef_re[..., None] * br_ - coef_im[..., None] * bi_
    bb_im = coef_re[..., None] * bi_ + coef_im[..., None] * br_
    uf = u.astype(f32)
    ug = uf.reshape(Bsz, L, S5_GROUPS, S5_GROUP)
    bu_re = jnp.einsum('blgh,gph->blgp', ug, bb_re)
    bu_im = jnp.einsum('blgh,gph->blgp', ug, bb_im)
    a_r = jnp.broadcast_to(lb_re, bu_re.shape)
    a_i = jnp.broadcast_to(lb_im, bu_re.shape)
    _, _, x_re, x_im = lax.associative_scan(_complex_affine_combine, (a_r, a_i, bu_re, bu_im), axis=1)
    y = (jnp.einsum('blgp,ghp->blgh', x_re, c_re.astype(f32))
         - jnp.einsum('blgp,ghp->blgh', x_im, c_im.astype(f32)))
    y = y.reshape(Bsz, L, D_MODEL) + d_skip.astype(f32) * uf
    g = jax.nn.gelu(y).astype(u.dtype)
    return (g @ w_glu1) * jax.nn.sigmoid(g @ w_glu2)


def swiglu_ffn(h, w_gate, w_up, w_down):
    return (jax.nn.silu(h @ w_gate) * (h @ w_up)) @ w_down


def setup_inputs(seed: int = 0) -> dict:
    key = jax.random.key(seed)
    ks = jax.random.split(key, 24)
    f32 = jnp.float32
    D = D_MODEL

    def nrm(k, shape, scale):
        return jax.random.normal(k, shape, f32) * scale

    x = nrm(ks[0], (BATCH, SEQ, D), 1.0)
    meta_tokens = nrm(ks[1], (N_META, D), 1.0)
    norm_g = 1.0 + nrm(ks[2], (DEPTH, 4, D), 0.02)
    ab_w_in = nrm(ks[3], (N_EVEN, D, PROJ_WIDTH), D ** -0.5)
    ab_b_f = 3.0 + nrm(ks[4], (N_EVEN, ATTN_HEADS), 0.5)
    ab_conv_w = nrm(ks[5], (N_EVEN, CONV_K, CONV_WIDTH), CONV_K ** -0.5)
    ab_conv_b = nrm(ks[6], (N_EVEN, CONV_WIDTH), 0.02)
    ab_w_o = nrm(ks[7], (N_EVEN, D, D), D ** -0.5)
    s5_a_re = -0.5 + nrm(ks[8], (N_ODD, S5_GROUPS, S5_STATE), 0.01)
    s5_a_im = (jnp.pi * jnp.arange(S5_STATE, dtype=f32))[None, None, :] + nrm(ks[9], (N_ODD, S5_GROUPS, S5_STATE), 0.01)
    s5_log_step = jax.random.uniform(ks[10], (N_ODD, S5_GROUPS), f32, math.log(S5_DT_MIN), math.log(S5_DT_MAX))
    s5_b_re = nrm(ks[11], (N_ODD, S5_GROUPS, S5_STATE, S5_GROUP), (2 * S5_GROUP) ** -0.5)
    s5_b_im = nrm(ks[12], (N_ODD, S5_GROUPS, S5_STATE, S5_GROUP), (2 * S5_GROUP) ** -0.5)
    s5_c_re = nrm(ks[13], (N_ODD, S5_GROUPS, S5_GROUP, S5_STATE), (2 * S5_STATE) ** -0.5)
    s5_c_im = nrm(ks[14], (N_ODD, S5_GROUPS, S5_GROUP, S5_STATE), (2 * S5_STATE) ** -0.5)
    s5_d = nrm(ks[15], (N_ODD, D), 1.0)
    s5_w_glu1 = nrm(ks[16], (N_ODD, D, D), D ** -0.5)
    s5_w_glu2 = nrm(ks[17], (N_ODD, D, D), D ** -0.5)
    ffn_w_gate = nrm(ks[18], (DEPTH, D, FFN_HIDDEN), D ** -0.5)
    ffn_w_up = nrm(ks[19], (DEPTH, D, FFN_HIDDEN), D ** -0.5)
    ffn_w_down = nrm(ks[20], (DEPTH, FFN_HIDDEN, D), FFN_HIDDEN ** -0.5)
    return {"x": x, "meta_tokens": meta_tokens, "norm_g": norm_g,
            "ab_w_in": ab_w_in, "ab_b_f": ab_b_f, "ab_conv_w": ab_conv_w, "ab_conv_b": ab_conv_b, "ab_w_o": ab_w_o,
            "s5_a_re": s5_a_re, "s5_a_im": s5_a_im, "s5_log_step": s5_log_step,
            "s5_b_re": s5_b_re, "s5_b_im": s5_b_im, "s5_c_re": s5_c_re, "s5_c_im": s5_c_im,
            "s5_d": s5_d, "s5_w_glu1": s5_w_glu1, "s5_w_glu2": s5_w_glu2,
            "ffn_w_gate": ffn_w_gate, "ffn_w_up": ffn_w_up, "ffn_w_down": ffn_w_down}


def reference(x, meta_tokens, norm_g, ab_w_in, ab_b_f, ab_conv_w, ab_conv_b, ab_w_o,
              s5_a_re, s5_a_im, s5_log_step, s5_b_re, s5_b_im, s5_c_re, s5_c_im,
              s5_d, s5_w_glu1, s5_w_glu2, ffn_w_gate, ffn_w_up, ffn_w_down):
    Bsz = x.shape[0]
    meta = jnp.broadcast_to(meta_tokens.astype(x.dtype)[None], (Bsz, N_META, D_MODEL))
    h = jnp.concatenate([meta, x], axis=1)
    for i in range(DEPTH):
        u = rms_norm(h, norm_g[i, 0])
        if i % 2 == 0:
            j = i // 2
            m = attn_conv_mixer(u, ab_w_in[j], ab_b_f[j], ab_conv_w[j], ab_conv_b[j], ab_w_o[j])
        else:
            j = i // 2
            m = s5_mixer(u, s5_a_re[j], s5_a_im[j], s5_log_step[j], s5_b_re[j], s5_b_im[j],
                         s5_c_re[j], s5_c_im[j], s5_d[j], s5_w_glu1[j], s5_w_glu2[j])
        h = h + rms_norm(m, norm_g[i, 1])
        f = swiglu_ffn(rms_norm(h, norm_g[i, 2]), ffn_w_gate[i], ffn_w_up[i], ffn_w_down[i])
        h = h + rms_norm(f, norm_g[i, 3])
    return h[:, N_META:]
```

```python
import contextlib
import math
import numpy as np
import concourse.bass as bass
import concourse.mybir as mybir
from concourse.bass_utils import run_bass_kernel_spmd

IMPLEMENTED_STAGES = ("even", "s5", "ffn")

F32 = mybir.dt.float32
BF16 = mybir.dt.bfloat16
AF = mybir.ActivationFunctionType
ALU = mybir.AluOpType
ENGS = ("pe", "act", "dve", "pool", "sp")
SAME_ENGINE_UNSYNCED = ("pe",)


class Sched:
    _seg = [0]

    def __init__(self, nc):
        self.nc = nc
        self.ops = []
        self.sem_handles = []

    def add(self, eng, fn, reads=(), writes=(), dma=False, semkey=None):
        if dma and semkey is None:
            semkey = writes[0]
        self.ops.append((eng, fn, tuple(reads), tuple(writes), dma, semkey))

    def dma(self, eng, out, in_, reads=(), writes=(), semkey=None):
        self.add(eng, lambda e: e.dma_start(out=out, in_=in_), reads, writes, True, semkey)

    def emit(self):
        nc, ops = self.nc, self.ops
        n = len(ops)
        last_w, readers = {}, {}
        deps = [None] * n
        has_dep = [False] * n
        for i, (eng, fn, rds, wrs, dma, sk) in enumerate(ops):
            d = set()
            for r in rds:
                if r in last_w:
                    d.add(last_w[r])
            for w in wrs:
                if w in last_w:
                    d.add(last_w[w])
                d.update(readers.get(w, ()))
            d.discard(i)
            d = {j for j in d if not (ops[j][0] == eng and eng in SAME_ENGINE_UNSYNCED and not ops[j][4] and not dma)}
            deps[i] = d
            for j in d:
                has_dep[j] = True
            for r in rds:
                readers.setdefault(r, []).append(i)
            for w in wrs:
                last_w[w] = i
                readers[w] = []
        eng_cnt = {e: 0 for e in ENGS}
        dma_cnt = {}
        ticket = [None] * n
        for i, (eng, fn, rds, wrs, dma, sk) in enumerate(ops):
            if dma:
                dma_cnt[sk] = dma_cnt.get(sk, 0) + 16
                ticket[i] = (("d", sk), dma_cnt[sk])
            elif has_dep[i]:
                eng_cnt[eng] += 1
                ticket[i] = (("e", eng), eng_cnt[eng])
        semnames = [("e", e) for e in ENGS if eng_cnt[e]] + [("d", k) for k in dma_cnt]
        self.nsems = len(semnames)
        self.sem_handles = [nc.alloc_semaphore(name="s%d_%d" % (Sched._seg[0], i)) for i in range(len(semnames))]
        Sched._seg[0] += 1
        sems = dict(zip(semnames, self.sem_handles))
        with nc.Block() as block:
            per_eng = {e: [i for i in range(n) if ops[i][0] == e] for e in ENGS}

            def run(eng_name, e):
                waited = {}
                for i in per_eng[eng_name]:
                    need = {}
                    for j in deps[i]:
                        sn, v = ticket[j]
                        if need.get(sn, 0) < v:
                            need[sn] = v
                    for sn, v in need.items():
                        if waited.get(sn, 0) < v:
                            e.wait_ge(sems[sn], v)
                            waited[sn] = v
                    ins = ops[i][1](e)
                    if ticket[i] is not None:
                        ins.then_inc(sems[ticket[i][0]], 16 if ops[i][4] else 1)
                for i in per_eng[eng_name]:
                    if ops[i][4]:
                        sn = ticket[i][0]
                        tot = dma_cnt[ops[i][5]]
                        if waited.get(sn, 0) < tot:
                            e.wait_ge(sems[sn], tot)
                            waited[sn] = tot

            for name, reg in (("sp", block.sync), ("pe", block.tensor), ("act", block.scalar),
                              ("dve", block.vector), ("pool", block.gpsimd)):
                if per_eng[name]:
                    reg(lambda e, name=name: run(name, e))


def host_constants():
    f = np.float32
    kk, mm = np.arange(128)[:, None], np.arange(128)[None, :]
    bsel = np.zeros((128, 128), f)
    bsel[64, :64] = 1
    bsel[0, 64:] = 1
    masks = [(128 * j + np.arange(128)[:, None] <= np.arange(384)[None, :]).astype(f) for j in range(3)]
    return np.ascontiguousarray(np.concatenate([(kk <= mm).astype(f), np.ones((128, 128), f),
                                                np.broadcast_to(kk == 127, (128, 128)).astype(f), bsel] + masks, 1))


def host_constants2():
    r = np.arange(128)
    tmask = ((r[None, :] // 16) >= (r[:, None] // 16)).astype(np.float32)
    return np.ascontiguousarray(np.concatenate([tmask, np.eye(128, dtype=np.float32)], 1))


class Cfg:
    def __init__(self, D=2048, FFN=5632, LP=4224, NT=352, NSEQ=1, DEPTH=4):
        self.D, self.FFN, self.LP, self.NT, self.NSEQ, self.DEPTH = D, FFN, LP, NT, NSEQ, DEPTH
        self.TB = 3 * NT
        assert LP % self.TB == 0 and D % 256 == 0 and FFN % 256 == 0 and LP % 384 == 0
        self.NB = LP // self.TB
        self.KC = D // 128
        self.HC = FFN // 128
        self.AW = D // 2
        self.CW = D // 2
        self.NH = self.AW // 64
        self.AC = self.AW // 128
        self.CC = self.CW // 128
        self.PW = 3 * self.AW + self.NH + 3 * self.CW
        self.NKT = LP // 128
        self.NQB = LP // 384
        self.TT = -(-self.TB // 128)
        self.G = D // 16
        self.GB = 16
        self.NCH = LP // 8
        self.TBC = self.TB // 8
        assert self.TB % 8 == 0 and self.G % self.GB == 0 and self.NCH % 2 == 0 and self.NCH // 2 <= 512
        self.EPS = 1e-6


class Phase:
    _uid = [0]

    def __init__(self, nc):
        self.nc = nc
        self.st = contextlib.ExitStack()
        self.s = Sched(nc)

    def _name(self, pfx):
        Phase._uid[0] += 1
        return "%s%d" % (pfx, Phase._uid[0])

    def sb(self, shape, dtype):
        return self.st.enter_context(self.nc.sbuf_tensor(self._name("t"), list(shape), dtype))

    def ps(self, shape=(128, 512), dtype=F32):
        return self.st.enter_context(self.nc.psum_tensor(self._name("p"), list(shape), dtype))

    def close(self):
        self.s.emit()
        self.st.close()
        self.nc.all_engine_barrier()
        self.nc.clear_and_free_semaphores(self.s.sem_handles)
        self.nc.all_engine_barrier()


class Builder:
    def __init__(self, cfg, stages, dbg=False):
        self.c = c = cfg
        self.nc = nc = bass.Bass("TRN2", target_bir_lowering=False)
        self.stages = stages
        self.dbg = dbg
        D, LP = c.D, c.LP
        dt = nc.dram_tensor
        NE = (c.DEPTH + 1) // 2
        ext = lambda name, shape: dt(name, list(shape), F32, kind="ExternalInput").ap()
        self.xT = ext("xT", [c.NSEQ, D, LP])
        self.ng = ext("ng", [128, c.DEPTH * 4 * c.KC])
        self.wg = ext("wg", [c.DEPTH, D, c.FFN])
        self.wu = ext("wu", [c.DEPTH, D, c.FFN])
        self.wd = ext("wd", [c.DEPTH, c.FFN, D])
        self.win = ext("win", [NE, D, c.PW])
        self.wo = ext("wo", [NE, D, D])
        self.bf = ext("bf", [NE, 128, c.NH])
        self.cw = ext("cw", [NE, 128, c.CC * 4])
        self.cst = ext("cst", [128, 4 * 128 + 3 * 384])
        NO = max(c.DEPTH // 2, 1)
        self.s5p = ext("s5p", [NO, 64, 3, c.G])
        self.s5b = ext("s5b", [NO, 64, 2, c.G, 16])
        self.s5c = ext("s5c", [NO, 64, 2, c.G, 16])
        self.s5d = ext("s5d", [128, NO * c.KC])
        self.wq1 = ext("wq1", [NO, D, D])
        self.wq2 = ext("wq2", [NO, D, D])
        self.cst2 = ext("cst2", [128, 256])
        self.out = dt("out", [c.NSEQ, D, LP], F32, kind="ExternalOutput").ap()
        self.UT = dt("UT", [D, LP], BF16).ap()
        self.U8D = dt("U8D", [c.KC, 8, 128, c.NCH], BF16).ap()
        self.Y8D = dt("Y8D", [c.KC, 8, 128, c.NCH], BF16).ap()
        self.H = dt("H", [D, LP], F32).ap()
        self.Fs = dt("Fs", [D, LP], F32).ap()
        self.QT = dt("QT", [c.AW, LP], BF16).ap()
        self.KT = dt("KT", [c.AW, LP], BF16).ap()
        self.GX = dt("GX", [3, c.CW, LP], BF16).ap()
        self.VT = dt("VT", [LP, c.AW], BF16).ap()
        self.FG = dt("FG", [LP, c.NH], F32).ap()
        self.CAT = dt("CAT", [D, LP], BF16).ap()
        if dbg:
            self.hdump = dt("hdump", [c.NSEQ, c.DEPTH, D, LP], F32, kind="ExternalOutput").ap()
            self.dbg_out = {n: dt("d" + n, list(a.shape), a.dtype, kind="ExternalOutput").ap()
                            for n, a in (("QT", self.QT), ("KT", self.KT), ("GX", self.GX), ("VT", self.VT), ("FG", self.FG), ("CAT", self.CAT),
                                         ("UT", self.UT), ("U8D", self.U8D), ("Y8D", self.Y8D))}
        sb = nc.alloc_sbuf_tensor
        self.ngs = sb("ngs", [128, c.DEPTH * 4 * c.KC], F32)
        self.ds = sb("ds", [128, NO * c.KC], F32)
        self.ones = sb("ones", [128, 128], BF16)
        p = Phase(nc)
        p.s.dma("sp", self.ngs[:, :], self.ng, writes=["ngs"])
        p.s.dma("sp", self.ds[:, :], self.s5d, writes=["ds"])
        p.s.add("dve", lambda e: e.memset(self.ones[:, :], 1.0), writes=["ones"])
        p.close()


    def gcol(self, layer, j, kc):
        i = (layer * 4 + j) * self.c.KC + kc
        return self.ngs[:, i:i + 1]

    def dkey(self, name, k, t0):
        return ("D", name, k, t0)

    def norm_bufs(self, p):
        c = self.c
        p.ch = [p.sb([128, c.TB], F32) for _ in range(4)]
        p.sq = [p.sb([128, c.NT], BF16) for _ in range(2)]
        p.rstd = p.sb([128, c.TB], F32)
        p.pS = p.ps()

    def rms_stats(self, p, src, sname, t0):
        c, s = self.c, p.s
        for n in range(3):
            n0 = n * c.NT
            for k in range(c.KC):
                slot = k % 4
                s.dma("sp", p.ch[slot][:, 0:c.NT], src[k * 128:(k + 1) * 128, t0 + n0:t0 + n0 + c.NT],
                      reads=[self.dkey(sname, k, t0)], writes=[("ch", slot)], semkey=("ch", slot))
                s.add("act", lambda e, slot=slot, k=k: e.activation(out=p.sq[k % 2][:, :], in_=p.ch[slot][:, 0:c.NT], func=AF.Square),
                      reads=[("ch", slot)], writes=[("sq", k % 2)])
                s.add("pe", lambda e, k=k: e.matmul(p.pS[:, 0:c.NT], lhsT=self.ones[:, :], rhs=p.sq[k % 2][:, :],
                                                    start=(k == 0), stop=(k == c.KC - 1)),
                      reads=[("sq", k % 2)], writes=["pS"])
            s.add("dve", lambda e, n0=n0: e.tensor_scalar(out=p.rstd[:, n0:n0 + c.NT], in0=p.pS[:, 0:c.NT],
                                                         scalar1=1.0 / c.D, scalar2=c.EPS, op0=ALU.mult, op1=ALU.add),
                  reads=["pS"], writes=[("rstd", n)])
            s.add("act", lambda e, n0=n0: e.activation(out=p.rstd[:, n0:n0 + c.NT], in_=p.rstd[:, n0:n0 + c.NT], func=AF.Sqrt),
                  reads=[("rstd", n)], writes=[("rstd", n)])
            s.add("dve", lambda e, n0=n0: e.reciprocal(out=p.rstd[:, n0:n0 + c.NT], in_=p.rstd[:, n0:n0 + c.NT]),
                  reads=[("rstd", n)], writes=[("rstd", n)])

    def norm_to_xa(self, p, src, sname, t0, layer, j):
        c, s = self.c, p.s
        self.rms_stats(p, src, sname, t0)
        rk = [("rstd", n) for n in range(3)]
        for k in range(c.KC):
            slot = k % 4
            s.dma("sp", p.ch[slot][:, :], src[k * 128:(k + 1) * 128, t0:t0 + c.TB],
                  reads=[self.dkey(sname, k, t0)], writes=[("ch", slot)], semkey=("ch", slot))
            s.add("dve", lambda e, slot=slot, k=k: e.scalar_tensor_tensor(out=p.xa[:, k, :], in0=p.ch[slot][:, :], scalar=self.gcol(layer, j, k),
                                                                         in1=p.rstd[:, :], op0=ALU.mult, op1=ALU.mult),
                  reads=[("ch", slot)] + rk, writes=[("xa", k)])

    def norm_residual(self, p, src, sname, base, bname, t0, layer, j):
        c, s = self.c, p.s
        self.rms_stats(p, src, sname, t0)
        rk = [("rstd", n) for n in range(3)]
        for k in range(c.KC):
            a, b = (2 * k) % 4, (2 * k + 1) % 4
            s.dma("sp", p.ch[a][:, :], src[k * 128:(k + 1) * 128, t0:t0 + c.TB], reads=[self.dkey(sname, k, t0)],
                  writes=[("ch", a)], semkey=("ch", a))
            s.dma("sp", p.ch[b][:, :], base[k * 128:(k + 1) * 128, t0:t0 + c.TB], reads=[self.dkey(bname, k, t0)],
                  writes=[("ch", b)], semkey=("ch", b))
            s.add("dve", lambda e, a=a, k=k: e.scalar_tensor_tensor(out=p.ch[a][:, :], in0=p.ch[a][:, :], scalar=self.gcol(layer, j, k),
                                                                   in1=p.rstd[:, :], op0=ALU.mult, op1=ALU.mult),
                  reads=[("ch", a)] + rk, writes=[("ch", a)])
            s.add("dve", lambda e, a=a, b=b: e.tensor_tensor(out=p.ch[b][:, :], in0=p.ch[b][:, :], in1=p.ch[a][:, :], op=ALU.add),
                  reads=[("ch", a), ("ch", b)], writes=[("ch", b)])
            s.dma("sp", base[k * 128:(k + 1) * 128, t0:t0 + c.TB], p.ch[b][:, :], reads=[("ch", b)],
                  writes=[self.dkey(bname, k, t0)], semkey=("st", b))

    def linear_fm(self, p, wv, c0, nchunks, xin, xkeys, nk, wbufs, wname, epi):
        c, s = self.c, p.s
        for m in range(nchunks):
            sl = m % 2
            s.dma("pool", wbufs[sl][:, :, :], wv[:, :, c0 + m * 128:c0 + (m + 1) * 128], writes=[(wname, sl)])
            pp, pk = (p.pA, "pA") if m % 2 == 0 else (p.pB, "pB")
            for k in range(nk):
                for n in range(3):
                    s.add("pe", lambda e, pp=pp, k=k, n=n, sl=sl: e.matmul(
                        pp[n][:, 0:c.NT], lhsT=wbufs[sl][:, k, :], rhs=xin[:, k, n * c.NT:(n + 1) * c.NT],
                        start=(k == 0), stop=(k == nk - 1)),
                        reads=[(wname, sl)] + xkeys, writes=[(pk, n)])
            epi(m, pp, pk)

    def ffn_phase(self, layer, t0):
        c = self.c
        p = Phase(self.nc)
        s = p.s
        self.norm_bufs(p)
        p.xa = p.sb([128, c.KC, c.TB], BF16)
        hid = p.sb([128, c.HC, c.TB], BF16)
        wA = [p.sb([128, c.KC, 256], BF16) for _ in range(2)]
        wB = [p.sb([128, c.KC, 256], BF16) for _ in range(2)]
        wD = [p.sb([128, c.HC, 128], BF16) for _ in range(2)]
        sg = [p.sb([128, c.TB], BF16) for _ in range(2)]
        p.pA = [p.ps() for _ in range(3)]
        p.pB = [p.ps() for _ in range(3)]
        self.norm_to_xa(p, self.H, "H", t0, layer, 2)
        xk = [("xa", k) for k in range(c.KC)]
        wgv = self.wg[layer].rearrange("(k p) m -> p k m", p=128)
        wuv = self.wu[layer].rearrange("(k p) m -> p k m", p=128)
        wdv = self.wd[layer].rearrange("(k p) m -> p k m", p=128)
        for cb in range(c.FFN // 256):
            sl = cb % 2
            s.dma("pool", wA[sl][:, :, :], wgv[:, :, cb * 256:(cb + 1) * 256], writes=[("wA", sl)])
            s.dma("pool", wB[sl][:, :, :], wuv[:, :, cb * 256:(cb + 1) * 256], writes=[("wB", sl)])
            for mi in range(2):
                j = cb * 2 + mi
                for wbuf, wkey, pp, pk in ((wA, "wA", p.pA, "pA"), (wB, "wB", p.pB, "pB")):
                    for k in range(c.KC):
                        for n in range(3):
                            s.add("pe", lambda e, wbuf=wbuf, pp=pp, k=k, n=n, mi=mi, sl=sl: e.matmul(
                                pp[n][:, 0:c.NT], lhsT=wbuf[sl][:, k, mi * 128:(mi + 1) * 128],
                                rhs=p.xa[:, k, n * c.NT:(n + 1) * c.NT], start=(k == 0), stop=(k == c.KC - 1)),
                                reads=[(wkey, sl)] + xk, writes=[(pk, n)])
                    for n in range(3):
                        if wkey == "wA":
                            s.add("act", lambda e, n=n, j=j: e.activation(out=sg[j % 2][:, n * c.NT:(n + 1) * c.NT], in_=p.pA[n][:, 0:c.NT], func=AF.Silu),
                                  reads=[("pA", n)], writes=[("sg", j % 2, n)])
                        else:
                            s.add("dve", lambda e, n=n, j=j: e.tensor_tensor(out=hid[:, j, n * c.NT:(n + 1) * c.NT], in0=p.pB[n][:, 0:c.NT],
                                                                            in1=sg[j % 2][:, n * c.NT:(n + 1) * c.NT], op=ALU.mult),
                                  reads=[("pB", n), ("sg", j % 2, n)], writes=[("hid", j)])
        hk = [("hid", j) for j in range(c.HC)]

        def epi(m, pp, pk):
            slot = m % 4
            for n in range(3):
                s.add("act", lambda e, n=n: e.activation(out=p.ch[slot][:, n * c.NT:(n + 1) * c.NT], in_=pp[n][:, 0:c.NT], func=AF.Copy),
                      reads=[(pk, n)], writes=[("ch", slot)])
            s.dma("sp", self.Fs[m * 128:(m + 1) * 128, t0:t0 + c.TB], p.ch[slot][:, :], reads=[("ch", slot)],
                  writes=[self.dkey("Fs", m, t0)], semkey=("st", slot))
        self.linear_fm(p, wdv, 0, c.KC, hid, hk, c.HC, wD, "wD", epi)
        self.norm_residual(p, self.Fs, "Fs", self.H, "H", t0, layer, 3)
        p.close()

    def inproj_phase(self, layer, t0):
        c = self.c
        e_i = layer // 2
        p = Phase(self.nc)
        s = p.s
        self.norm_bufs(p)
        p.xa = p.sb([128, c.KC, c.TB], BF16)
        wP = [p.sb([128, c.KC, 128], BF16) for _ in range(2)]
        wV = p.sb([128, c.KC, 512], BF16)
        wF = p.sb([128, c.KC, c.NH], BF16)
        stb = [p.sb([128, c.TB], BF16) for _ in range(2)]
        vst = [p.sb([128, 512], BF16) for _ in range(2)]
        fst = [p.sb([128, c.NH], F32) for _ in range(2)]
        p.pA = [p.ps() for _ in range(3)]
        p.pB = [p.ps() for _ in range(3)]
        pF = p.ps()
        self.norm_to_xa(p, self.H, "H", t0, layer, 0)
        xk = [("xa", k) for k in range(c.KC)]
        wv = self.win[e_i].rearrange("(k p) m -> p k m", p=128)
        targets = [(self.QT, 0, c.AC), (self.KT, c.AW, c.AC)] + [(self.GX[i], 3 * c.AW + c.NH + i * c.CW, c.CC) for i in range(3)]
        cnt = [0]
        for dst, c0, nch in targets:
            def epi(m, pp, pk, dst=dst):
                sl = cnt[0] % 2
                cnt[0] += 1
                for n in range(3):
                    s.add("act", lambda e, n=n, sl=sl: e.activation(out=stb[sl][:, n * c.NT:(n + 1) * c.NT], in_=pp[n][:, 0:c.NT], func=AF.Copy),
                          reads=[(pk, n)], writes=[("stb", sl)])
                s.dma("sp", dst[m * 128:(m + 1) * 128, t0:t0 + c.TB], stb[sl][:, :], reads=[("stb", sl)], writes=[("o", "fm")], semkey=("stb", sl))
            self.linear_fm(p, wv, c0, nch, p.xa, xk, c.KC, wP, "wP", epi)
        s.dma("pool", wF[:, :, :], wv[:, :, 3 * c.AW:3 * c.AW + c.NH], writes=["wF"])
        for vb in range(-(-c.AW // 512)):
            vw = min(512, c.AW - vb * 512)
            s.dma("pool", wV[:, :, 0:vw], wv[:, :, 2 * c.AW + vb * 512:2 * c.AW + vb * 512 + vw], writes=["wV"])
            for tt in range(c.TT):
                r0 = tt * 128
                rows = min(128, c.TB - r0)
                pp, pk = (p.pA[tt % 3], ("pA", tt % 3)) if (tt // 3) % 2 == 0 else (p.pB[tt % 3], ("pB", tt % 3))
                for k in range(c.KC):
                    s.add("pe", lambda e, pp=pp, k=k, r0=r0, rows=rows, vw=vw: e.matmul(
                        pp[0:rows, 0:vw], lhsT=p.xa[:, k, r0:r0 + rows], rhs=wV[:, k, 0:vw], start=(k == 0), stop=(k == c.KC - 1)),
                        reads=["wV"] + xk, writes=[pk])
                sl = tt % 2
                s.add("act", lambda e, pp=pp, rows=rows, vw=vw, sl=sl: e.activation(out=vst[sl][0:rows, 0:vw], in_=pp[0:rows, 0:vw], func=AF.Copy),
                      reads=[pk], writes=[("vst", sl)])
                s.dma("sp", self.VT[t0 + r0:t0 + r0 + rows, vb * 512:vb * 512 + vw], vst[sl][0:rows, 0:vw], reads=[("vst", sl)],
                      writes=[("o", "v")], semkey=("vst", sl))
                if vb == 0:
                    for k in range(c.KC):
                        s.add("pe", lambda e, k=k, r0=r0, rows=rows: e.matmul(
                            pF[0:rows, 0:c.NH], lhsT=p.xa[:, k, r0:r0 + rows], rhs=wF[:, k, :], start=(k == 0), stop=(k == c.KC - 1)),
                            reads=["wF"] + xk, writes=["pF"])
                    s.add("dve", lambda e, rows=rows, sl=sl: e.tensor_copy(out=fst[sl][0:rows, :], in_=pF[0:rows, 0:c.NH]),
                          reads=["pF"], writes=[("fst", sl)])
                    s.dma("sp", self.FG[t0 + r0:t0 + r0 + rows, :], fst[sl][0:rows, :], reads=[("fst", sl)], writes=[("o", "f")], semkey=("fst", sl))
        p.close()

    def conv_phase(self, layer):
        c = self.c
        e_i = layer // 2
        p = Phase(self.nc)
        s = p.s
        cwt = p.sb([128, c.CC * 4], F32)
        g = [[p.sb([128, c.LP], BF16) for _ in range(3)] for _ in range(2)]
        z = [p.sb([128, c.LP + 2], F32) for _ in range(2)]
        acc = [p.sb([128, c.LP], F32) for _ in range(2)]
        ob = [p.sb([128, c.LP], BF16) for _ in range(2)]
        s.dma("sp", cwt[:, :], self.cw[e_i], writes=["cwt"])
        for b in range(2):
            s.add("dve", lambda e, b=b: e.memset(z[b][:, 0:2], 0.0), writes=[("zpad", b)])
        for cc in range(c.CC):
            b = cc % 2
            for i in range(3):
                s.dma("sp", g[b][i][:, :], self.GX[i, cc * 128:(cc + 1) * 128, :], writes=[("g", b, i)])
            w = lambda j, cc=cc: cwt[:, cc * 4 + j:cc * 4 + j + 1]
            s.add("dve", lambda e, b=b: e.tensor_tensor(out=z[b][:, 2:], in0=g[b][1][:, :], in1=g[b][2][:, :], op=ALU.mult),
                  reads=[("g", b, 1), ("g", b, 2)], writes=[("z", b)])
            s.add("dve", lambda e, b=b, w=w: e.tensor_scalar(out=acc[b][:, :], in0=z[b][:, 2:], scalar1=w(2), scalar2=w(3), op0=ALU.mult, op1=ALU.add),
                  reads=[("z", b), "cwt"], writes=[("acc", b)])
            s.add("dve", lambda e, b=b, w=w: e.scalar_tensor_tensor(out=acc[b][:, :], in0=z[b][:, 1:c.LP + 1], scalar=w(1), in1=acc[b][:, :], op0=ALU.mult, op1=ALU.add),
                  reads=[("z", b), ("zpad", b), ("acc", b)], writes=[("acc", b)])
            s.add("dve", lambda e, b=b, w=w: e.scalar_tensor_tensor(out=acc[b][:, :], in0=z[b][:, 0:c.LP], scalar=w(0), in1=acc[b][:, :], op0=ALU.mult, op1=ALU.add),
                  reads=[("z", b), ("zpad", b), ("acc", b)], writes=[("acc", b)])
            s.add("dve", lambda e, b=b: e.tensor_tensor(out=ob[b][:, :], in0=acc[b][:, :], in1=g[b][0][:, :], op=ALU.mult),
                  reads=[("acc", b), ("g", b, 0)], writes=[("ob", b)])
            s.dma("sp", self.CAT[c.AW + cc * 128:c.AW + (cc + 1) * 128, :], ob[b][:, :], reads=[("ob", b)], writes=[("o", cc)], semkey=("ob", b))
        p.close()

    def attn_phase(self, layer):
        c = self.c
        e_i = layer // 2
        p = Phase(self.nc)
        s = p.s
        NKT, NQB, NH = c.NKT, c.NQB, c.NH
        cs = p.sb([128, 4 * 128 + 3 * 384], F32)
        msk = p.sb([128, 3, 384], BF16)
        tri, onesf, sel127, bsel = cs[:, 0:128], cs[:, 128:256], cs[:, 256:384], cs[:, 384:512]
        s.dma("sp", cs[:, :], self.cst, writes=["cs"])
        s.add("dve", lambda e: e.tensor_copy(out=msk[:, :, :], in_=cs[:, 512:512 + 1152].rearrange("p (j t) -> p j t", j=3)),
              reads=["cs"], writes=["msk"])
        bft = p.sb([128, NH], F32)
        lf = p.sb([128, NKT, NH], F32)
        tot = p.sb([128, NH], F32)
        cref = p.sb([128, 3, NH], F32)
        bias = [[p.sb([128, NKT, NH], F32) for _ in range(3)] for _ in range(2)]
        qt = [p.sb([128, c.LP], BF16) for _ in range(2)]
        kt_ = [p.sb([128, c.LP], BF16) for _ in range(2)]
        va = [p.sb([128, NKT, 192], BF16) for _ in range(2)]
        pt = [p.sb([128, 384], BF16) for _ in range(4)]
        rec = p.sb([128, 384], F32)
        recb = p.sb([128, 384], F32)
        ao = [p.sb([128, 384], BF16) for _ in range(2)]
        pS = [p.ps() for _ in range(3)]
        pO = [p.ps() for _ in range(2)]
        pY = p.ps()
        pC = p.ps()
        s.dma("sp", bft[:, :], self.bf[e_i], writes=["bft"])
        s.dma("sp", lf[:, :, :], self.FG.rearrange("(j p) h -> p j h", p=128), writes=["lf"])
        s.add("dve", lambda e: e.tensor_tensor(out=lf[:, :, :], in0=lf[:, :, :], in1=bft[:, :].unsqueeze(1).to_broadcast([128, NKT, NH]), op=ALU.add),
              reads=["lf", "bft"], writes=["lf"])
        s.add("act", lambda e: e.activation(out=lf[:, :, :], in_=lf[:, :, :], func=AF.Exp, scale=-1.0), reads=["lf"], writes=["lf"])
        s.add("act", lambda e: e.activation(out=lf[:, :, :], in_=lf[:, :, :], func=AF.Ln, bias=1.0), reads=["lf"], writes=["lf"])
        s.add("dve", lambda e: e.tensor_single_scalar(out=lf[:, :, :], in_=lf[:, :, :], scalar=-1.0, op=ALU.mult), reads=["lf"], writes=["lf"])
        s.add("dve", lambda e: e.memset(tot[:, :], 0.0), writes=["tot"])
        for j in range(NKT):
            s.add("pe", lambda e, j=j: e.matmul(pC[:, 0:NH], lhsT=tri, rhs=lf[:, j, :], start=True, stop=True), reads=["lf", ("lfj", j), "cs"], writes=["pC"])
            s.add("pe", lambda e, j=j: e.matmul(pC[:, NH:2 * NH], lhsT=onesf, rhs=lf[:, j, :], start=True, stop=True), reads=["lf", ("lfj", j), "cs"], writes=["pC2"])
            s.add("dve", lambda e, j=j: e.tensor_tensor(out=lf[:, j, :], in0=pC[:, 0:NH], in1=tot[:, :], op=ALU.add), reads=["pC", "tot"], writes=[("lfj", j)])
            s.add("dve", lambda e: e.tensor_tensor(out=tot[:, :], in0=pC[:, NH:2 * NH], in1=tot[:, :], op=ALU.add), reads=["pC2", "tot"], writes=["tot"])
        ckeys = [("lfj", j) for j in range(NKT)]
        for hp in range(c.AC):
            b = hp % 2
            s.dma("sp", qt[b][:, :], self.QT[hp * 128:(hp + 1) * 128, :], writes=[("qt", b)])
            s.dma("sp", kt_[b][:, :], self.KT[hp * 128:(hp + 1) * 128, :], writes=[("kt", b)])
            if hp < 2:
                s.add("dve", lambda e, b=b: e.memset(va[b][:, :, 64:128], 0.0), writes=[("vaE", b)])
                s.add("dve", lambda e, b=b: e.memset(va[b][:, :, 64:65], 1.0), reads=[("vaE", b)], writes=[("vaE", b)])
            vsrc = self.VT.rearrange("(j p) w -> p j w", p=128)
            s.dma("sp", va[b][:, :, 0:64], vsrc[:, :, hp * 128:hp * 128 + 64], writes=[("vaA", b)])
            s.dma("sp", va[b][:, :, 128:192], vsrc[:, :, hp * 128 + 64:hp * 128 + 128], writes=[("vaB", b)])
            for qb in range(NQB):
                q0 = qb * 384
                nkt = 3 * qb + 3
                bb = bias[qb % 2]
                for j2 in range(3):
                    nk2 = 3 * qb + j2 + 1
                    s.add("pe", lambda e, qb=qb, j2=j2: e.matmul(pC[:, (2 + j2) * NH:(3 + j2) * NH], lhsT=sel127, rhs=lf[:, 3 * qb + j2, :], start=True, stop=True),
                          reads=ckeys + ["cs"], writes=[("pC3", j2)])
                    s.add("dve", lambda e, j2=j2: e.tensor_copy(out=cref[:, j2, :], in_=pC[:, (2 + j2) * NH:(3 + j2) * NH]), reads=[("pC3", j2)], writes=[("cref", j2)])
                    s.add("dve", lambda e, bb=bb, j2=j2, nk2=nk2: e.tensor_tensor(out=bb[j2][:, 0:nk2, :], in0=cref[:, j2, :].unsqueeze(1).to_broadcast([128, nk2, NH]),
                                                                            in1=lf[:, 0:nk2, :], op=ALU.subtract),
                          reads=[("cref", j2)] + ckeys, writes=[("bias", qb % 2, j2)])
                for hh in range(2):
                    h = hp * 2 + hh
                    r0 = hh * 64
                    lcols = slice(0, 128) if hh == 0 else slice(64, 192)
                    for kt in range(nkt):
                        sp_ = pS[kt % 3]
                        pb = pt[kt % 4]
                        s.add("pe", lambda e, sp_=sp_, kt=kt, r0=r0, b=b, q0=q0: e.matmul(
                            sp_[:, 0:384], lhsT=kt_[b][r0:r0 + 64, kt * 128:(kt + 1) * 128], rhs=qt[b][r0:r0 + 64, q0:q0 + 384], start=True, stop=True),
                            reads=[("qt", b), ("kt", b)], writes=[("pS", kt % 3)])
                        jd = kt - 3 * qb
                        if jd > 0:
                            s.add("dve", lambda e, pb=pb, jd=jd: e.memset(pb[:, 0:128 * jd], 0.0), reads=[], writes=[("pt", kt % 4)])
                        for j2 in range(max(jd, 0), 3):
                            s.add("act", lambda e, sp_=sp_, pb=pb, bb=bb, kt=kt, h=h, j2=j2: e.activation(
                                out=pb[:, 128 * j2:128 * (j2 + 1)], in_=sp_[:, 128 * j2:128 * (j2 + 1)], func=AF.Exp, bias=bb[j2][:, kt, h:h + 1], scale=0.125),
                                reads=[("pS", kt % 3), ("bias", qb % 2, j2)], writes=[("pt", kt % 4)])
                        if kt >= 3 * qb:
                            s.add("dve", lambda e, pb=pb, kt=kt, qb=qb: e.tensor_tensor(out=pb[:, :], in0=pb[:, :], in1=msk[:, kt - 3 * qb, :], op=ALU.mult),
                                  reads=[("pt", kt % 4), "msk"], writes=[("pt", kt % 4)])
                        s.add("pe", lambda e, pb=pb, kt=kt, hh=hh, b=b, lcols=lcols, nkt=nkt: e.matmul(
                            pO[hh][:, 0:384], lhsT=va[b][:, kt, lcols], rhs=pb[:, :], start=(kt == 0), stop=(kt == nkt - 1)),
                            reads=[("pt", kt % 4), ("vaA", b), ("vaB", b), ("vaE", b)], writes=[("pO", hh)])
                s.add("dve", lambda e: e.reciprocal(out=rec[64:65, :], in_=pO[0][64:65, 0:384]), reads=[("pO", 0)], writes=["recA"])
                s.add("dve", lambda e: e.reciprocal(out=rec[0:1, :], in_=pO[1][0:1, 0:384]), reads=[("pO", 1)], writes=["recB"])
                s.add("pe", lambda e: e.matmul(pY[:, 0:384], lhsT=bsel[64:65, :], rhs=rec[64:65, :], start=True, stop=False),
                      reads=["recA", "cs"], writes=["pY"])
                s.add("pe", lambda e: e.matmul(pY[:, 0:384], lhsT=bsel[0:1, :], rhs=rec[0:1, :], start=False, stop=True),
                      reads=["recB", "cs"], writes=["pY"])
                s.add("act", lambda e: e.activation(out=recb[:, :], in_=pY[:, 0:384], func=AF.Copy), reads=["pY"], writes=["recb"])
                a_ = ao[qb % 2]
                s.add("dve", lambda e, a_=a_: e.tensor_tensor(out=a_[0:64, :], in0=pO[0][0:64, 0:384], in1=recb[0:64, :], op=ALU.mult),
                      reads=[("pO", 0), "recb"], writes=[("ao", qb % 2, 0)])
                s.add("dve", lambda e, a_=a_: e.tensor_tensor(out=a_[64:128, :], in0=pO[1][64:128, 0:384], in1=recb[64:128, :], op=ALU.mult),
                      reads=[("pO", 1), "recb"], writes=[("ao", qb % 2, 1)])
                s.dma("sp", self.CAT[hp * 128:(hp + 1) * 128, q0:q0 + 384], a_[:, :], reads=[("ao", qb % 2, 0), ("ao", qb % 2, 1)],
                      writes=[("o", hp, qb)], semkey=("ao", qb % 2))
        p.close()

    def outproj_phase(self, layer, t0, wmat, src):
        c = self.c
        p = Phase(self.nc)
        s = p.s
        self.norm_bufs(p)
        xin = p.sb([128, c.KC, c.TB], BF16)
        wP = [p.sb([128, c.KC, 128], BF16) for _ in range(2)]
        p.pA = [p.ps() for _ in range(3)]
        p.pB = [p.ps() for _ in range(3)]
        for k in range(c.KC):
            s.dma("sp", xin[:, k, :], src[k * 128:(k + 1) * 128, t0:t0 + c.TB], writes=[("xin", k)], semkey=("xin", k % 4))
        xk = [("xin", k) for k in range(c.KC)]

        def epi(m, pp, pk):
            slot = m % 4
            for n in range(3):
                s.add("act", lambda e, n=n: e.activation(out=p.ch[slot][:, n * c.NT:(n + 1) * c.NT], in_=pp[n][:, 0:c.NT], func=AF.Copy),
                      reads=[(pk, n)], writes=[("ch", slot)])
            s.dma("sp", self.Fs[m * 128:(m + 1) * 128, t0:t0 + c.TB], p.ch[slot][:, :], reads=[("ch", slot)],
                  writes=[self.dkey("Fs", m, t0)], semkey=("st", slot))
        self.linear_fm(p, wmat.rearrange("(k p) m -> p k m", p=128), 0, c.KC, xin, xk, c.KC, wP, "wP", epi)
        self.norm_residual(p, self.Fs, "Fs", self.H, "H", t0, layer, 1)
        p.close()


    def s5_norm_phase(self, layer, t0):
        c = self.c
        p = Phase(self.nc)
        s = p.s
        self.norm_bufs(p)
        p.xa = p.sb([128, c.KC, c.TB], BF16)
        xs = [p.sb([128, 8, c.TBC], BF16) for _ in range(2)]
        self.norm_to_xa(p, self.H, "H", t0, layer, 0)
        n0 = t0 // 8
        for k in range(c.KC):
            b = k % 2
            s.dma("sp", self.UT[k * 128:(k + 1) * 128, t0:t0 + c.TB], p.xa[:, k, :], reads=[("xa", k)], writes=[("o", "ut", k)], semkey=("ut", b))
            s.add("pool", lambda e, k=k, b=b: e.tensor_copy(out=xs[b][:, :, :], in_=p.xa[:, k, :].rearrange("p (n s) -> p s n", s=8)),
                  reads=[("xa", k)], writes=[("xs", b)])
            s.dma("sp", self.U8D[k, :, :, n0:n0 + c.TBC].rearrange("s p n -> p s n"), xs[b][:, :, :], reads=[("xs", b)], writes=[("o", "u8", k)], semkey=("xs", b))
        p.close()

    def s5_core_phase(self, layer, gb):
        c = self.c
        o_i = layer // 2
        GB, NCH = c.GB, c.NCH
        NH2 = NCH // 2
        g0 = gb * GB
        p = Phase(self.nc)
        s = p.s
        cnt = [0]

        def op(eng, fn, reads, writes):
            s.add(eng, fn, reads=reads, writes=writes)

        def T(shape, dtype=F32):
            return p.sb(shape, dtype)

        def tt(out, a, b, alu, eng="dve"):
            op(eng, lambda e: e.tensor_tensor(out=out[0], in0=a[0], in1=b[0], op=alu), [a[1], b[1]], [out[1]])

        def key(t, name):
            return (t, name)

        def new(name, shape=None, dtype=F32):
            t = T([64] + list(shape or [GB]), dtype)
            return (t[:], name)

        prm = T([64, 3, c.G])
        bt = T([64, 2, GB, 16])
        ct = T([64, 2, GB, 16])
        c2 = T([128, 256])
        tmask = T([128, 128], BF16)
        ident = T([128, 128], BF16)
        s.dma("sp", prm[:, :, :], self.s5p[o_i], writes=["prm"])
        s.dma("sp", bt[:, :, :, :], self.s5b[o_i, :, :, g0:g0 + GB, :], writes=["bt"])
        s.dma("sp", ct[:, :, :, :], self.s5c[o_i, :, :, g0:g0 + GB, :], writes=["ct"])
        s.dma("sp", c2[:, :], self.cst2, writes=["c2"])
        op("dve", lambda e: e.tensor_copy(out=tmask[:, :], in_=c2[:, 0:128]), ["c2"], ["tmask"])
        op("dve", lambda e: e.tensor_copy(out=ident[:, :], in_=c2[:, 128:256]), ["c2"], ["ident"])
        are = (prm[:, 0, g0:g0 + GB], "prm")
        aim = (prm[:, 1, g0:g0 + GB], "prm")
        lst = (prm[:, 2, g0:g0 + GB], "prm")
        lre, dl, ex, mag, ang = new("lre"), new("dl"), new("ex"), new("mag"), new("ang")
        op("dve", lambda e: e.tensor_scalar_min(out=lre[0], in0=are[0], scalar1=-1e-4), ["prm"], ["lre"])
        op("act", lambda e: e.activation(out=dl[0], in_=lst[0], func=AF.Exp), ["prm"], ["dl"])
        tt(ex, lre, dl, ALU.mult)
        op("act", lambda e: e.activation(out=mag[0], in_=ex[0], func=AF.Exp), ["ex"], ["mag"])
        tt(ang, aim, dl, ALU.mult)

        def sin_of(name, src, shift):
            r, m = new(name + "r"), new(name + "m")
            op("dve", lambda e: e.tensor_scalar_add(out=r[0], in0=src[0], scalar1=shift), [src[1]], [r[1]])
            for _ in range(4):
                op("dve", lambda e: e.tensor_single_scalar(out=m[0], in_=r[0], scalar=math.pi, op=ALU.is_gt), [r[1]], [m[1]])
                op("dve", lambda e: e.scalar_tensor_tensor(out=r[0], in0=m[0], scalar=-2.0 * math.pi, in1=r[0], op0=ALU.mult, op1=ALU.add), [m[1], r[1]], [r[1]])
            o = new(name)
            op("act", lambda e: e.activation(out=o[0], in_=r[0], func=AF.Sin), [r[1]], [o[1]])
            return o
        sn = sin_of("sn", ang, 0.0)
        cs_ = sin_of("cs", ang, 0.5 * math.pi)
        lbr, lbi = new("lbr"), new("lbi")
        tt(lbr, mag, cs_, ALU.mult)
        tt(lbi, mag, sn, ALU.mult)
        den, t1, t2, nr, cr, ci = new("den"), new("t1"), new("t2"), new("nr"), new("cr"), new("ci")
        tt(den, lre, lre, ALU.mult)
        tt(t1, aim, aim, ALU.mult)
        tt(den, den, t1, ALU.add)
        op("dve", lambda e: e.reciprocal(out=den[0], in_=den[0]), ["den"], ["den"])
        op("dve", lambda e: e.tensor_scalar_add(out=nr[0], in0=lbr[0], scalar1=-1.0), ["lbr"], ["nr"])
        tt(t1, nr, lre, ALU.mult)
        tt(t2, lbi, aim, ALU.mult)
        tt(cr, t1, t2, ALU.add)
        tt(cr, cr, den, ALU.mult)
        tt(t1, lbi, lre, ALU.mult)
        tt(t2, nr, aim, ALU.mult)
        tt(ci, t1, t2, ALU.subtract)
        tt(ci, ci, den, ALU.mult)
        bbr, bbi, u1, u2 = new("bbr", [GB, 16]), new("bbi", [GB, 16]), new("u1", [GB, 16]), new("u2", [GB, 16])
        bc16 = lambda t: (t[0].unsqueeze(2).to_broadcast([64, GB, 16]), t[1])
        b_re, b_im = (bt[:, 0, :, :], "bt"), (bt[:, 1, :, :], "bt")
        c_re, c_im = (ct[:, 0, :, :], "ct"), (ct[:, 1, :, :], "ct")

        def cmul(outr, outi, ar, ai, br, bi, neg_im=False):
            tt(u1, ar, br, ALU.mult)
            tt(u2, ai, bi, ALU.mult)
            tt(outr, u1, u2, ALU.subtract)
            tt(u1, ar, bi, ALU.mult)
            tt(u2, ai, br, ALU.mult)
            if neg_im:
                tt(outi, u1, u2, ALU.add)
                op("dve", lambda e: e.tensor_single_scalar(out=outi[0], in_=outi[0], scalar=-1.0, op=ALU.mult), [outi[1]], [outi[1]])
            else:
                tt(outi, u1, u2, ALU.add)
        cmul(bbr, bbi, bc16(cr), bc16(ci), b_re, b_im)
        v1, v2 = new("v1"), new("v2")
        Pw = [(new("p0r"), new("p0i"))]
        op("dve", lambda e: e.memset(Pw[0][0][0], 1.0), [], ["p0r"])
        op("dve", lambda e: e.memset(Pw[0][1][0], 0.0), [], ["p0i"])
        for k in range(1, 9):
            pr, pi = new("p%dr" % k), new("p%di" % k)
            a_r, a_i = Pw[k - 1]
            tt(v1, a_r, lbr, ALU.mult); tt(v2, a_i, lbi, ALU.mult); tt(pr, v1, v2, ALU.subtract)
            tt(v1, a_r, lbi, ALU.mult); tt(v2, a_i, lbr, ALU.mult); tt(pi, v1, v2, ALU.add)
            Pw.append((pr, pi))
        m8, i8r, i8i = new("m8"), new("i8r"), new("i8i")
        tt(m8, Pw[8][0], Pw[8][0], ALU.mult); tt(v1, Pw[8][1], Pw[8][1], ALU.mult); tt(m8, m8, v1, ALU.add)
        op("dve", lambda e: e.reciprocal(out=m8[0], in_=m8[0]), ["m8"], ["m8"])
        tt(i8r, Pw[8][0], m8, ALU.mult)
        tt(i8i, Pw[8][1], m8, ALU.mult)
        op("dve", lambda e: e.tensor_single_scalar(out=i8i[0], in_=i8i[0], scalar=-1.0, op=ALU.mult), ["i8i"], ["i8i"])
        Rj = []
        for j in range(8):
            rr, ri = new("r%dr" % j), new("r%di" % j)
            a_r, a_i = Pw[j + 1]
            tt(v1, a_r, i8r, ALU.mult); tt(v2, a_i, i8i, ALU.mult); tt(rr, v1, v2, ALU.subtract)
            tt(v1, a_r, i8i, ALU.mult); tt(v2, a_i, i8r, ALU.mult); tt(ri, v1, v2, ALU.add)
            Rj.append((rr, ri))
        A1r, A1i = T([64, GB, 8, 16], BF16), T([64, GB, 8, 16], BF16)
        W2r, W2n = T([64, GB, 8, 16], BF16), T([64, GB, 8, 16], BF16)
        Wsr, Wsn = T([64, GB, 8, 16], BF16), T([64, GB, 8, 16], BF16)
        fr, fi = new("fr", [GB, 16]), new("fi", [GB, 16])
        for q in range(8):
            cmul(fr, fi, bc16(Pw[7 - q][0]), bc16(Pw[7 - q][1]), bbr, bbi)
            op("act", lambda e, q=q: e.activation(out=A1r[:, :, q, :], in_=fr[0], func=AF.Copy), ["fr"], [("A1r", q)])
            op("act", lambda e, q=q: e.activation(out=A1i[:, :, q, :], in_=fi[0], func=AF.Copy), ["fi"], [("A1i", q)])
            cmul(fr, fi, bc16(Pw[q + 1][0]), bc16(Pw[q + 1][1]), c_re, c_im, neg_im=True)
            op("act", lambda e, q=q: e.activation(out=W2r[:, :, q, :], in_=fr[0], func=AF.Copy), ["fr"], [("W2r", q)])
            op("act", lambda e, q=q: e.activation(out=W2n[:, :, q, :], in_=fi[0], func=AF.Copy), ["fi"], [("W2n", q)])
            cmul(fr, fi, bc16(Rj[q][0]), bc16(Rj[q][1]), c_re, c_im, neg_im=True)
            op("act", lambda e, q=q: e.activation(out=Wsr[:, :, q, :], in_=fr[0], func=AF.Copy), ["fr"], [("Wsr", q)])
            op("act", lambda e, q=q: e.activation(out=Wsn[:, :, q, :], in_=fi[0], func=AF.Copy), ["fi"], [("Wsn", q)])
        allk = lambda nm: [(nm, q) for q in range(8)]
        flat = lambda t, gi: t[:, gi, :, :].rearrange("p s h -> p (s h)")
        Tb = T([128, GB, 128], BF16)
        W1b = T([128, GB, 128], BF16)
        pT = [p.ps() for _ in range(2)]
        pW = [p.ps([128, 128], BF16) for _ in range(2)]
        for gi in range(GB):
            b = gi % 2
            op("pe", lambda e, gi=gi, b=b: e.matmul(pT[b][:, 0:128], lhsT=flat(A1r, gi), rhs=flat(Wsr, gi), start=True, stop=False),
               allk("A1r") + allk("Wsr"), [("pT", b)])
            op("pe", lambda e, gi=gi, b=b: e.matmul(pT[b][:, 0:128], lhsT=flat(A1i, gi), rhs=flat(Wsn, gi), start=False, stop=True),
               allk("A1i") + allk("Wsn"), [("pT", b)])
            op("dve", lambda e, gi=gi, b=b: e.tensor_tensor(out=Tb[:, gi, :], in0=pT[b][:, 0:128], in1=tmask[:, :], op=ALU.mult),
               [("pT", b), "tmask"], [("Tb", gi)])
            op("pe", lambda e, gi=gi, b=b: e.transpose(pW[b][:, 0:64], flat(A1r, gi), ident[0:64, 0:64]), allk("A1r") + ["ident"], [("pW", b, 0)])
            op("pe", lambda e, gi=gi, b=b: e.transpose(pW[b][:, 64:128], flat(A1i, gi), ident[0:64, 0:64]), allk("A1i") + ["ident"], [("pW", b, 1)])
            op("act", lambda e, gi=gi, b=b: e.activation(out=W1b[:, gi, :], in_=pW[b][:, :], func=AF.Copy), [("pW", b, 0), ("pW", b, 1)], [("W1b", gi)])
        u8 = T([128, GB, NCH], BF16)
        for kl in range(GB // 8):
            kc = g0 // 8 + kl
            for q in range(8):
                s.dma("sp", u8[q * 16:(q + 1) * 16, kl * 8:(kl + 1) * 8, :], self.U8D[kc, q, :, :].rearrange("(gl h) n -> h gl n", h=16),
                      writes=[("u8", kl, q)], semkey=("u8", q % 4))
        u8k = [("u8", kl, q) for kl in range(GB // 8) for q in range(8)]
        Vb = T([64, 2, GB, NCH], BF16)
        Xb = T([64, 2, GB, NCH + 1], BF16)
        pV = [p.ps() for _ in range(2)]
        for gi in range(GB):
            for hf in range(2):
                for cc in range(2):
                    b = cc
                    op("pe", lambda e, gi=gi, hf=hf, cc=cc, b=b: e.matmul(pV[b][0:64, 0:NH2], lhsT=W1b[:, gi, cc * 64:(cc + 1) * 64],
                                                                       rhs=u8[:, gi, hf * NH2:(hf + 1) * NH2], start=True, stop=True),
                       [("W1b", gi)] + u8k, [("pV", b)])
                    op("act", lambda e, gi=gi, hf=hf, cc=cc, b=b: e.activation(out=Vb[:, cc, gi, hf * NH2:(hf + 1) * NH2], in_=pV[b][0:64, 0:NH2], func=AF.Copy),
                       [("pV", b)], [("Vb", gi)])
        A8r, A8i = Pw[8]
        vk = [("Vb", gi) for gi in range(GB)]
        HG = GB // 2
        xk = []
        for hf, eng in ((0, "dve"), (1, "pool")):
            gs = slice(hf * HG, (hf + 1) * HG)
            st, w1, w2 = T([64, 2, HG]), T([64, 2, HG]), T([64, 2, HG])
            a8b = A8r[0][:, gs].unsqueeze(1).to_broadcast([64, 2, HG])
            a8i = A8i[0][:, gs]
            K_ = lambda nm, hf=hf: (nm, hf)
            op(eng, lambda e, st=st: e.memset(st[:, :, :], 0.0), [], [K_("st")])
            op(eng, lambda e, gs=gs: e.memset(Xb[:, :, gs, 0:1], 0.0), [], [("Xb", hf, 0)])
            for n in range(NCH):
                op(eng, lambda e, st=st, w1=w1, a8b=a8b: e.tensor_tensor(out=w1[:, :, :], in0=st[:, :, :], in1=a8b, op=ALU.mult), [K_("st"), "p8r"], [K_("w1")])
                op(eng, lambda e, st=st, w2=w2, a8i=a8i: e.tensor_tensor(out=w2[:, 0, :], in0=st[:, 1, :], in1=a8i, op=ALU.mult), [K_("st"), "p8i"], [K_("w2a")])
                op(eng, lambda e, st=st, w2=w2, a8i=a8i: e.tensor_tensor(out=w2[:, 1, :], in0=st[:, 0, :], in1=a8i, op=ALU.mult), [K_("st"), "p8i"], [K_("w2b")])
                op(eng, lambda e, w1=w1, w2=w2: e.tensor_tensor(out=w1[:, 0, :], in0=w1[:, 0, :], in1=w2[:, 0, :], op=ALU.subtract), [K_("w1"), K_("w2a")], [K_("w1")])
                op(eng, lambda e, w1=w1, w2=w2: e.tensor_tensor(out=w1[:, 1, :], in0=w1[:, 1, :], in1=w2[:, 1, :], op=ALU.add), [K_("w1"), K_("w2b")], [K_("w1")])
                op(eng, lambda e, st=st, w1=w1, gs=gs, n=n: e.tensor_tensor(out=st[:, :, :], in0=w1[:, :, :], in1=Vb[:, :, gs, n], op=ALU.add),
                   [K_("w1")] + (vk if n == 0 else []), [K_("st")])
                op(eng, lambda e, st=st, gs=gs, n=n: e.tensor_copy(out=Xb[:, :, gs, n + 1], in_=st[:, :, :]), [K_("st")], [("Xb", hf, n + 1)])
            xk.append(("Xb", hf, NCH))
        y8 = T([128, GB, NCH], BF16)
        pY = [p.ps() for _ in range(2)]
        for gi in range(GB):
            for hf in range(2):
                b = hf
                sl = slice(hf * NH2, (hf + 1) * NH2)
                op("pe", lambda e, gi=gi, sl=sl, b=b: e.matmul(pY[b][:, 0:NH2], lhsT=Tb[:, gi, :], rhs=u8[:, gi, sl], start=True, stop=False),
                   [("Tb", gi)] + u8k, [("pY", b)])
                op("pe", lambda e, gi=gi, sl=sl, b=b: e.matmul(pY[b][:, 0:NH2], lhsT=flat(W2r, gi), rhs=Xb[:, 0, gi, sl], start=False, stop=False),
                   allk("W2r") + xk, [("pY", b)])
                op("pe", lambda e, gi=gi, sl=sl, b=b: e.matmul(pY[b][:, 0:NH2], lhsT=flat(W2n, gi), rhs=Xb[:, 1, gi, sl], start=False, stop=True),
                   allk("W2n") + xk, [("pY", b)])
                op("act", lambda e, gi=gi, sl=sl, b=b: e.activation(out=y8[:, gi, sl], in_=pY[b][:, 0:NH2], func=AF.Copy), [("pY", b)], [("y8", gi)])
        yk = [("y8", gi) for gi in range(GB)]
        for kl in range(GB // 8):
            kc = g0 // 8 + kl
            for q in range(8):
                s.dma("sp", self.Y8D[kc, q, :, :].rearrange("(gl h) n -> h gl n", h=16), y8[q * 16:(q + 1) * 16, kl * 8:(kl + 1) * 8, :],
                      reads=yk, writes=[("o", "y8", kl, q)], semkey=("y8o", q % 4))
        p.close()

    def s5_out_phase(self, layer, t0):
        c = self.c
        o_i = layer // 2
        p = Phase(self.nc)
        s = p.s
        self.norm_bufs(p)
        ga = p.sb([128, c.KC, c.TB], BF16)
        yt = [p.sb([128, 8, c.TBC], BF16) for _ in range(2)]
        ut = [p.sb([128, c.TB], BF16) for _ in range(2)]
        yv = [p.sb([128, c.TB], F32) for _ in range(2)]
        w_ = [p.sb([128, c.TB], F32) for _ in range(2)]
        wA = [p.sb([128, c.KC, 128], BF16) for _ in range(2)]
        wB = [p.sb([128, c.KC, 128], BF16) for _ in range(2)]
        sg = [p.sb([128, c.TB], BF16) for _ in range(2)]
        p.pA = [p.ps() for _ in range(3)]
        p.pB = [p.ps() for _ in range(3)]
        n0 = t0 // 8
        v3 = lambda ap: ap.rearrange("p (n j) -> p n j", j=8)
        for k in range(c.KC):
            b = k % 2
            s.dma("sp", yt[b][:, :, :], self.Y8D[k, :, :, n0:n0 + c.TBC].rearrange("j p n -> p j n"), writes=[("yt", b)])
            s.dma("sp", ut[b][:, :], self.UT[k * 128:(k + 1) * 128, t0:t0 + c.TB], writes=[("ut", b)])
            dcol = self.ds[:, o_i * c.KC + k:o_i * c.KC + k + 1]
            s.add("dve", lambda e, b=b, dcol=dcol: e.scalar_tensor_tensor(out=v3(yv[b][:, :]), in0=v3(ut[b][:, :]), scalar=dcol,
                                                                       in1=yt[b][:, :, :].rearrange("p j n -> p n j"), op0=ALU.mult, op1=ALU.add),
                  reads=[("yt", b), ("ut", b)], writes=[("yv", b)])
            s.add("dve", lambda e, b=b: e.tensor_tensor(out=w_[b][:, :], in0=yv[b][:, :], in1=yv[b][:, :], op=ALU.mult), reads=[("yv", b)], writes=[("w", b)])
            s.add("dve", lambda e, b=b: e.tensor_scalar(out=w_[b][:, :], in0=w_[b][:, :], scalar1=0.044715, scalar2=1.0, op0=ALU.mult, op1=ALU.add),
                  reads=[("w", b)], writes=[("w", b)])
            s.add("dve", lambda e, b=b: e.tensor_tensor(out=w_[b][:, :], in0=w_[b][:, :], in1=yv[b][:, :], op=ALU.mult), reads=[("w", b), ("yv", b)], writes=[("w", b)])
            s.add("act", lambda e, b=b: e.activation(out=w_[b][:, :], in_=w_[b][:, :], func=AF.Sigmoid, scale=1.5957691216057308), reads=[("w", b)], writes=[("w", b)])
            s.add("dve", lambda e, b=b, k=k: e.tensor_tensor(out=ga[:, k, :], in0=w_[b][:, :], in1=yv[b][:, :], op=ALU.mult), reads=[("w", b), ("yv", b)], writes=[("ga", k)])
        gk = [("ga", k) for k in range(c.KC)]
        w1v = self.wq1[o_i].rearrange("(k p) m -> p k m", p=128)
        w2v = self.wq2[o_i].rearrange("(k p) m -> p k m", p=128)
        for m in range(c.KC):
            sl = m % 2
            s.dma("pool", wA[sl][:, :, :], w2v[:, :, m * 128:(m + 1) * 128], writes=[("wA", sl)])
            s.dma("pool", wB[sl][:, :, :], w1v[:, :, m * 128:(m + 1) * 128], writes=[("wB", sl)])
            for wbuf, wkey, pp, pk in ((wA, "wA", p.pA, "pA"), (wB, "wB", p.pB, "pB")):
                for k in range(c.KC):
                    for n in range(3):
                        s.add("pe", lambda e, wbuf=wbuf, pp=pp, k=k, n=n, sl=sl: e.matmul(
                            pp[n][:, 0:c.NT], lhsT=wbuf[sl][:, k, :], rhs=ga[:, k, n * c.NT:(n + 1) * c.NT], start=(k == 0), stop=(k == c.KC - 1)),
                            reads=[(wkey, sl)] + gk, writes=[(pk, n)])
                slot = m % 4
                for n in range(3):
                    if wkey == "wA":
                        s.add("act", lambda e, n=n, m=m: e.activation(out=sg[m % 2][:, n * c.NT:(n + 1) * c.NT], in_=p.pA[n][:, 0:c.NT], func=AF.Sigmoid),
                              reads=[("pA", n)], writes=[("sg", m % 2, n)])
                    else:
                        s.add("dve", lambda e, n=n, m=m, slot=slot: e.tensor_tensor(out=p.ch[slot][:, n * c.NT:(n + 1) * c.NT], in0=p.pB[n][:, 0:c.NT],
                                                                                in1=sg[m % 2][:, n * c.NT:(n + 1) * c.NT], op=ALU.mult),
                              reads=[("pB", n), ("sg", m % 2, n)], writes=[("ch", slot)])
            s.dma("sp", self.Fs[m * 128:(m + 1) * 128, t0:t0 + c.TB], p.ch[m % 4][:, :], reads=[("ch", m % 4)],
                  writes=[self.dkey("Fs", m, t0)], semkey=("st", m % 4))
        self.norm_residual(p, self.Fs, "Fs", self.H, "H", t0, layer, 1)
        p.close()

    def build(self):
        c = self.c
        for q in range(c.NSEQ):
            p = Phase(self.nc)
            for k in range(c.KC):
                p.s.dma("sp", self.H[k * 128:(k + 1) * 128, :], self.xT[q, k * 128:(k + 1) * 128, :], writes=[("h", k)], semkey=("hinit",))
            p.close()
            for layer in range(c.DEPTH):
                if layer % 2 == 0 and "even" in self.stages:
                    for b in range(c.NB):
                        self.inproj_phase(layer, b * c.TB)
                    self.conv_phase(layer)
                    self.attn_phase(layer)
                    if self.dbg and layer == 0:
                        p = Phase(self.nc)
                        for i, (n, o) in enumerate(self.dbg_out.items()):
                            p.s.dma("sp", o, getattr(self, n), writes=[("dbg", i)], semkey=("dbg", i))
                        p.close()
                    for b in range(c.NB):
                        self.outproj_phase(layer, b * c.TB, self.wo[layer // 2], self.CAT)
                if layer % 2 == 1 and "s5" in self.stages:
                    for b in range(c.NB):
                        self.s5_norm_phase(layer, b * c.TB)
                    for gb in range(c.G // c.GB):
                        self.s5_core_phase(layer, gb)
                    if self.dbg and layer == 1:
                        p = Phase(self.nc)
                        for i, (n, o) in enumerate(self.dbg_out.items()):
                            p.s.dma("sp", o, getattr(self, n), writes=[("dbg", i)], semkey=("dbg", i))
                        p.close()
                    for b in range(c.NB):
                        self.s5_out_phase(layer, b * c.TB)
                if "ffn" in self.stages:
                    for b in range(c.NB):
                        self.ffn_phase(layer, b * c.TB)
                if self.dbg:
                    p = Phase(self.nc)
                    for k in range(c.KC):
                        p.s.dma("sp", self.hdump[q, layer, k * 128:(k + 1) * 128, :], self.H[k * 128:(k + 1) * 128, :], writes=[("hd", k)], semkey=("hd",))
                    p.close()
            p = Phase(self.nc)
            for k in range(c.KC):
                p.s.dma("sp", self.out[q, k * 128:(k + 1) * 128, :], self.H[k * 128:(k + 1) * 128, :], writes=[("o", k)], semkey=("hout",))
            p.close()
        return self.nc


N_META = 16
_DEBUG_DUMPS = False
_LAST = {}


def kernel(x, meta_tokens, norm_g, ab_w_in, ab_b_f, ab_conv_w, ab_conv_b, ab_w_o,
           s5_a_re, s5_a_im, s5_log_step, s5_b_re, s5_b_im, s5_c_re, s5_c_im,
           s5_d, s5_w_glu1, s5_w_glu2, ffn_w_gate, ffn_w_up, ffn_w_down):
    f = np.float32
    x = np.asarray(x, f)
    B, S, D = x.shape
    depth = norm_g.shape[0]
    cfg = Cfg(D=D, FFN=ffn_w_gate.shape[2], LP=4224, NT=352, NSEQ=1, DEPTH=depth)
    L = N_META + S
    assert L <= cfg.LP and cfg.PW == ab_w_in.shape[2]
    xT = np.zeros((B, D, cfg.LP), f)
    xT[:, :, :N_META] = np.asarray(meta_tokens, f).T[None]
    xT[:, :, N_META:L] = x.transpose(0, 2, 1)
    ngl = np.ascontiguousarray(np.asarray(norm_g, f).reshape(depth * 4, cfg.KC, 128).transpose(2, 0, 1).reshape(128, -1))
    NE = ab_w_in.shape[0]
    bfl = np.ascontiguousarray(np.broadcast_to(np.asarray(ab_b_f, f)[:, None, :], (NE, 128, cfg.NH)))
    cwl = np.zeros((NE, 128, cfg.CC * 4), f)
    cw3 = np.asarray(ab_conv_w, f).reshape(NE, 3, cfg.CC, 128)
    cwl.reshape(NE, 128, cfg.CC, 4)[:, :, :, :3] = cw3.transpose(0, 3, 2, 1)
    cwl.reshape(NE, 128, cfg.CC, 4)[:, :, :, 3] = np.asarray(ab_conv_b, f).reshape(NE, cfg.CC, 128).transpose(0, 2, 1)
    NO = s5_a_re.shape[0]
    G = cfg.G
    s5p = np.ascontiguousarray(np.stack([np.asarray(s5_a_re, f).transpose(0, 2, 1), np.asarray(s5_a_im, f).transpose(0, 2, 1),
                                         np.broadcast_to(np.asarray(s5_log_step, f)[:, None, :], (NO, 64, G))], 2))
    s5b = np.ascontiguousarray(np.stack([np.asarray(s5_b_re, f).transpose(0, 2, 1, 3), np.asarray(s5_b_im, f).transpose(0, 2, 1, 3)], 2))
    s5c = np.ascontiguousarray(np.stack([np.asarray(s5_c_re, f).transpose(0, 3, 1, 2), np.asarray(s5_c_im, f).transpose(0, 3, 1, 2)], 2))
    s5dl = np.ascontiguousarray(np.asarray(s5_d, f).reshape(NO, cfg.KC, 128).transpose(2, 0, 1).reshape(128, -1))
    ins = {"xT": xT, "ng": ngl, "wg": np.ascontiguousarray(ffn_w_gate, f), "wu": np.ascontiguousarray(ffn_w_up, f),
           "wd": np.ascontiguousarray(ffn_w_down, f), "win": np.ascontiguousarray(ab_w_in, f), "wo": np.ascontiguousarray(ab_w_o, f),
           "bf": bfl, "cw": cwl, "cst": host_constants(),
           "s5p": s5p, "s5b": s5b, "s5c": s5c, "s5d": s5dl, "wq1": np.ascontiguousarray(s5_w_glu1, f), "wq2": np.ascontiguousarray(s5_w_glu2, f),
           "cst2": host_constants2()}
    nc = Builder(cfg, stages=IMPLEMENTED_STAGES, dbg=_DEBUG_DUMPS).build()
    shared = {k: v for k, v in ins.items() if k != "xT"}
    in_maps = [dict(shared, xT=np.ascontiguousarray(xT[b:b + 1])) for b in range(B)]
    res = run_bass_kernel_spmd(nc, in_maps, core_ids=list(range(B)))
    out = np.concatenate([res.results[b]["out"] for b in range(B)], axis=0)
    if _DEBUG_DUMPS:
        _LAST.clear()
        _LAST.update(res.results[0])
    return np.ascontiguousarray(out[:, :, N_META:L].transpose(0, 2, 1)).astype(f)
```

```python
import contextlib
import math
import numpy as np
import concourse.bass as bass
import concourse.mybir as mybir
from concourse.bass_utils import run_bass_kernel_spmd

IMPLEMENTED_STAGES = ("even", "s5", "ffn")

F32 = mybir.dt.float32
BF16 = mybir.dt.bfloat16
AF = mybir.ActivationFunctionType
ALU = mybir.AluOpType
ENGS = ("pe", "act", "dve", "pool", "sp")
SAME_ENGINE_UNSYNCED = ("pe",)


class Sched:
    _seg = [0]

    def __init__(self, nc):
        self.nc = nc
        self.ops = []
        self.sem_handles = []

    def add(self, eng, fn, reads=(), writes=(), dma=False, semkey=None):
        if dma and semkey is None:
            semkey = writes[0]
        self.ops.append((eng, fn, tuple(reads), tuple(writes), dma, semkey))

    def dma(self, eng, out, in_, reads=(), writes=(), semkey=None):
        self.add(eng, lambda e: e.dma_start(out=out, in_=in_), reads, writes, True, semkey)

    def emit(self):
        nc, ops = self.nc, self.ops
        n = len(ops)
        last_w, readers = {}, {}
        deps = [None] * n
        has_dep = [False] * n
        for i, (eng, fn, rds, wrs, dma, sk) in enumerate(ops):
            d = set()
            for r in rds:
                if r in last_w:
                    d.add(last_w[r])
            for w in wrs:
                if w in last_w:
                    d.add(last_w[w])
                d.update(readers.get(w, ()))
            d.discard(i)
            d = {j for j in d if not (ops[j][0] == eng and eng in SAME_ENGINE_UNSYNCED and not ops[j][4] and not dma)}
            deps[i] = d
            for j in d:
                has_dep[j] = True
            for r in rds:
                readers.setdefault(r, []).append(i)
            for w in wrs:
                last_w[w] = i
                readers[w] = []
        eng_cnt = {e: 0 for e in ENGS}
        dma_cnt = {}
        ticket = [None] * n
        for i, (eng, fn, rds, wrs, dma, sk) in enumerate(ops):
            if dma:
                dma_cnt[sk] = dma_cnt.get(sk, 0) + 16
                ticket[i] = (("d", sk), dma_cnt[sk])
            elif has_dep[i]:
                eng_cnt[eng] += 1
                ticket[i] = (("e", eng), eng_cnt[eng])
        semnames = [("e", e) for e in ENGS if eng_cnt[e]] + [("d", k) for k in dma_cnt]
        self.nsems = len(semnames)
        self.sem_handles = [nc.alloc_semaphore(name="s%d_%d" % (Sched._seg[0], i)) for i in range(len(semnames))]
        Sched._seg[0] += 1
        sems = dict(zip(semnames, self.sem_handles))
        with nc.Block() as block:
            per_eng = {e: [i for i in range(n) if ops[i][0] == e] for e in ENGS}

            def run(eng_name, e):
                waited = {}
                for i in per_eng[eng_name]:
                    need = {}
                    for j in deps[i]:
                        sn, v = ticket[j]
                        if need.get(sn, 0) < v:
                            need[sn] = v
                    for sn, v in need.items():
                        if waited.get(sn, 0) < v:
                            e.wait_ge(sems[sn], v)
                            waited[sn] = v
                    ins = ops[i][1](e)
                    if ticket[i] is not None:
                        ins.then_inc(sems[ticket[i][0]], 16 if ops[i][4] else 1)
                for i in per_eng[eng_name]:
                    if ops[i][4]:
                        sn = ticket[i][0]
                        tot = dma_cnt[ops[i][5]]
                        if waited.get(sn, 0) < tot:
                            e.wait_ge(sems[sn], tot)
                            waited[sn] = tot

            for name, reg in (("sp", block.sync), ("pe", block.tensor), ("act", block.scalar),
                              ("dve", block.vector), ("pool", block.gpsimd)):
                if per_eng[name]:
                    reg(lambda e, name=name: run(name, e))


def host_constants():
    f = np.float32
    kk, mm = np.arange(128)[:, None], np.arange(128)[None, :]
    bsel = np.zeros((128, 128), f)
    bsel[64, :64] = 1
    bsel[0, 64:] = 1
    masks = [(128 * j + np.arange(128)[:, None] <= np.arange(384)[None, :]).astype(f) for j in range(3)]
    return np.ascontiguousarray(np.concatenate([(kk <= mm).astype(f), np.ones((128, 128), f),
                                                np.broadcast_to(kk == 127, (128, 128)).astype(f), bsel] + masks, 1))


def host_constants2():
    r = np.arange(128)
    tmask = ((r[None, :] // 16) >= (r[:, None] // 16)).astype(np.float32)
    return np.ascontiguousarray(np.concatenate([tmask, np.eye(128, dtype=np.float32)], 1))


class Cfg:
    def __init__(self, D=2048, FFN=5632, LP=4224, NT=352, NSEQ=1, DEPTH=4):
        self.D, self.FFN, self.LP, self.NT, self.NSEQ, self.DEPTH = D, FFN, LP, NT, NSEQ, DEPTH
        self.TB = 3 * NT
        assert LP % self.TB == 0 and D % 256 == 0 and FFN % 256 == 0 and LP % 384 == 0
        self.NB = LP // self.TB
        self.KC = D // 128
        self.HC = FFN // 128
        self.AW = D // 2
        self.CW = D // 2
        self.NH = self.AW // 64
        self.AC = self.AW // 128
        self.CC = self.CW // 128
        self.PW = 3 * self.AW + self.NH + 3 * self.CW
        self.NKT = LP // 128
        self.NQB = LP // 384
        self.TT = -(-self.TB // 128)
        self.G = D // 16
        self.GB = 16
        self.NCH = LP // 8
        self.TBC = self.TB // 8
        assert self.TB % 8 == 0 and self.G % self.GB == 0 and self.NCH % 2 == 0 and self.NCH // 2 <= 512
        self.EPS = 1e-6


class Phase:
    _uid = [0]

    def __init__(self, nc):
        self.nc = nc
        self.st = contextlib.ExitStack()
        self.s = Sched(nc)

    def _name(self, pfx):
        Phase._uid[0] += 1
        return "%s%d" % (pfx, Phase._uid[0])

    def sb(self, shape, dtype):
        return self.st.enter_context(self.nc.sbuf_tensor(self._name("t"), list(shape), dtype))

    def ps(self, shape=(128, 512), dtype=F32):
        return self.st.enter_context(self.nc.psum_tensor(self._name("p"), list(shape), dtype))

    def close(self):
        self.s.emit()
        self.st.close()
        self.nc.all_engine_barrier()
        self.nc.clear_and_free_semaphores(self.s.sem_handles)
        self.nc.all_engine_barrier()


class Builder:
    def __init__(self, cfg, stages, dbg=False):
        self.c = c = cfg
        self.nc = nc = bass.Bass("TRN2", target_bir_lowering=False)
        self.stages = stages
        self.dbg = dbg
        D, LP = c.D, c.LP
        dt = nc.dram_tensor
        NE = (c.DEPTH + 1) // 2
        ext = lambda name, shape: dt(name, list(shape), F32, kind="ExternalInput").ap()
        self.xT = ext("xT", [c.NSEQ, D, LP])
        self.ng = ext("ng", [128, c.DEPTH * 4 * c.KC])
        self.wg = ext("wg", [c.DEPTH, D, c.FFN])
        self.wu = ext("wu", [c.DEPTH, D, c.FFN])
        self.wd = ext("wd", [c.DEPTH, c.FFN, D])
        self.win = ext("win", [NE, D, c.PW])
        self.wo = ext("wo", [NE, D, D])
        self.bf = ext("bf", [NE, 128, c.NH])
        self.cw = ext("cw", [NE, 128, c.CC * 4])
        self.cst = ext("cst", [128, 4 * 128 + 3 * 384])
        NO = max(c.DEPTH // 2, 1)
        self.s5p = ext("s5p", [NO, 64, 3, c.G])
        self.s5b = ext("s5b", [NO, 64, 2, c.G, 16])
        self.s5c = ext("s5c", [NO, 64, 2, c.G, 16])
        self.s5d = ext("s5d", [128, NO * c.KC])
        self.wq1 = ext("wq1", [NO, D, D])
        self.wq2 = ext("wq2", [NO, D, D])
        self.cst2 = ext("cst2", [128, 256])
        self.out = dt("out", [c.NSEQ, D, LP], F32, kind="ExternalOutput").ap()
        self.UT = dt("UT", [D, LP], BF16).ap()
        self.U8D = dt("U8D", [c.KC, 8, 128, c.NCH], BF16).ap()
        self.Y8D = dt("Y8D", [c.KC, 8, 128, c.NCH], BF16).ap()
        self.H = dt("H", [D, LP], F32).ap()
        self.Fs = dt("Fs", [D, LP], F32).ap()
        self.QT = dt("QT", [c.AW, LP], BF16).ap()
        self.KT = dt("KT", [c.AW, LP], BF16).ap()
        self.GX = dt("GX", [3, c.CW, LP], BF16).ap()
        self.VT = dt("VT", [LP, c.AW], BF16).ap()
        self.FG = dt("FG", [LP, c.NH], F32).ap()
        self.CAT = dt("CAT", [D, LP], BF16).ap()
        if dbg:
            self.hdump = dt("hdump", [c.NSEQ, c.DEPTH, D, LP], F32, kind="ExternalOutput").ap()
            self.dbg_out = {n: dt("d" + n, list(a.shape), a.dtype, kind="ExternalOutput").ap()
                            for n, a in (("QT", self.QT), ("KT", self.KT), ("GX", self.GX), ("VT", self.VT), ("FG", self.FG), ("CAT", self.CAT),
                                         ("UT", self.UT), ("U8D", self.U8D), ("Y8D", self.Y8D))}
        sb = nc.alloc_sbuf_tensor
        self.ngs = sb("ngs", [128, c.DEPTH * 4 * c.KC], F32)
        self.ds = sb("ds", [128, NO * c.KC], F32)
        self.ones = sb("ones", [128, 128], BF16)
        p = Phase(nc)
        p.s.dma("sp", self.ngs[:, :], self.ng, writes=["ngs"])
        p.s.dma("sp", self.ds[:, :], self.s5d, writes=["ds"])
        p.s.add("dve", lambda e: e.memset(self.ones[:, :], 1.0), writes=["ones"])
        p.close()


    def gcol(self, layer, j, kc):
        i = (layer * 4 + j) * self.c.KC + kc
        return self.ngs[:, i:i + 1]

    def dkey(self, name, k, t0):
        return ("D", name, k, t0)

    def norm_bufs(self, p):
        c = self.c
        p.ch = [p.sb([128, c.TB], F32) for _ in range(4)]
        p.sq = [p.sb([128, c.NT], BF16) for _ in range(2)]
        p.rstd = p.sb([128, c.TB], F32)
        p.pS = p.ps()

    def rms_stats(self, p, src, sname, t0):
        c, s = self.c, p.s
        for n in range(3):
            n0 = n * c.NT
            for k in range(c.KC):
                slot = k % 4
                s.dma("sp", p.ch[slot][:, 0:c.NT], src[k * 128:(k + 1) * 128, t0 + n0:t0 + n0 + c.NT],
                      reads=[self.dkey(sname, k, t0)], writes=[("ch", slot)], semkey=("ch", slot))
                s.add("act", lambda e, slot=slot, k=k: e.activation(out=p.sq[k % 2][:, :], in_=p.ch[slot][:, 0:c.NT], func=AF.Square),
                      reads=[("ch", slot)], writes=[("sq", k % 2)])
                s.add("pe", lambda e, k=k: e.matmul(p.pS[:, 0:c.NT], lhsT=self.ones[:, :], rhs=p.sq[k % 2][:, :],
                                                    start=(k == 0), stop=(k == c.KC - 1)),
                      reads=[("sq", k % 2)], writes=["pS"])
            s.add("dve", lambda e, n0=n0: e.tensor_scalar(out=p.rstd[:, n0:n0 + c.NT], in0=p.pS[:, 0:c.NT],
                                                         scalar1=1.0 / c.D, scalar2=c.EPS, op0=ALU.mult, op1=ALU.add),
                  reads=["pS"], writes=[("rstd", n)])
            s.add("act", lambda e, n0=n0: e.activation(out=p.rstd[:, n0:n0 + c.NT], in_=p.rstd[:, n0:n0 + c.NT], func=AF.Sqrt),
                  reads=[("rstd", n)], writes=[("rstd", n)])
            s.add("dve", lambda e, n0=n0: e.reciprocal(out=p.rstd[:, n0:n0 + c.NT], in_=p.rstd[:, n0:n0 + c.NT]),
                  reads=[("rstd", n)], writes=[("rstd", n)])

    def norm_to_xa(self, p, src, sname, t0, layer, j):
        c, s = self.c, p.s
        self.rms_stats(p, src, sname, t0)
        rk = [("rstd", n) for n in range(3)]
        for k in range(c.KC):
            slot = k % 4
            s.dma("sp", p.ch[slot][:, :], src[k * 128:(k + 1) * 128, t0:t0 + c.TB],
                  reads=[self.dkey(sname, k, t0)], writes=[("ch", slot)], semkey=("ch", slot))
            s.add("dve", lambda e, slot=slot, k=k: e.scalar_tensor_tensor(out=p.xa[:, k, :], in0=p.ch[slot][:, :], scalar=self.gcol(layer, j, k),
                                                                         in1=p.rstd[:, :], op0=ALU.mult, op1=ALU.mult),
                  reads=[("ch", slot)] + rk, writes=[("xa", k)])

    def norm_residual(self, p, src, sname, base, bname, t0, layer, j):
        c, s = self.c, p.s
        self.rms_stats(p, src, sname, t0)
        rk = [("rstd", n) for n in range(3)]
        for k in range(c.KC):
            a, b = (2 * k) % 4, (2 * k + 1) % 4
            s.dma("sp", p.ch[a][:, :], src[k * 128:(k + 1) * 128, t0:t0 + c.TB], reads=[self.dkey(sname, k, t0)],
                  writes=[("ch", a)], semkey=("ch", a))
            s.dma("sp", p.ch[b][:, :], base[k * 128:(k + 1) * 128, t0:t0 + c.TB], reads=[self.dkey(bname, k, t0)],
                  writes=[("ch", b)], semkey=("ch", b))
            s.add("dve", lambda e, a=a, k=k: e.scalar_tensor_tensor(out=p.ch[a][:, :], in0=p.ch[a][:, :], scalar=self.gcol(layer, j, k),
                                                                   in1=p.rstd[:, :], op0=ALU.mult, op1=ALU.mult),
                  reads=[("ch", a)] + rk, writes=[("ch", a)])
            s.add("dve", lambda e, a=a, b=b: e.tensor_tensor(out=p.ch[b][:, :], in0=p.ch[b][:, :], in1=p.ch[a][:, :], op=ALU.add),
                  reads=[("ch", a), ("ch", b)], writes=[("ch", b)])
            s.dma("sp", base[k * 128:(k + 1) * 128, t0:t0 + c.TB], p.ch[b][:, :], reads=[("ch", b)],
                  writes=[self.dkey(bname, k, t0)], semkey=("st", b))

    def linear_fm(self, p, wv, c0, nchunks, xin, xkeys, nk, wbufs, wname, epi):
        c, s = self.c, p.s
        for m in range(nchunks):
            sl = m % 2
            s.dma("pool", wbufs[sl][:, :, :], wv[:, :, c0 + m * 128:c0 + (m + 1) * 128], writes=[(wname, sl)])
            pp, pk = (p.pA, "pA") if m % 2 == 0 else (p.pB, "pB")
            for k in range(nk):
                for n in range(3):
                    s.add("pe", lambda e, pp=pp, k=k, n=n, sl=sl: e.matmul(
                        pp[n][:, 0:c.NT], lhsT=wbufs[sl][:, k, :], rhs=xin[:, k, n * c.NT:(n + 1) * c.NT],
                        start=(k == 0), stop=(k == nk - 1)),
                        reads=[(wname, sl)] + xkeys, writes=[(pk, n)])
            epi(m, pp, pk)

    def ffn_phase(self, layer, t0):
        c = self.c
        p = Phase(self.nc)
        s = p.s
        self.norm_bufs(p)
        p.xa = p.sb([128, c.KC, c.TB], BF16)
        hid = p.sb([128, c.HC, c.TB], BF16)
        wA = [p.sb([128, c.KC, 256], BF16) for _ in range(2)]
        wB = [p.sb([128, c.KC, 256], BF16) for _ in range(2)]
        wD = [p.sb([128, c.HC, 128], BF16) for _ in range(2)]
        sg = [p.sb([128, c.TB], BF16) for _ in range(2)]
        p.pA = [p.ps() for _ in range(3)]
        p.pB = [p.ps() for _ in range(3)]
        self.norm_to_xa(p, self.H, "H", t0, layer, 2)
        xk = [("xa", k) for k in range(c.KC)]
        wgv = self.wg[layer].rearrange("(k p) m -> p k m", p=128)
        wuv = self.wu[layer].rearrange("(k p) m -> p k m", p=128)
        wdv = self.wd[layer].rearrange("(k p) m -> p k m", p=128)
        for cb in range(c.FFN // 256):
            sl = cb % 2
            s.dma("pool", wA[sl][:, :, :], wgv[:, :, cb * 256:(cb + 1) * 256], writes=[("wA", sl)])
            s.dma("pool", wB[sl][:, :, :], wuv[:, :, cb * 256:(cb + 1) * 256], writes=[("wB", sl)])
            for mi in range(2):
                j = cb * 2 + mi
                for wbuf, wkey, pp, pk in ((wA, "wA", p.pA, "pA"), (wB, "wB", p.pB, "pB")):
                    for k in range(c.KC):
                        for n in range(3):
                            s.add("pe", lambda e, wbuf=wbuf, pp=pp, k=k, n=n, mi=mi, sl=sl: e.matmul(
                                pp[n][:, 0:c.NT], lhsT=wbuf[sl][:, k, mi * 128:(mi + 1) * 128],
                                rhs=p.xa[:, k, n * c.NT:(n + 1) * c.NT], start=(k == 0), stop=(k == c.KC - 1)),
                                reads=[(wkey, sl)] + xk, writes=[(pk, n)])
                    for n in range(3):
                        if wkey == "wA":
                            s.add("act", lambda e, n=n, j=j: e.activation(out=sg[j % 2][:, n * c.NT:(n + 1) * c.NT], in_=p.pA[n][:, 0:c.NT], func=AF.Silu),
                                  reads=[("pA", n)], writes=[("sg", j % 2, n)])
                        else:
                            s.add("dve", lambda e, n=n, j=j: e.tensor_tensor(out=hid[:, j, n * c.NT:(n + 1) * c.NT], in0=p.pB[n][:, 0:c.NT],
                                                                            in1=sg[j % 2][:, n * c.NT:(n + 1) * c.NT], op=ALU.mult),
                                  reads=[("pB", n), ("sg", j % 2, n)], writes=[("hid", j)])
        hk = [("hid", j) for j in range(c.HC)]

        def epi(m, pp, pk):
            slot = m % 4
            for n in range(3):
                s.add("act", lambda e, n=n: e.activation(out=p.ch[slot][:, n * c.NT:(n + 1) * c.NT], in_=pp[n][:, 0:c.NT], func=AF.Copy),
                      reads=[(pk, n)], writes=[("ch", slot)])
            s.dma("sp", self.Fs[m * 128:(m + 1) * 128, t0:t0 + c.TB], p.ch[slot][:, :], reads=[("ch", slot)],
                  writes=[self.dkey("Fs", m, t0)], semkey=("st", slot))
        self.linear_fm(p, wdv, 0, c.KC, hid, hk, c.HC, wD, "wD", epi)
        self.norm_residual(p, self.Fs, "Fs", self.H, "H", t0, layer, 3)
        p.close()

    def inproj_phase(self, layer, t0):
        c = self.c
        e_i = layer // 2
        p = Phase(self.nc)
        s = p.s
        self.norm_bufs(p)
        p.xa = p.sb([128, c.KC, c.TB], BF16)
        wP = [p.sb([128, c.KC, 128], BF16) for _ in range(2)]
        wV = p.sb([128, c.KC, 512], BF16)
        wF = p.sb([128, c.KC, c.NH], BF16)
        stb = [p.sb([128, c.TB], BF16) for _ in range(2)]
        vst = [p.sb([128, 512], BF16) for _ in range(2)]
        fst = [p.sb([128, c.NH], F32) for _ in range(2)]
        p.pA = [p.ps() for _ in range(3)]
        p.pB = [p.ps() for _ in range(3)]
        pF = p.ps()
        self.norm_to_xa(p, self.H, "H", t0, layer, 0)
        xk = [("xa", k) for k in range(c.KC)]
        wv = self.win[e_i].rearrange("(k p) m -> p k m", p=128)
        targets = [(self.QT, 0, c.AC), (self.KT, c.AW, c.AC)] + [(self.GX[i], 3 * c.AW + c.NH + i * c.CW, c.CC) for i in range(3)]
        cnt = [0]
        for dst, c0, nch in targets:
            def epi(m, pp, pk, dst=dst):
                sl = cnt[0] % 2
                cnt[0] += 1
                for n in range(3):
                    s.add("act", lambda e, n=n, sl=sl: e.activation(out=stb[sl][:, n * c.NT:(n + 1) * c.NT], in_=pp[n][:, 0:c.NT], func=AF.Copy),
                          reads=[(pk, n)], writes=[("stb", sl)])
                s.dma("sp", dst[m * 128:(m + 1) * 128, t0:t0 + c.TB], stb[sl][:, :], reads=[("stb", sl)], writes=[("o", "fm")], semkey=("stb", sl))
            self.linear_fm(p, wv, c0, nch, p.xa, xk, c.KC, wP, "wP", epi)
        s.dma("pool", wF[:, :, :], wv[:, :, 3 * c.AW:3 * c.AW + c.NH], writes=["wF"])
        for vb in range(-(-c.AW // 512)):
            vw = min(512, c.AW - vb * 512)
            s.dma("pool", wV[:, :, 0:vw], wv[:, :, 2 * c.AW + vb * 512:2 * c.AW + vb * 512 + vw], writes=["wV"])
            for tt in range(c.TT):
                r0 = tt * 128
                rows = min(128, c.TB - r0)
                pp, pk = (p.pA[tt % 3], ("pA", tt % 3)) if (tt // 3) % 2 == 0 else (p.pB[tt % 3], ("pB", tt % 3))
                for k in range(c.KC):
                    s.add("pe", lambda e, pp=pp, k=k, r0=r0, rows=rows, vw=vw: e.matmul(
                        pp[0:rows, 0:vw], lhsT=p.xa[:, k, r0:r0 + rows], rhs=wV[:, k, 0:vw], start=(k == 0), stop=(k == c.KC - 1)),
                        reads=["wV"] + xk, writes=[pk])
                sl = tt % 2
                s.add("act", lambda e, pp=pp, rows=rows, vw=vw, sl=sl: e.activation(out=vst[sl][0:rows, 0:vw], in_=pp[0:rows, 0:vw], func=AF.Copy),
                      reads=[pk], writes=[("vst", sl)])
                s.dma("sp", self.VT[t0 + r0:t0 + r0 + rows, vb * 512:vb * 512 + vw], vst[sl][0:rows, 0:vw], reads=[("vst", sl)],
                      writes=[("o", "v")], semkey=("vst", sl))
                if vb == 0:
                    for k in range(c.KC):
                        s.add("pe", lambda e, k=k, r0=r0, rows=rows: e.matmul(
                            pF[0:rows, 0:c.NH], lhsT=p.xa[:, k, r0:r0 + rows], rhs=wF[:, k, :], start=(k == 0), stop=(k == c.KC - 1)),
                            reads=["wF"] + xk, writes=["pF"])
                    s.add("dve", lambda e, rows=rows, sl=sl: e.tensor_copy(out=fst[sl][0:rows, :], in_=pF[0:rows, 0:c.NH]),
                          reads=["pF"], writes=[("fst", sl)])
                    s.dma("sp", self.FG[t0 + r0:t0 + r0 + rows, :], fst[sl][0:rows, :], reads=[("fst", sl)], writes=[("o", "f")], semkey=("fst", sl))
        p.close()

    def conv_phase(self, layer):
        c = self.c
        e_i = layer // 2
        p = Phase(self.nc)
        s = p.s
        cwt = p.sb([128, c.CC * 4], F32)
        g = [[p.sb([128, c.LP], BF16) for _ in range(3)] for _ in range(2)]
        z = [p.sb([128, c.LP + 2], F32) for _ in range(2)]
        acc = [p.sb([128, c.LP], F32) for _ in range(2)]
        ob = [p.sb([128, c.LP], BF16) for _ in range(2)]
        s.dma("sp", cwt[:, :], self.cw[e_i], writes=["cwt"])
        for b in range(2):
            s.add("dve", lambda e, b=b: e.memset(z[b][:, 0:2], 0.0), writes=[("zpad", b)])
        for cc in range(c.CC):
            b = cc % 2
            for i in range(3):
                s.dma("sp", g[b][i][:, :], self.GX[i, cc * 128:(cc + 1) * 128, :], writes=[("g", b, i)])
            w = lambda j, cc=cc: cwt[:, cc * 4 + j:cc * 4 + j + 1]
            s.add("dve", lambda e, b=b: e.tensor_tensor(out=z[b][:, 2:], in0=g[b][1][:, :], in1=g[b][2][:, :], op=ALU.mult),
                  reads=[("g", b, 1), ("g", b, 2)], writes=[("z", b)])
            s.add("dve", lambda e, b=b, w=w: e.tensor_scalar(out=acc[b][:, :], in0=z[b][:, 2:], scalar1=w(2), scalar2=w(3), op0=ALU.mult, op1=ALU.add),
                  reads=[("z", b), "cwt"], writes=[("acc", b)])
            s.add("dve", lambda e, b=b, w=w: e.scalar_tensor_tensor(out=acc[b][:, :], in0=z[b][:, 1:c.LP + 1], scalar=w(1), in1=acc[b][:, :], op0=ALU.mult, op1=ALU.add),
                  reads=[("z", b), ("zpad", b), ("acc", b)], writes=[("acc", b)])
            s.add("dve", lambda e, b=b, w=w: e.scalar_tensor_tensor(out=acc[b][:, :], in0=z[b][:, 0:c.LP], scalar=w(0), in1=acc[b][:, :], op0=ALU.mult, op1=ALU.add),
                  reads=[("z", b), ("zpad", b), ("acc", b)], writes=[("acc", b)])
            s.add("dve", lambda e, b=b: e.tensor_tensor(out=ob[b][:, :], in0=acc[b][:, :], in1=g[b][0][:, :], op=ALU.mult),
                  reads=[("acc", b), ("g", b, 0)], writes=[("ob", b)])
            s.dma("sp", self.CAT[c.AW + cc * 128:c.AW + (cc + 1) * 128, :], ob[b][:, :], reads=[("ob", b)], writes=[("o", cc)], semkey=("ob", b))
        p.close()

    def attn_phase(self, layer):
        c = self.c
        e_i = layer // 2
        p = Phase(self.nc)
        s = p.s
        NKT, NQB, NH = c.NKT, c.NQB, c.NH
        cs = p.sb([128, 4 * 128 + 3 * 384], F32)
        msk = p.sb([128, 3, 384], BF16)
        tri, onesf, sel127, bsel = cs[:, 0:128], cs[:, 128:256], cs[:, 256:384], cs[:, 384:512]
        s.dma("sp", cs[:, :], self.cst, writes=["cs"])
        s.add("dve", lambda e: e.tensor_copy(out=msk[:, :, :], in_=cs[:, 512:512 + 1152].rearrange("p (j t) -> p j t", j=3)),
              reads=["cs"], writes=["msk"])
        bft = p.sb([128, NH], F32)
        lf = p.sb([128, NKT, NH], F32)
        tot = p.sb([128, NH], F32)
        cref = p.sb([128, 3, NH], F32)
        bias = [[p.sb([128, NKT, NH], F32) for _ in range(3)] for _ in range(2)]
        qt = [p.sb([128, c.LP], BF16) for _ in range(2)]
        kt_ = [p.sb([128, c.LP], BF16) for _ in range(2)]
        va = [p.sb([128, NKT, 192], BF16) for _ in range(2)]
        pt = [p.sb([128, 384], BF16) for _ in range(4)]
        rec = p.sb([128, 384], F32)
        recb = p.sb([128, 384], F32)
        ao = [p.sb([128, 384], BF16) for _ in range(2)]
        pS = [p.ps() for _ in range(3)]
        pO = [p.ps() for _ in range(2)]
        pY = p.ps()
        pC = p.ps()
        s.dma("sp", bft[:, :], self.bf[e_i], writes=["bft"])
        s.dma("sp", lf[:, :, :], self.FG.rearrange("(j p) h -> p j h", p=128), writes=["lf"])
        s.add("dve", lambda e: e.tensor_tensor(out=lf[:, :, :], in0=lf[:, :, :], in1=bft[:, :].unsqueeze(1).to_broadcast([128, NKT, NH]), op=ALU.add),
              reads=["lf", "bft"], writes=["lf"])
        s.add("act", lambda e: e.activation(out=lf[:, :, :], in_=lf[:, :, :], func=AF.Exp, scale=-1.0), reads=["lf"], writes=["lf"])
        s.add("act", lambda e: e.activation(out=lf[:, :, :], in_=lf[:, :, :], func=AF.Ln, bias=1.0), reads=["lf"], writes=["lf"])
        s.add("dve", lambda e: e.tensor_single_scalar(out=lf[:, :, :], in_=lf[:, :, :], scalar=-1.0, op=ALU.mult), reads=["lf"], writes=["lf"])
        s.add("dve", lambda e: e.memset(tot[:, :], 0.0), writes=["tot"])
        for j in range(NKT):
            s.add("pe", lambda e, j=j: e.matmul(pC[:, 0:NH], lhsT=tri, rhs=lf[:, j, :], start=True, stop=True), reads=["lf", ("lfj", j), "cs"], writes=["pC"])
            s.add("pe", lambda e, j=j: e.matmul(pC[:, NH:2 * NH], lhsT=onesf, rhs=lf[:, j, :], start=True, stop=True), reads=["lf", ("lfj", j), "cs"], writes=["pC2"])
            s.add("dve", lambda e, j=j: e.tensor_tensor(out=lf[:, j, :], in0=pC[:, 0:NH], in1=tot[:, :], op=ALU.add), reads=["pC", "tot"], writes=[("lfj", j)])
            s.add("dve", lambda e: e.tensor_tensor(out=tot[:, :], in0=pC[:, NH:2 * NH], in1=tot[:, :], op=ALU.add), reads=["pC2", "tot"], writes=["tot"])
        ckeys = [("lfj", j) for j in range(NKT)]
        for hp in range(c.AC):
            b = hp % 2
            s.dma("sp", qt[b][:, :], self.QT[hp * 128:(hp + 1) * 128, :], writes=[("qt", b)])
            s.dma("sp", kt_[b][:, :], self.KT[hp * 128:(hp + 1) * 128, :], writes=[("kt", b)])
            if hp < 2:
                s.add("dve", lambda e, b=b: e.memset(va[b][:, :, 64:128], 0.0), writes=[("vaE", b)])
                s.add("dve", lambda e, b=b: e.memset(va[b][:, :, 64:65], 1.0), reads=[("vaE", b)], writes=[("vaE", b)])
            vsrc = self.VT.rearrange("(j p) w -> p j w", p=128)
            s.dma("sp", va[b][:, :, 0:64], vsrc[:, :, hp * 128:hp * 128 + 64], writes=[("vaA", b)])
            s.dma("sp", va[b][:, :, 128:192], vsrc[:, :, hp * 128 + 64:hp * 128 + 128], writes=[("vaB", b)])
            for qb in range(NQB):
                q0 = qb * 384
                nkt = 3 * qb + 3
                bb = bias[qb % 2]
                for j2 in range(3):
                    nk2 = 3 * qb + j2 + 1
                    s.add("pe", lambda e, qb=qb, j2=j2: e.matmul(pC[:, (2 + j2) * NH:(3 + j2) * NH], lhsT=sel127, rhs=lf[:, 3 * qb + j2, :], start=True, stop=True),
                          reads=ckeys + ["cs"], writes=[("pC3", j2)])
                    s.add("dve", lambda e, j2=j2: e.tensor_copy(out=cref[:, j2, :], in_=pC[:, (2 + j2) * NH:(3 + j2) * NH]), reads=[("pC3", j2)], writes=[("cref", j2)])
                    s.add("dve", lambda e, bb=bb, j2=j2, nk2=nk2: e.tensor_tensor(out=bb[j2][:, 0:nk2, :], in0=cref[:, j2, :].unsqueeze(1).to_broadcast([128, nk2, NH]),
                                                                            in1=lf[:, 0:nk2, :], op=ALU.subtract),
                          reads=[("cref", j2)] + ckeys, writes=[("bias", qb % 2, j2)])
                for hh in range(2):
                    h = hp * 2 + hh
                    r0 = hh * 64
                    lcols = slice(0, 128) if hh == 0 else slice(64, 192)
                    LA = 2

                    def emit_scores(kt, r0=r0, b=b, q0=q0):
                        sp_ = pS[kt % 3]
                        s.add("pe", lambda e, sp_=sp_, kt=kt: e.matmul(
                            sp_[:, 0:384], lhsT=kt_[b][r0:r0 + 64, kt * 128:(kt + 1) * 128], rhs=qt[b][r0:r0 + 64, q0:q0 + 384], start=True, stop=True),
                            reads=[("qt", b), ("kt", b)], writes=[("pS", kt % 3)])

                    def emit_softmax_pv(kt, hh=hh, h=h, b=b, bb=bb, qb=qb, nkt=nkt, lcols=lcols):
                        sp_ = pS[kt % 3]
                        pb = pt[kt % 4]
                        jd = kt - 3 * qb
                        if jd > 0:
                            s.add("dve", lambda e, pb=pb, jd=jd: e.memset(pb[:, 0:128 * jd], 0.0), reads=[], writes=[("pt", kt % 4)])
                        for j2 in range(max(jd, 0), 3):
                            s.add("act", lambda e, sp_=sp_, pb=pb, kt=kt, j2=j2: e.activation(
                                out=pb[:, 128 * j2:128 * (j2 + 1)], in_=sp_[:, 128 * j2:128 * (j2 + 1)], func=AF.Exp, bias=bb[j2][:, kt, h:h + 1], scale=0.125),
                                reads=[("pS", kt % 3), ("bias", qb % 2, j2)], writes=[("pt", kt % 4)])
                        if kt >= 3 * qb:
                            s.add("dve", lambda e, pb=pb, kt=kt: e.tensor_tensor(out=pb[:, :], in0=pb[:, :], in1=msk[:, kt - 3 * qb, :], op=ALU.mult),
                                  reads=[("pt", kt % 4), "msk"], writes=[("pt", kt % 4)])
                        s.add("pe", lambda e, pb=pb, kt=kt: e.matmul(
                            pO[hh][:, 0:384], lhsT=va[b][:, kt, lcols], rhs=pb[:, :], start=(kt == 0), stop=(kt == nkt - 1)),
                            reads=[("pt", kt % 4), ("vaA", b), ("vaB", b), ("vaE", b)], writes=[("pO", hh)])

                    for i in range(nkt + LA):
                        if i < nkt:
                            emit_scores(i)
                        if i - LA >= 0:
                            emit_softmax_pv(i - LA)
                s.add("dve", lambda e: e.reciprocal(out=rec[64:65, :], in_=pO[0][64:65, 0:384]), reads=[("pO", 0)], writes=["recA"])
                s.add("dve", lambda e: e.reciprocal(out=rec[0:1, :], in_=pO[1][0:1, 0:384]), reads=[("pO", 1)], writes=["recB"])
                s.add("pe", lambda e: e.matmul(pY[:, 0:384], lhsT=bsel[64:65, :], rhs=rec[64:65, :], start=True, stop=False),
                      reads=["recA", "cs"], writes=["pY"])
                s.add("pe", lambda e: e.matmul(pY[:, 0:384], lhsT=bsel[0:1, :], rhs=rec[0:1, :], start=False, stop=True),
                      reads=["recB", "cs"], writes=["pY"])
                s.add("act", lambda e: e.activation(out=recb[:, :], in_=pY[:, 0:384], func=AF.Copy), reads=["pY"], writes=["recb"])
                a_ = ao[qb % 2]
                s.add("dve", lambda e, a_=a_: e.tensor_tensor(out=a_[0:64, :], in0=pO[0][0:64, 0:384], in1=recb[0:64, :], op=ALU.mult),
                      reads=[("pO", 0), "recb"], writes=[("ao", qb % 2, 0)])
                s.add("dve", lambda e, a_=a_: e.tensor_tensor(out=a_[64:128, :], in0=pO[1][64:128, 0:384], in1=recb[64:128, :], op=ALU.mult),
                      reads=[("pO", 1), "recb"], writes=[("ao", qb % 2, 1)])
                s.dma("sp", self.CAT[hp * 128:(hp + 1) * 128, q0:q0 + 384], a_[:, :], reads=[("ao", qb % 2, 0), ("ao", qb % 2, 1)],
                      writes=[("o", hp, qb)], semkey=("ao", qb % 2))
        p.close()

    def outproj_phase(self, layer, t0, wmat, src):
        c = self.c
        p = Phase(self.nc)
        s = p.s
        self.norm_bufs(p)
        xin = p.sb([128, c.KC, c.TB], BF16)
        wP = [p.sb([128, c.KC, 128], BF16) for _ in range(2)]
        p.pA = [p.ps() for _ in range(3)]
        p.pB = [p.ps() for _ in range(3)]
        for k in range(c.KC):
            s.dma("sp", xin[:, k, :], src[k * 128:(k + 1) * 128, t0:t0 + c.TB], writes=[("xin", k)], semkey=("xin", k % 4))
        xk = [("xin", k) for k in range(c.KC)]

        def epi(m, pp, pk):
            slot = m % 4
            for n in range(3):
                s.add("act", lambda e, n=n: e.activation(out=p.ch[slot][:, n * c.NT:(n + 1) * c.NT], in_=pp[n][:, 0:c.NT], func=AF.Copy),
                      reads=[(pk, n)], writes=[("ch", slot)])
            s.dma("sp", self.Fs[m * 128:(m + 1) * 128, t0:t0 + c.TB], p.ch[slot][:, :], reads=[("ch", slot)],
                  writes=[self.dkey("Fs", m, t0)], semkey=("st", slot))
        self.linear_fm(p, wmat.rearrange("(k p) m -> p k m", p=128), 0, c.KC, xin, xk, c.KC, wP, "wP", epi)
        self.norm_residual(p, self.Fs, "Fs", self.H, "H", t0, layer, 1)
        p.close()


    def s5_norm_phase(self, layer, t0):
        c = self.c
        p = Phase(self.nc)
        s = p.s
        self.norm_bufs(p)
        p.xa = p.sb([128, c.KC, c.TB], BF16)
        xs = [p.sb([128, 8, c.TBC], BF16) for _ in range(2)]
        self.norm_to_xa(p, self.H, "H", t0, layer, 0)
        n0 = t0 // 8
        for k in range(c.KC):
            b = k % 2
            s.dma("sp", self.UT[k * 128:(k + 1) * 128, t0:t0 + c.TB], p.xa[:, k, :], reads=[("xa", k)], writes=[("o", "ut", k)], semkey=("ut", b))
            s.add("pool", lambda e, k=k, b=b: e.tensor_copy(out=xs[b][:, :, :], in_=p.xa[:, k, :].rearrange("p (n s) -> p s n", s=8)),
                  reads=[("xa", k)], writes=[("xs", b)])
            s.dma("sp", self.U8D[k, :, :, n0:n0 + c.TBC].rearrange("s p n -> p s n"), xs[b][:, :, :], reads=[("xs", b)], writes=[("o", "u8", k)], semkey=("xs", b))
        p.close()

    def s5_core_phase(self, layer, gb):
        c = self.c
        o_i = layer // 2
        GB, NCH = c.GB, c.NCH
        NH2 = NCH // 2
        g0 = gb * GB
        p = Phase(self.nc)
        s = p.s
        cnt = [0]

        def op(eng, fn, reads, writes):
            s.add(eng, fn, reads=reads, writes=writes)

        def T(shape, dtype=F32):
            return p.sb(shape, dtype)

        def tt(out, a, b, alu, eng="dve"):
            op(eng, lambda e: e.tensor_tensor(out=out[0], in0=a[0], in1=b[0], op=alu), [a[1], b[1]], [out[1]])

        def key(t, name):
            return (t, name)

        def new(name, shape=None, dtype=F32):
            t = T([64] + list(shape or [GB]), dtype)
            return (t[:], name)

        prm = T([64, 3, c.G])
        bt = T([64, 2, GB, 16])
        ct = T([64, 2, GB, 16])
        c2 = T([128, 256])
        tmask = T([128, 128], BF16)
        ident = T([128, 128], BF16)
        s.dma("sp", prm[:, :, :], self.s5p[o_i], writes=["prm"])
        s.dma("sp", bt[:, :, :, :], self.s5b[o_i, :, :, g0:g0 + GB, :], writes=["bt"])
        s.dma("sp", ct[:, :, :, :], self.s5c[o_i, :, :, g0:g0 + GB, :], writes=["ct"])
        s.dma("sp", c2[:, :], self.cst2, writes=["c2"])
        op("dve", lambda e: e.tensor_copy(out=tmask[:, :], in_=c2[:, 0:128]), ["c2"], ["tmask"])
        op("dve", lambda e: e.tensor_copy(out=ident[:, :], in_=c2[:, 128:256]), ["c2"], ["ident"])
        are = (prm[:, 0, g0:g0 + GB], "prm")
        aim = (prm[:, 1, g0:g0 + GB], "prm")
        lst = (prm[:, 2, g0:g0 + GB], "prm")
        lre, dl, ex, mag, ang = new("lre"), new("dl"), new("ex"), new("mag"), new("ang")
        op("dve", lambda e: e.tensor_scalar_min(out=lre[0], in0=are[0], scalar1=-1e-4), ["prm"], ["lre"])
        op("act", lambda e: e.activation(out=dl[0], in_=lst[0], func=AF.Exp), ["prm"], ["dl"])
        tt(ex, lre, dl, ALU.mult)
        op("act", lambda e: e.activation(out=mag[0], in_=ex[0], func=AF.Exp), ["ex"], ["mag"])
        tt(ang, aim, dl, ALU.mult)

        def sin_of(name, src, shift):
            r, m = new(name + "r"), new(name + "m")
            op("dve", lambda e: e.tensor_scalar_add(out=r[0], in0=src[0], scalar1=shift), [src[1]], [r[1]])
            for _ in range(4):
                op("dve", lambda e: e.tensor_single_scalar(out=m[0], in_=r[0], scalar=math.pi, op=ALU.is_gt), [r[1]], [m[1]])
                op("dve", lambda e: e.scalar_tensor_tensor(out=r[0], in0=m[0], scalar=-2.0 * math.pi, in1=r[0], op0=ALU.mult, op1=ALU.add), [m[1], r[1]], [r[1]])
            o = new(name)
            op("act", lambda e: e.activation(out=o[0], in_=r[0], func=AF.Sin), [r[1]], [o[1]])
            return o
        sn = sin_of("sn", ang, 0.0)
        cs_ = sin_of("cs", ang, 0.5 * math.pi)
        lbr, lbi = new("lbr"), new("lbi")
        tt(lbr, mag, cs_, ALU.mult)
        tt(lbi, mag, sn, ALU.mult)
        den, t1, t2, nr, cr, ci = new("den"), new("t1"), new("t2"), new("nr"), new("cr"), new("ci")
        tt(den, lre, lre, ALU.mult)
        tt(t1, aim, aim, ALU.mult)
        tt(den, den, t1, ALU.add)
        op("dve", lambda e: e.reciprocal(out=den[0], in_=den[0]), ["den"], ["den"])
        op("dve", lambda e: e.tensor_scalar_add(out=nr[0], in0=lbr[0], scalar1=-1.0), ["lbr"], ["nr"])
        tt(t1, nr, lre, ALU.mult)
        tt(t2, lbi, aim, ALU.mult)
        tt(cr, t1, t2, ALU.add)
        tt(cr, cr, den, ALU.mult)
        tt(t1, lbi, lre, ALU.mult)
        tt(t2, nr, aim, ALU.mult)
        tt(ci, t1, t2, ALU.subtract)
        tt(ci, ci, den, ALU.mult)
        bbr, bbi, u1, u2 = new("bbr", [GB, 16]), new("bbi", [GB, 16]), new("u1", [GB, 16]), new("u2", [GB, 16])
        bc16 = lambda t: (t[0].unsqueeze(2).to_broadcast([64, GB, 16]), t[1])
        b_re, b_im = (bt[:, 0, :, :], "bt"), (bt[:, 1, :, :], "bt")
        c_re, c_im = (ct[:, 0, :, :], "ct"), (ct[:, 1, :, :], "ct")

        def cmul(outr, outi, ar, ai, br, bi, neg_im=False):
            tt(u1, ar, br, ALU.mult)
            tt(u2, ai, bi, ALU.mult)
            tt(outr, u1, u2, ALU.subtract)
            tt(u1, ar, bi, ALU.mult)
            tt(u2, ai, br, ALU.mult)
            if neg_im:
                tt(outi, u1, u2, ALU.add)
                op("dve", lambda e: e.tensor_single_scalar(out=outi[0], in_=outi[0], scalar=-1.0, op=ALU.mult), [outi[1]], [outi[1]])
            else:
                tt(outi, u1, u2, ALU.add)
        cmul(bbr, bbi, bc16(cr), bc16(ci), b_re, b_im)
        v1, v2 = new("v1"), new("v2")
        Pw = [(new("p0r"), new("p0i"))]
        op("dve", lambda e: e.memset(Pw[0][0][0], 1.0), [], ["p0r"])
        op("dve", lambda e: e.memset(Pw[0][1][0], 0.0), [], ["p0i"])
        for k in range(1, 9):
            pr, pi = new("p%dr" % k), new("p%di" % k)
            a_r, a_i = Pw[k - 1]
            tt(v1, a_r, lbr, ALU.mult); tt(v2, a_i, lbi, ALU.mult); tt(pr, v1, v2, ALU.subtract)
            tt(v1, a_r, lbi, ALU.mult); tt(v2, a_i, lbr, ALU.mult); tt(pi, v1, v2, ALU.add)
            Pw.append((pr, pi))
        m8, i8r, i8i = new("m8"), new("i8r"), new("i8i")
        tt(m8, Pw[8][0], Pw[8][0], ALU.mult); tt(v1, Pw[8][1], Pw[8][1], ALU.mult); tt(m8, m8, v1, ALU.add)
        op("dve", lambda e: e.reciprocal(out=m8[0], in_=m8[0]), ["m8"], ["m8"])
        tt(i8r, Pw[8][0], m8, ALU.mult)
        tt(i8i, Pw[8][1], m8, ALU.mult)
        op("dve", lambda e: e.tensor_single_scalar(out=i8i[0], in_=i8i[0], scalar=-1.0, op=ALU.mult), ["i8i"], ["i8i"])
        Rj = []
        for j in range(8):
            rr, ri = new("r%dr" % j), new("r%di" % j)
            a_r, a_i = Pw[j + 1]
            tt(v1, a_r, i8r, ALU.mult); tt(v2, a_i, i8i, ALU.mult); tt(rr, v1, v2, ALU.subtract)
            tt(v1, a_r, i8i, ALU.mult); tt(v2, a_i, i8r, ALU.mult); tt(ri, v1, v2, ALU.add)
            Rj.append((rr, ri))
        A1r, A1i = T([64, GB, 8, 16], BF16), T([64, GB, 8, 16], BF16)
        W2r, W2n = T([64, GB, 8, 16], BF16), T([64, GB, 8, 16], BF16)
        Wsr, Wsn = T([64, GB, 8, 16], BF16), T([64, GB, 8, 16], BF16)
        fr, fi = new("fr", [GB, 16]), new("fi", [GB, 16])
        for q in range(8):
            cmul(fr, fi, bc16(Pw[7 - q][0]), bc16(Pw[7 - q][1]), bbr, bbi)
            op("act", lambda e, q=q: e.activation(out=A1r[:, :, q, :], in_=fr[0], func=AF.Copy), ["fr"], [("A1r", q)])
            op("act", lambda e, q=q: e.activation(out=A1i[:, :, q, :], in_=fi[0], func=AF.Copy), ["fi"], [("A1i", q)])
            cmul(fr, fi, bc16(Pw[q + 1][0]), bc16(Pw[q + 1][1]), c_re, c_im, neg_im=True)
            op("act", lambda e, q=q: e.activation(out=W2r[:, :, q, :], in_=fr[0], func=AF.Copy), ["fr"], [("W2r", q)])
            op("act", lambda e, q=q: e.activation(out=W2n[:, :, q, :], in_=fi[0], func=AF.Copy), ["fi"], [("W2n", q)])
            cmul(fr, fi, bc16(Rj[q][0]), bc16(Rj[q][1]), c_re, c_im, neg_im=True)
            op("act", lambda e, q=q: e.activation(out=Wsr[:, :, q, :], in_=fr[0], func=AF.Copy), ["fr"], [("Wsr", q)])
            op("act", lambda e, q=q: e.activation(out=Wsn[:, :, q, :], in_=fi[0], func=AF.Copy), ["fi"], [("Wsn", q)])
        allk = lambda nm: [(nm, q) for q in range(8)]
        flat = lambda t, gi: t[:, gi, :, :].rearrange("p s h -> p (s h)")
        Tb = T([128, GB, 128], BF16)
        W1b = T([128, GB, 128], BF16)
        pT = [p.ps() for _ in range(2)]
        pW = [p.ps([128, 128], BF16) for _ in range(2)]
        for gi in range(GB):
            b = gi % 2
            op("pe", lambda e, gi=gi, b=b: e.matmul(pT[b][:, 0:128], lhsT=flat(A1r, gi), rhs=flat(Wsr, gi), start=True, stop=False),
               allk("A1r") + allk("Wsr"), [("pT", b)])
            op("pe", lambda e, gi=gi, b=b: e.matmul(pT[b][:, 0:128], lhsT=flat(A1i, gi), rhs=flat(Wsn, gi), start=False, stop=True),
               allk("A1i") + allk("Wsn"), [("pT", b)])
            op("dve", lambda e, gi=gi, b=b: e.tensor_tensor(out=Tb[:, gi, :], in0=pT[b][:, 0:128], in1=tmask[:, :], op=ALU.mult),
               [("pT", b), "tmask"], [("Tb", gi)])
            op("pe", lambda e, gi=gi, b=b: e.transpose(pW[b][:, 0:64], flat(A1r, gi), ident[0:64, 0:64]), allk("A1r") + ["ident"], [("pW", b, 0)])
            op("pe", lambda e, gi=gi, b=b: e.transpose(pW[b][:, 64:128], flat(A1i, gi), ident[0:64, 0:64]), allk("A1i") + ["ident"], [("pW", b, 1)])
            op("act", lambda e, gi=gi, b=b: e.activation(out=W1b[:, gi, :], in_=pW[b][:, :], func=AF.Copy), [("pW", b, 0), ("pW", b, 1)], [("W1b", gi)])
        u8 = T([128, GB, NCH], BF16)
        for kl in range(GB // 8):
            kc = g0 // 8 + kl
            for q in range(8):
                s.dma("sp", u8[q * 16:(q + 1) * 16, kl * 8:(kl + 1) * 8, :], self.U8D[kc, q, :, :].rearrange("(gl h) n -> h gl n", h=16),
                      writes=[("u8", kl, q)], semkey=("u8", q % 4))
        u8k = [("u8", kl, q) for kl in range(GB // 8) for q in range(8)]
        Vb = T([64, 2, GB, NCH], BF16)
        Xb = T([64, 2, GB, NCH + 1], BF16)
        pV = [p.ps() for _ in range(2)]
        for gi in range(GB):
            for hf in range(2):
                for cc in range(2):
                    b = cc
                    op("pe", lambda e, gi=gi, hf=hf, cc=cc, b=b: e.matmul(pV[b][0:64, 0:NH2], lhsT=W1b[:, gi, cc * 64:(cc + 1) * 64],
                                                                       rhs=u8[:, gi, hf * NH2:(hf + 1) * NH2], start=True, stop=True),
                       [("W1b", gi)] + u8k, [("pV", b)])
                    op("act", lambda e, gi=gi, hf=hf, cc=cc, b=b: e.activation(out=Vb[:, cc, gi, hf * NH2:(hf + 1) * NH2], in_=pV[b][0:64, 0:NH2], func=AF.Copy),
                       [("pV", b)], [("Vb", gi)])
        st, w1, w2 = T([64, 2, GB]), T([64, 2, GB]), T([64, 2, GB])
        A8r, A8i = Pw[8]
        a8b = A8r[0].unsqueeze(1).to_broadcast([64, 2, GB])
        op("dve", lambda e: e.memset(st[:, :, :], 0.0), [], ["st"])
        op("dve", lambda e: e.memset(Xb[:, :, :, 0:1], 0.0), [], [("Xb", 0)])
        vk = [("Vb", gi) for gi in range(GB)]
        for n in range(NCH):
            op("dve", lambda e: e.tensor_tensor(out=w1[:, :, :], in0=st[:, :, :], in1=a8b, op=ALU.mult), ["st", "p8r"], ["w1"])
            op("dve", lambda e: e.tensor_tensor(out=w2[:, 0, :], in0=st[:, 1, :], in1=A8i[0], op=ALU.mult), ["st", "p8i"], ["w2a"])
            op("dve", lambda e: e.tensor_tensor(out=w2[:, 1, :], in0=st[:, 0, :], in1=A8i[0], op=ALU.mult), ["st", "p8i"], ["w2b"])
            op("dve", lambda e: e.tensor_tensor(out=w1[:, 0, :], in0=w1[:, 0, :], in1=w2[:, 0, :], op=ALU.subtract), ["w1", "w2a"], ["w1"])
            op("dve", lambda e: e.tensor_tensor(out=w1[:, 1, :], in0=w1[:, 1, :], in1=w2[:, 1, :], op=ALU.add), ["w1", "w2b"], ["w1"])
            op("dve", lambda e, n=n: e.tensor_tensor(out=st[:, :, :], in0=w1[:, :, :], in1=Vb[:, :, :, n], op=ALU.add), ["w1"] + (vk if n == 0 else []), ["st"])
            op("dve", lambda e, n=n: e.tensor_copy(out=Xb[:, :, :, n + 1], in_=st[:, :, :]), ["st"], [("Xb", n + 1)])
        xk = [("Xb", NCH)]
        y8 = T([128, GB, NCH], BF16)
        pY = [p.ps() for _ in range(2)]
        for gi in range(GB):
            for hf in range(2):
                b = hf
                sl = slice(hf * NH2, (hf + 1) * NH2)
                op("pe", lambda e, gi=gi, sl=sl, b=b: e.matmul(pY[b][:, 0:NH2], lhsT=Tb[:, gi, :], rhs=u8[:, gi, sl], start=True, stop=False),
                   [("Tb", gi)] + u8k, [("pY", b)])
                op("pe", lambda e, gi=gi, sl=sl, b=b: e.matmul(pY[b][:, 0:NH2], lhsT=flat(W2r, gi), rhs=Xb[:, 0, gi, sl], start=False, stop=False),
                   allk("W2r") + xk, [("pY", b)])
                op("pe", lambda e, gi=gi, sl=sl, b=b: e.matmul(pY[b][:, 0:NH2], lhsT=flat(W2n, gi), rhs=Xb[:, 1, gi, sl], start=False, stop=True),
                   allk("W2n") + xk, [("pY", b)])
                op("act", lambda e, gi=gi, sl=sl, b=b: e.activation(out=y8[:, gi, sl], in_=pY[b][:, 0:NH2], func=AF.Copy), [("pY", b)], [("y8", gi)])
        yk = [("y8", gi) for gi in range(GB)]
        for kl in range(GB // 8):
            kc = g0 // 8 + kl
            for q in range(8):
                s.dma("sp", self.Y8D[kc, q, :, :].rearrange("(gl h) n -> h gl n", h=16), y8[q * 16:(q + 1) * 16, kl * 8:(kl + 1) * 8, :],
                      reads=yk, writes=[("o", "y8", kl, q)], semkey=("y8o", q % 4))
        p.close()

    def s5_out_phase(self, layer, t0):
        c = self.c
        o_i = layer // 2
        p = Phase(self.nc)
        s = p.s
        self.norm_bufs(p)
        ga = p.sb([128, c.KC, c.TB], BF16)
        yt = [p.sb([128, 8, c.TBC], BF16) for _ in range(2)]
        ut = [p.sb([128, c.TB], BF16) for _ in range(2)]
        yv = [p.sb([128, c.TB], F32) for _ in range(2)]
        w_ = [p.sb([128, c.TB], F32) for _ in range(2)]
        wA = [p.sb([128, c.KC, 128], BF16) for _ in range(2)]
        wB = [p.sb([128, c.KC, 128], BF16) for _ in range(2)]
        sg = [p.sb([128, c.TB], BF16) for _ in range(2)]
        p.pA = [p.ps() for _ in range(3)]
        p.pB = [p.ps() for _ in range(3)]
        n0 = t0 // 8
        v3 = lambda ap: ap.rearrange("p (n j) -> p n j", j=8)
        for k in range(c.KC):
            b = k % 2
            s.dma("sp", yt[b][:, :, :], self.Y8D[k, :, :, n0:n0 + c.TBC].rearrange("j p n -> p j n"), writes=[("yt", b)])
            s.dma("sp", ut[b][:, :], self.UT[k * 128:(k + 1) * 128, t0:t0 + c.TB], writes=[("ut", b)])
            dcol = self.ds[:, o_i * c.KC + k:o_i * c.KC + k + 1]
            s.add("dve", lambda e, b=b, dcol=dcol: e.scalar_tensor_tensor(out=v3(yv[b][:, :]), in0=v3(ut[b][:, :]), scalar=dcol,
                                                                       in1=yt[b][:, :, :].rearrange("p j n -> p n j"), op0=ALU.mult, op1=ALU.add),
                  reads=[("yt", b), ("ut", b)], writes=[("yv", b)])
            s.add("dve", lambda e, b=b: e.tensor_tensor(out=w_[b][:, :], in0=yv[b][:, :], in1=yv[b][:, :], op=ALU.mult), reads=[("yv", b)], writes=[("w", b)])
            s.add("dve", lambda e, b=b: e.tensor_scalar(out=w_[b][:, :], in0=w_[b][:, :], scalar1=0.044715, scalar2=1.0, op0=ALU.mult, op1=ALU.add),
                  reads=[("w", b)], writes=[("w", b)])
            s.add("dve", lambda e, b=b: e.tensor_tensor(out=w_[b][:, :], in0=w_[b][:, :], in1=yv[b][:, :], op=ALU.mult), reads=[("w", b), ("yv", b)], writes=[("w", b)])
            s.add("act", lambda e, b=b: e.activation(out=w_[b][:, :], in_=w_[b][:, :], func=AF.Sigmoid, scale=1.5957691216057308), reads=[("w", b)], writes=[("w", b)])
            s.add("dve", lambda e, b=b, k=k: e.tensor_tensor(out=ga[:, k, :], in0=w_[b][:, :], in1=yv[b][:, :], op=ALU.mult), reads=[("w", b), ("yv", b)], writes=[("ga", k)])
        gk = [("ga", k) for k in range(c.KC)]
        w1v = self.wq1[o_i].rearrange("(k p) m -> p k m", p=128)
        w2v = self.wq2[o_i].rearrange("(k p) m -> p k m", p=128)
        for m in range(c.KC):
            sl = m % 2
            s.dma("pool", wA[sl][:, :, :], w2v[:, :, m * 128:(m + 1) * 128], writes=[("wA", sl)])
            s.dma("pool", wB[sl][:, :, :], w1v[:, :, m * 128:(m + 1) * 128], writes=[("wB", sl)])
            for wbuf, wkey, pp, pk in ((wA, "wA", p.pA, "pA"), (wB, "wB", p.pB, "pB")):
                for k in range(c.KC):
                    for n in range(3):
                        s.add("pe", lambda e, wbuf=wbuf, pp=pp, k=k, n=n, sl=sl: e.matmul(
                            pp[n][:, 0:c.NT], lhsT=wbuf[sl][:, k, :], rhs=ga[:, k, n * c.NT:(n + 1) * c.NT], start=(k == 0), stop=(k == c.KC - 1)),
                            reads=[(wkey, sl)] + gk, writes=[(pk, n)])
                slot = m % 4
                for n in range(3):
                    if wkey == "wA":
                        s.add("act", lambda e, n=n, m=m: e.activation(out=sg[m % 2][:, n * c.NT:(n + 1) * c.NT], in_=p.pA[n][:, 0:c.NT], func=AF.Sigmoid),
                              reads=[("pA", n)], writes=[("sg", m % 2, n)])
                    else:
                        s.add("dve", lambda e, n=n, m=m, slot=slot: e.tensor_tensor(out=p.ch[slot][:, n * c.NT:(n + 1) * c.NT], in0=p.pB[n][:, 0:c.NT],
                                                                                in1=sg[m % 2][:, n * c.NT:(n + 1) * c.NT], op=ALU.mult),
                              reads=[("pB", n), ("sg", m % 2, n)], writes=[("ch", slot)])
            s.dma("sp", self.Fs[m * 128:(m + 1) * 128, t0:t0 + c.TB], p.ch[m % 4][:, :], reads=[("ch", m % 4)],
                  writes=[self.dkey("Fs", m, t0)], semkey=("st", m % 4))
        self.norm_residual(p, self.Fs, "Fs", self.H, "H", t0, layer, 1)
        p.close()

    def build(self):
        c = self.c
        for q in range(c.NSEQ):
            p = Phase(self.nc)
            for k in range(c.KC):
                p.s.dma("sp", self.H[k * 128:(k + 1) * 128, :], self.xT[q, k * 128:(k + 1) * 128, :], writes=[("h", k)], semkey=("hinit",))
            p.close()
            for layer in range(c.DEPTH):
                if layer % 2 == 0 and "even" in self.stages:
                    for b in range(c.NB):
                        self.inproj_phase(layer, b * c.TB)
                    self.conv_phase(layer)
                    self.attn_phase(layer)
                    if self.dbg and layer == 0:
                        p = Phase(self.nc)
                        for i, (n, o) in enumerate(self.dbg_out.items()):
                            p.s.dma("sp", o, getattr(self, n), writes=[("dbg", i)], semkey=("dbg", i))
                        p.close()
                    for b in range(c.NB):
                        self.outproj_phase(layer, b * c.TB, self.wo[layer // 2], self.CAT)
                if layer % 2 == 1 and "s5" in self.stages:
                    for b in range(c.NB):
                        self.s5_norm_phase(layer, b * c.TB)
                    for gb in range(c.G // c.GB):
                        self.s5_core_phase(layer, gb)
                    if self.dbg and layer == 1:
                        p = Phase(self.nc)
                        for i, (n, o) in enumerate(self.dbg_out.items()):
                            p.s.dma("sp", o, getattr(self, n), writes=[("dbg", i)], semkey=("dbg", i))
                        p.close()
                    for b in range(c.NB):
                        self.s5_out_phase(layer, b * c.TB)
                if "ffn" in self.stages:
                    for b in range(c.NB):
                        self.ffn_phase(layer, b * c.TB)
                if self.dbg:
                    p = Phase(self.nc)
                    for k in range(c.KC):
                        p.s.dma("sp", self.hdump[q, layer, k * 128:(k + 1) * 128, :], self.H[k * 128:(k + 1) * 128, :], writes=[("hd", k)], semkey=("hd",))
                    p.close()
            p = Phase(self.nc)
            for k in range(c.KC):
                p.s.dma("sp", self.out[q, k * 128:(k + 1) * 128, :], self.H[k * 128:(k + 1) * 128, :], writes=[("o", k)], semkey=("hout",))
            p.close()
        return self.nc


N_META = 16
_DEBUG_DUMPS = False
_LAST = {}


def kernel(x, meta_tokens, norm_g, ab_w_in, ab_b_f, ab_conv_w, ab_conv_b, ab_w_o,
           s5_a_re, s5_a_im, s5_log_step, s5_b_re, s5_b_im, s5_c_re, s5_c_im,
           s5_d, s5_w_glu1, s5_w_glu2, ffn_w_gate, ffn_w_up, ffn_w_down):
    f = np.float32
    x = np.asarray(x, f)
    B, S, D = x.shape
    depth = norm_g.shape[0]
    cfg = Cfg(D=D, FFN=ffn_w_gate.shape[2], LP=4224, NT=352, NSEQ=1, DEPTH=depth)
    L = N_META + S
    assert L <= cfg.LP and cfg.PW == ab_w_in.shape[2]
    xT = np.zeros((B, D, cfg.LP), f)
    xT[:, :, :N_META] = np.asarray(meta_tokens, f).T[None]
    xT[:, :, N_META:L] = x.transpose(0, 2, 1)
    ngl = np.ascontiguousarray(np.asarray(norm_g, f).reshape(depth * 4, cfg.KC, 128).transpose(2, 0, 1).reshape(128, -1))
    NE = ab_w_in.shape[0]
    bfl = np.ascontiguousarray(np.broadcast_to(np.asarray(ab_b_f, f)[:, None, :], (NE, 128, cfg.NH)))
    cwl = np.zeros((NE, 128, cfg.CC * 4), f)
    cw3 = np.asarray(ab_conv_w, f).reshape(NE, 3, cfg.CC, 128)
    cwl.reshape(NE, 128, cfg.CC, 4)[:, :, :, :3] = cw3.transpose(0, 3, 2, 1)
    cwl.reshape(NE, 128, cfg.CC, 4)[:, :, :, 3] = np.asarray(ab_conv_b, f).reshape(NE, cfg.CC, 128).transpose(0, 2, 1)
    NO = s5_a_re.shape[0]
    G = cfg.G
    s5p = np.ascontiguousarray(np.stack([np.asarray(s5_a_re, f).transpose(0, 2, 1), np.asarray(s5_a_im, f).transpose(0, 2, 1),
                                         np.broadcast_to(np.asarray(s5_log_step, f)[:, None, :], (NO, 64, G))], 2))
    s5b = np.ascontiguousarray(np.stack([np.asarray(s5_b_re, f).transpose(0, 2, 1, 3), np.asarray(s5_b_im, f).transpose(0, 2, 1, 3)], 2))
    s5c = np.ascontiguousarray(np.stack([np.asarray(s5_c_re, f).transpose(0, 3, 1, 2), np.asarray(s5_c_im, f).transpose(0, 3, 1, 2)], 2))
    s5dl = np.ascontiguousarray(np.asarray(s5_d, f).reshape(NO, cfg.KC, 128).transpose(2, 0, 1).reshape(128, -1))
    ins = {"xT": xT, "ng": ngl, "wg": np.ascontiguousarray(ffn_w_gate, f), "wu": np.ascontiguousarray(ffn_w_up, f),
           "wd": np.ascontiguousarray(ffn_w_down, f), "win": np.ascontiguousarray(ab_w_in, f), "wo": np.ascontiguousarray(ab_w_o, f),
           "bf": bfl, "cw": cwl, "cst": host_constants(),
           "s5p": s5p, "s5b": s5b, "s5c": s5c, "s5d": s5dl, "wq1": np.ascontiguousarray(s5_w_glu1, f), "wq2": np.ascontiguousarray(s5_w_glu2, f),
           "cst2": host_constants2()}
    nc = Builder(cfg, stages=IMPLEMENTED_STAGES, dbg=_DEBUG_DUMPS).build()
    shared = {k: v for k, v in ins.items() if k != "xT"}
    in_maps = [dict(shared, xT=np.ascontiguousarray(xT[b:b + 1])) for b in range(B)]
    res = run_bass_kernel_spmd(nc, in_maps, core_ids=list(range(B)))
    out = np.concatenate([res.results[b]["out"] for b in range(B)], axis=0)
    if _DEBUG_DUMPS:
        _LAST.clear()
        _LAST.update(res.results[0])
    return np.ascontiguousarray(out[:, :, N_META:L].transpose(0, 2, 1)).astype(f)
```

```python
import contextlib
import math
import numpy as np
import concourse.bass as bass
import concourse.mybir as mybir
from concourse.bass_utils import run_bass_kernel_spmd

IMPLEMENTED_STAGES = ("even", "s5", "ffn")

F32 = mybir.dt.float32
BF16 = mybir.dt.bfloat16
AF = mybir.ActivationFunctionType
ALU = mybir.AluOpType
ENGS = ("pe", "act", "dve", "pool", "sp")
SAME_ENGINE_UNSYNCED = ("pe",)


class Sched:
    _seg = [0]

    def __init__(self, nc):
        self.nc = nc
        self.ops = []
        self.sem_handles = []

    def add(self, eng, fn, reads=(), writes=(), dma=False, semkey=None):
        if dma and semkey is None:
            semkey = writes[0]
        self.ops.append((eng, fn, tuple(reads), tuple(writes), dma, semkey))

    def dma(self, eng, out, in_, reads=(), writes=(), semkey=None):
        self.add(eng, lambda e: e.dma_start(out=out, in_=in_), reads, writes, True, semkey)

    def emit(self):
        nc, ops = self.nc, self.ops
        n = len(ops)
        last_w, readers = {}, {}
        deps = [None] * n
        has_dep = [False] * n
        for i, (eng, fn, rds, wrs, dma, sk) in enumerate(ops):
            d = set()
            for r in rds:
                if r in last_w:
                    d.add(last_w[r])
            for w in wrs:
                if w in last_w:
                    d.add(last_w[w])
                d.update(readers.get(w, ()))
            d.discard(i)
            d = {j for j in d if not (ops[j][0] == eng and eng in SAME_ENGINE_UNSYNCED and not ops[j][4] and not dma)}
            deps[i] = d
            for j in d:
                has_dep[j] = True
            for r in rds:
                readers.setdefault(r, []).append(i)
            for w in wrs:
                last_w[w] = i
                readers[w] = []
        eng_cnt = {e: 0 for e in ENGS}
        dma_cnt = {}
        ticket = [None] * n
        for i, (eng, fn, rds, wrs, dma, sk) in enumerate(ops):
            if dma:
                dma_cnt[sk] = dma_cnt.get(sk, 0) + 16
                ticket[i] = (("d", sk), dma_cnt[sk])
            elif has_dep[i]:
                eng_cnt[eng] += 1
                ticket[i] = (("e", eng), eng_cnt[eng])
        semnames = [("e", e) for e in ENGS if eng_cnt[e]] + [("d", k) for k in dma_cnt]
        self.nsems = len(semnames)
        self.sem_handles = [nc.alloc_semaphore(name="s%d_%d" % (Sched._seg[0], i)) for i in range(len(semnames))]
        Sched._seg[0] += 1
        sems = dict(zip(semnames, self.sem_handles))
        with nc.Block() as block:
            per_eng = {e: [i for i in range(n) if ops[i][0] == e] for e in ENGS}

            def run(eng_name, e):
                waited = {}
                for i in per_eng[eng_name]:
                    need = {}
                    for j in deps[i]:
                        sn, v = ticket[j]
                        if need.get(sn, 0) < v:
                            need[sn] = v
                    for sn, v in need.items():
                        if waited.get(sn, 0) < v:
                            e.wait_ge(sems[sn], v)
                            waited[sn] = v
                    ins = ops[i][1](e)
                    if ticket[i] is not None:
                        ins.then_inc(sems[ticket[i][0]], 16 if ops[i][4] else 1)
                for i in per_eng[eng_name]:
                    if ops[i][4]:
                        sn = ticket[i][0]
                        tot = dma_cnt[ops[i][5]]
                        if waited.get(sn, 0) < tot:
                            e.wait_ge(sems[sn], tot)
                            waited[sn] = tot

            for name, reg in (("sp", block.sync), ("pe", block.tensor), ("act", block.scalar),
                              ("dve", block.vector), ("pool", block.gpsimd)):
                if per_eng[name]:
                    reg(lambda e, name=name: run(name, e))


def host_constants():
    f = np.float32
    kk, mm = np.arange(128)[:, None], np.arange(128)[None, :]
    bsel = np.zeros((128, 128), f)
    bsel[64, :64] = 1
    bsel[0, 64:] = 1
    masks = [(128 * j + np.arange(128)[:, None] <= np.arange(384)[None, :]).astype(f) for j in range(3)]
    return np.ascontiguousarray(np.concatenate([(kk <= mm).astype(f), np.ones((128, 128), f),
                                                np.broadcast_to(kk == 127, (128, 128)).astype(f), bsel] + masks, 1))


def host_constants2():
    r = np.arange(128)
    tmask = ((r[None, :] // 16) >= (r[:, None] // 16)).astype(np.float32)
    return np.ascontiguousarray(np.concatenate([tmask, np.eye(128, dtype=np.float32)], 1))


class Cfg:
    def __init__(self, D=2048, FFN=5632, LP=4224, NT=352, NSEQ=1, DEPTH=4):
        self.D, self.FFN, self.LP, self.NT, self.NSEQ, self.DEPTH = D, FFN, LP, NT, NSEQ, DEPTH
        self.TB = 3 * NT
        assert LP % self.TB == 0 and D % 256 == 0 and FFN % 256 == 0 and LP % 384 == 0
        self.NB = LP // self.TB
        self.KC = D // 128
        self.HC = FFN // 128
        self.AW = D // 2
        self.CW = D // 2
        self.NH = self.AW // 64
        self.AC = self.AW // 128
        self.CC = self.CW // 128
        self.PW = 3 * self.AW + self.NH + 3 * self.CW
        self.NKT = LP // 128
        self.NQB = LP // 384
        self.TT = -(-self.TB // 128)
        self.G = D // 16
        self.GB = 16
        self.NCH = LP // 8
        self.TBC = self.TB // 8
        assert self.TB % 8 == 0 and self.G % self.GB == 0 and self.NCH % 2 == 0 and self.NCH // 2 <= 512
        self.EPS = 1e-6


class Phase:
    _uid = [0]

    def __init__(self, nc):
        self.nc = nc
        self.st = contextlib.ExitStack()
        self.s = Sched(nc)

    def _name(self, pfx):
        Phase._uid[0] += 1
        return "%s%d" % (pfx, Phase._uid[0])

    def sb(self, shape, dtype):
        return self.st.enter_context(self.nc.sbuf_tensor(self._name("t"), list(shape), dtype))

    def ps(self, shape=(128, 512), dtype=F32):
        return self.st.enter_context(self.nc.psum_tensor(self._name("p"), list(shape), dtype))

    def close(self):
        self.s.emit()
        self.st.close()
        self.nc.all_engine_barrier()
        self.nc.clear_and_free_semaphores(self.s.sem_handles)
        self.nc.all_engine_barrier()


class Builder:
    def __init__(self, cfg, stages, dbg=False):
        self.c = c = cfg
        self.nc = nc = bass.Bass("TRN2", target_bir_lowering=False)
        self.stages = stages
        self.dbg = dbg
        D, LP = c.D, c.LP
        dt = nc.dram_tensor
        NE = (c.DEPTH + 1) // 2
        ext = lambda name, shape: dt(name, list(shape), F32, kind="ExternalInput").ap()
        self.xT = ext("xT", [c.NSEQ, D, LP])
        self.ng = ext("ng", [128, c.DEPTH * 4 * c.KC])
        self.wg = ext("wg", [c.DEPTH, D, c.FFN])
        self.wu = ext("wu", [c.DEPTH, D, c.FFN])
        self.wd = ext("wd", [c.DEPTH, c.FFN, D])
        self.win = ext("win", [NE, D, c.PW])
        self.wo = ext("wo", [NE, D, D])
        self.bf = ext("bf", [NE, 128, c.NH])
        self.cw = ext("cw", [NE, 128, c.CC * 4])
        self.cst = ext("cst", [128, 4 * 128 + 3 * 384])
        NO = max(c.DEPTH // 2, 1)
        self.s5p = ext("s5p", [NO, 64, 3, c.G])
        self.s5b = ext("s5b", [NO, 64, 2, c.G, 16])
        self.s5c = ext("s5c", [NO, 64, 2, c.G, 16])
        self.s5d = ext("s5d", [128, NO * c.KC])
        self.wq1 = ext("wq1", [NO, D, D])
        self.wq2 = ext("wq2", [NO, D, D])
        self.cst2 = ext("cst2", [128, 256])
        self.out = dt("out", [c.NSEQ, D, LP], F32, kind="ExternalOutput").ap()
        self.UT = dt("UT", [D, LP], BF16).ap()
        self.U8D = dt("U8D", [c.KC, 8, 128, c.NCH], BF16).ap()
        self.Y8D = dt("Y8D", [c.KC, 8, 128, c.NCH], BF16).ap()
        self.H = dt("H", [D, LP], F32).ap()
        self.Fs = dt("Fs", [D, LP], F32).ap()
        self.QT = dt("QT", [c.AW, LP], BF16).ap()
        self.KT = dt("KT", [c.AW, LP], BF16).ap()
        self.GX = dt("GX", [3, c.CW, LP], BF16).ap()
        self.VT = dt("VT", [LP, c.AW], BF16).ap()
        self.FG = dt("FG", [LP, c.NH], F32).ap()
        self.CAT = dt("CAT", [D, LP], BF16).ap()
        if dbg:
            self.hdump = dt("hdump", [c.NSEQ, c.DEPTH, D, LP], F32, kind="ExternalOutput").ap()
            self.dbg_out = {n: dt("d" + n, list(a.shape), a.dtype, kind="ExternalOutput").ap()
                            for n, a in (("QT", self.QT), ("KT", self.KT), ("GX", self.GX), ("VT", self.VT), ("FG", self.FG), ("CAT", self.CAT),
                                         ("UT", self.UT), ("U8D", self.U8D), ("Y8D", self.Y8D))}
        sb = nc.alloc_sbuf_tensor
        self.ngs = sb("ngs", [128, c.DEPTH * 4 * c.KC], F32)
        self.ds = sb("ds", [128, NO * c.KC], F32)
        self.ones = sb("ones", [128, 128], BF16)
        p = Phase(nc)
        p.s.dma("sp", self.ngs[:, :], self.ng, writes=["ngs"])
        p.s.dma("sp", self.ds[:, :], self.s5d, writes=["ds"])
        p.s.add("dve", lambda e: e.memset(self.ones[:, :], 1.0), writes=["ones"])
        p.close()


    def gcol(self, layer, j, kc):
        i = (layer * 4 + j) * self.c.KC + kc
        return self.ngs[:, i:i + 1]

    def dkey(self, name, k, t0):
        return ("D", name, k, t0)

    def norm_bufs(self, p):
        c = self.c
        p.ch = [p.sb([128, c.TB], F32) for _ in range(4)]
        p.sq = [p.sb([128, c.NT], BF16) for _ in range(2)]
        p.rstd = p.sb([128, c.TB], F32)
        p.pS = p.ps()

    def rms_stats(self, p, src, sname, t0):
        c, s = self.c, p.s
        for n in range(3):
            n0 = n * c.NT
            for k in range(c.KC):
                slot = k % 4
                s.dma("sp", p.ch[slot][:, 0:c.NT], src[k * 128:(k + 1) * 128, t0 + n0:t0 + n0 + c.NT],
                      reads=[self.dkey(sname, k, t0)], writes=[("ch", slot)], semkey=("ch", slot))
                s.add("act", lambda e, slot=slot, k=k: e.activation(out=p.sq[k % 2][:, :], in_=p.ch[slot][:, 0:c.NT], func=AF.Square),
                      reads=[("ch", slot)], writes=[("sq", k % 2)])
                s.add("pe", lambda e, k=k: e.matmul(p.pS[:, 0:c.NT], lhsT=self.ones[:, :], rhs=p.sq[k % 2][:, :],
                                                    start=(k == 0), stop=(k == c.KC - 1)),
                      reads=[("sq", k % 2)], writes=["pS"])
            s.add("dve", lambda e, n0=n0: e.tensor_scalar(out=p.rstd[:, n0:n0 + c.NT], in0=p.pS[:, 0:c.NT],
                                                         scalar1=1.0 / c.D, scalar2=c.EPS, op0=ALU.mult, op1=ALU.add),
                  reads=["pS"], writes=[("rstd", n)])
            s.add("act", lambda e, n0=n0: e.activation(out=p.rstd[:, n0:n0 + c.NT], in_=p.rstd[:, n0:n0 + c.NT], func=AF.Sqrt),
                  reads=[("rstd", n)], writes=[("rstd", n)])
            s.add("dve", lambda e, n0=n0: e.reciprocal(out=p.rstd[:, n0:n0 + c.NT], in_=p.rstd[:, n0:n0 + c.NT]),
                  reads=[("rstd", n)], writes=[("rstd", n)])

    def norm_to_xa(self, p, src, sname, t0, layer, j):
        c, s = self.c, p.s
        self.rms_stats(p, src, sname, t0)
        rk = [("rstd", n) for n in range(3)]
        for k in range(c.KC):
            slot = k % 4
            s.dma("sp", p.ch[slot][:, :], src[k * 128:(k + 1) * 128, t0:t0 + c.TB],
                  reads=[self.dkey(sname, k, t0)], writes=[("ch", slot)], semkey=("ch", slot))
            s.add("dve", lambda e, slot=slot, k=k: e.scalar_tensor_tensor(out=p.xa[:, k, :], in0=p.ch[slot][:, :], scalar=self.gcol(layer, j, k),
                                                                         in1=p.rstd[:, :], op0=ALU.mult, op1=ALU.mult),
                  reads=[("ch", slot)] + rk, writes=[("xa", k)])

    def norm_residual(self, p, src, sname, base, bname, t0, layer, j):
        c, s = self.c, p.s
        self.rms_stats(p, src, sname, t0)
        rk = [("rstd", n) for n in range(3)]
        for k in range(c.KC):
            a, b = (2 * k) % 4, (2 * k + 1) % 4
            s.dma("sp", p.ch[a][:, :], src[k * 128:(k + 1) * 128, t0:t0 + c.TB], reads=[self.dkey(sname, k, t0)],
                  writes=[("ch", a)], semkey=("ch", a))
            s.dma("sp", p.ch[b][:, :], base[k * 128:(k + 1) * 128, t0:t0 + c.TB], reads=[self.dkey(bname, k, t0)],
                  writes=[("ch", b)], semkey=("ch", b))
            s.add("dve", lambda e, a=a, k=k: e.scalar_tensor_tensor(out=p.ch[a][:, :], in0=p.ch[a][:, :], scalar=self.gcol(layer, j, k),
                                                                   in1=p.rstd[:, :], op0=ALU.mult, op1=ALU.mult),
                  reads=[("ch", a)] + rk, writes=[("ch", a)])
            s.add("dve", lambda e, a=a, b=b: e.tensor_tensor(out=p.ch[b][:, :], in0=p.ch[b][:, :], in1=p.ch[a][:, :], op=ALU.add),
                  reads=[("ch", a), ("ch", b)], writes=[("ch", b)])
            s.dma("sp", base[k * 128:(k + 1) * 128, t0:t0 + c.TB], p.ch[b][:, :], reads=[("ch", b)],
                  writes=[self.dkey(bname, k, t0)], semkey=("st", b))

    def linear_fm(self, p, wv, c0, nchunks, xin, xkeys, nk, wbufs, wname, epi):
        c, s = self.c, p.s
        for m in range(nchunks):
            sl = m % 2
            s.dma("pool", wbufs[sl][:, :, :], wv[:, :, c0 + m * 128:c0 + (m + 1) * 128], writes=[(wname, sl)])
            pp, pk = (p.pA, "pA") if m % 2 == 0 else (p.pB, "pB")
            for k in range(nk):
                for n in range(3):
                    s.add("pe", lambda e, pp=pp, k=k, n=n, sl=sl: e.matmul(
                        pp[n][:, 0:c.NT], lhsT=wbufs[sl][:, k, :], rhs=xin[:, k, n * c.NT:(n + 1) * c.NT],
                        start=(k == 0), stop=(k == nk - 1)),
                        reads=[(wname, sl)] + xkeys, writes=[(pk, n)])
            epi(m, pp, pk)

    def ffn_phase(self, layer, t0):
        c = self.c
        p = Phase(self.nc)
        s = p.s
        self.norm_bufs(p)
        p.xa = p.sb([128, c.KC, c.TB], BF16)
        hid = p.sb([128, c.HC, c.TB], BF16)
        wA = [p.sb([128, c.KC, 256], BF16) for _ in range(2)]
        wB = [p.sb([128, c.KC, 256], BF16) for _ in range(2)]
        wD = [p.sb([128, c.HC, 128], BF16) for _ in range(2)]
        sg = [p.sb([128, c.TB], BF16) for _ in range(2)]
        p.pA = [p.ps() for _ in range(3)]
        p.pB = [p.ps() for _ in range(3)]
        self.norm_to_xa(p, self.H, "H", t0, layer, 2)
        xk = [("xa", k) for k in range(c.KC)]
        wgv = self.wg[layer].rearrange("(k p) m -> p k m", p=128)
        wuv = self.wu[layer].rearrange("(k p) m -> p k m", p=128)
        wdv = self.wd[layer].rearrange("(k p) m -> p k m", p=128)
        for cb in range(c.FFN // 256):
            sl = cb % 2
            s.dma("pool", wA[sl][:, :, :], wgv[:, :, cb * 256:(cb + 1) * 256], writes=[("wA", sl)])
            s.dma("pool", wB[sl][:, :, :], wuv[:, :, cb * 256:(cb + 1) * 256], writes=[("wB", sl)])
            for mi in range(2):
                j = cb * 2 + mi
                for wbuf, wkey, pp, pk in ((wA, "wA", p.pA, "pA"), (wB, "wB", p.pB, "pB")):
                    for k in range(c.KC):
                        for n in range(3):
                            s.add("pe", lambda e, wbuf=wbuf, pp=pp, k=k, n=n, mi=mi, sl=sl: e.matmul(
                                pp[n][:, 0:c.NT], lhsT=wbuf[sl][:, k, mi * 128:(mi + 1) * 128],
                                rhs=p.xa[:, k, n * c.NT:(n + 1) * c.NT], start=(k == 0), stop=(k == c.KC - 1)),
                                reads=[(wkey, sl)] + xk, writes=[(pk, n)])
                    for n in range(3):
                        if wkey == "wA":
                            s.add("act", lambda e, n=n, j=j: e.activation(out=sg[j % 2][:, n * c.NT:(n + 1) * c.NT], in_=p.pA[n][:, 0:c.NT], func=AF.Silu),
                                  reads=[("pA", n)], writes=[("sg", j % 2, n)])
                        else:
                            s.add("dve", lambda e, n=n, j=j: e.tensor_tensor(out=hid[:, j, n * c.NT:(n + 1) * c.NT], in0=p.pB[n][:, 0:c.NT],
                                                                            in1=sg[j % 2][:, n * c.NT:(n + 1) * c.NT], op=ALU.mult),
                                  reads=[("pB", n), ("sg", j % 2, n)], writes=[("hid", j)])
        hk = [("hid", j) for j in range(c.HC)]

        def epi(m, pp, pk):
            slot = m % 4
            for n in range(3):
                s.add("act", lambda e, n=n: e.activation(out=p.ch[slot][:, n * c.NT:(n + 1) * c.NT], in_=pp[n][:, 0:c.NT], func=AF.Copy),
                      reads=[(pk, n)], writes=[("ch", slot)])
            s.dma("sp", self.Fs[m * 128:(m + 1) * 128, t0:t0 + c.TB], p.ch[slot][:, :], reads=[("ch", slot)],
                  writes=[self.dkey("Fs", m, t0)], semkey=("st", slot))
        self.linear_fm(p, wdv, 0, c.KC, hid, hk, c.HC, wD, "wD", epi)
        self.norm_residual(p, self.Fs, "Fs", self.H, "H", t0, layer, 3)
        p.close()

    def inproj_phase(self, layer, t0):
        c = self.c
        e_i = layer // 2
        p = Phase(self.nc)
        s = p.s
        self.norm_bufs(p)
        p.xa = p.sb([128, c.KC, c.TB], BF16)
        wP = [p.sb([128, c.KC, 128], BF16) for _ in range(2)]
        wV = p.sb([128, c.KC, 512], BF16)
        wF = p.sb([128, c.KC, c.NH], BF16)
        stb = [p.sb([128, c.TB], BF16) for _ in range(2)]
        vst = [p.sb([128, 512], BF16) for _ in range(2)]
        fst = [p.sb([128, c.NH], F32) for _ in range(2)]
        p.pA = [p.ps() for _ in range(3)]
        p.pB = [p.ps() for _ in range(3)]
        pF = p.ps()
        self.norm_to_xa(p, self.H, "H", t0, layer, 0)
        xk = [("xa", k) for k in range(c.KC)]
        wv = self.win[e_i].rearrange("(k p) m -> p k m", p=128)
        targets = [(self.QT, 0, c.AC), (self.KT, c.AW, c.AC)] + [(self.GX[i], 3 * c.AW + c.NH + i * c.CW, c.CC) for i in range(3)]
        cnt = [0]
        for dst, c0, nch in targets:
            def epi(m, pp, pk, dst=dst):
                sl = cnt[0] % 2
                cnt[0] += 1
                for n in range(3):
                    s.add("act", lambda e, n=n, sl=sl: e.activation(out=stb[sl][:, n * c.NT:(n + 1) * c.NT], in_=pp[n][:, 0:c.NT], func=AF.Copy),
                          reads=[(pk, n)], writes=[("stb", sl)])
                s.dma("sp", dst[m * 128:(m + 1) * 128, t0:t0 + c.TB], stb[sl][:, :], reads=[("stb", sl)], writes=[("o", "fm")], semkey=("stb", sl))
            self.linear_fm(p, wv, c0, nch, p.xa, xk, c.KC, wP, "wP", epi)
        s.dma("pool", wF[:, :, :], wv[:, :, 3 * c.AW:3 * c.AW + c.NH], writes=["wF"])
        for vb in range(-(-c.AW // 512)):
            vw = min(512, c.AW - vb * 512)
            s.dma("pool", wV[:, :, 0:vw], wv[:, :, 2 * c.AW + vb * 512:2 * c.AW + vb * 512 + vw], writes=["wV"])
            for tt in range(c.TT):
                r0 = tt * 128
                rows = min(128, c.TB - r0)
                pp, pk = (p.pA[tt % 3], ("pA", tt % 3)) if (tt // 3) % 2 == 0 else (p.pB[tt % 3], ("pB", tt % 3))
                for k in range(c.KC):
                    s.add("pe", lambda e, pp=pp, k=k, r0=r0, rows=rows, vw=vw: e.matmul(
                        pp[0:rows, 0:vw], lhsT=p.xa[:, k, r0:r0 + rows], rhs=wV[:, k, 0:vw], start=(k == 0), stop=(k == c.KC - 1)),
                        reads=["wV"] + xk, writes=[pk])
                sl = tt % 2
                s.add("act", lambda e, pp=pp, rows=rows, vw=vw, sl=sl: e.activation(out=vst[sl][0:rows, 0:vw], in_=pp[0:rows, 0:vw], func=AF.Copy),
                      reads=[pk], writes=[("vst", sl)])
                s.dma("sp", self.VT[t0 + r0:t0 + r0 + rows, vb * 512:vb * 512 + vw], vst[sl][0:rows, 0:vw], reads=[("vst", sl)],
                      writes=[("o", "v")], semkey=("vst", sl))
                if vb == 0:
                    for k in range(c.KC):
                        s.add("pe", lambda e, k=k, r0=r0, rows=rows: e.matmul(
                            pF[0:rows, 0:c.NH], lhsT=p.xa[:, k, r0:r0 + rows], rhs=wF[:, k, :], start=(k == 0), stop=(k == c.KC - 1)),
                            reads=["wF"] + xk, writes=["pF"])
                    s.add("dve", lambda e, rows=rows, sl=sl: e.tensor_copy(out=fst[sl][0:rows, :], in_=pF[0:rows, 0:c.NH]),
                          reads=["pF"], writes=[("fst", sl)])
                    s.dma("sp", self.FG[t0 + r0:t0 + r0 + rows, :], fst[sl][0:rows, :], reads=[("fst", sl)], writes=[("o", "f")], semkey=("fst", sl))
        p.close()

    def conv_phase(self, layer):
        c = self.c
        e_i = layer // 2
        p = Phase(self.nc)
        s = p.s
        cwt = p.sb([128, c.CC * 4], F32)
        g = [[p.sb([128, c.LP], BF16) for _ in range(3)] for _ in range(2)]
        z = [p.sb([128, c.LP + 2], F32) for _ in range(2)]
        acc = [p.sb([128, c.LP], F32) for _ in range(2)]
        ob = [p.sb([128, c.LP], BF16) for _ in range(2)]
        s.dma("sp", cwt[:, :], self.cw[e_i], writes=["cwt"])
        for b in range(2):
            s.add("dve", lambda e, b=b: e.memset(z[b][:, 0:2], 0.0), writes=[("zpad", b)])
        for cc in range(c.CC):
            b = cc % 2
            for i in range(3):
                s.dma("sp", g[b][i][:, :], self.GX[i, cc * 128:(cc + 1) * 128, :], writes=[("g", b, i)])
            w = lambda j, cc=cc: cwt[:, cc * 4 + j:cc * 4 + j + 1]
            s.add("dve", lambda e, b=b: e.tensor_tensor(out=z[b][:, 2:], in0=g[b][1][:, :], in1=g[b][2][:, :], op=ALU.mult),
                  reads=[("g", b, 1), ("g", b, 2)], writes=[("z", b)])
            s.add("dve", lambda e, b=b, w=w: e.tensor_scalar(out=acc[b][:, :], in0=z[b][:, 2:], scalar1=w(2), scalar2=w(3), op0=ALU.mult, op1=ALU.add),
                  reads=[("z", b), "cwt"], writes=[("acc", b)])
            s.add("dve", lambda e, b=b, w=w: e.scalar_tensor_tensor(out=acc[b][:, :], in0=z[b][:, 1:c.LP + 1], scalar=w(1), in1=acc[b][:, :], op0=ALU.mult, op1=ALU.add),
                  reads=[("z", b), ("zpad", b), ("acc", b)], writes=[("acc", b)])
            s.add("dve", lambda e, b=b, w=w: e.scalar_tensor_tensor(out=acc[b][:, :], in0=z[b][:, 0:c.LP], scalar=w(0), in1=acc[b][:, :], op0=ALU.mult, op1=ALU.add),
                  reads=[("z", b), ("zpad", b), ("acc", b)], writes=[("acc", b)])
            s.add("dve", lambda e, b=b: e.tensor_tensor(out=ob[b][:, :], in0=acc[b][:, :], in1=g[b][0][:, :], op=ALU.mult),
                  reads=[("acc", b), ("g", b, 0)], writes=[("ob", b)])
            s.dma("sp", self.CAT[c.AW + cc * 128:c.AW + (cc + 1) * 128, :], ob[b][:, :], reads=[("ob", b)], writes=[("o", cc)], semkey=("ob", b))
        p.close()

    def attn_phase(self, layer):
        c = self.c
        e_i = layer // 2
        p = Phase(self.nc)
        s = p.s
        NKT, NQB, NH = c.NKT, c.NQB, c.NH
        cs = p.sb([128, 4 * 128 + 3 * 384], F32)
        msk = p.sb([128, 3, 384], BF16)
        tri, onesf, sel127, bsel = cs[:, 0:128], cs[:, 128:256], cs[:, 256:384], cs[:, 384:512]
        s.dma("sp", cs[:, :], self.cst, writes=["cs"])
        s.add("dve", lambda e: e.tensor_copy(out=msk[:, :, :], in_=cs[:, 512:512 + 1152].rearrange("p (j t) -> p j t", j=3)),
              reads=["cs"], writes=["msk"])
        bft = p.sb([128, NH], F32)
        lf = p.sb([128, NKT, NH], F32)
        tot = p.sb([128, NH], F32)
        cref = p.sb([128, 3, NH], F32)
        bias = [[p.sb([128, NKT, NH], F32) for _ in range(3)] for _ in range(2)]
        qt = [p.sb([128, c.LP], BF16) for _ in range(2)]
        kt_ = [p.sb([128, c.LP], BF16) for _ in range(2)]
        va = [p.sb([128, NKT, 192], BF16) for _ in range(2)]
        pt = [p.sb([128, 384], BF16) for _ in range(4)]
        rec = p.sb([128, 384], F32)
        recb = p.sb([128, 384], F32)
        ao = [p.sb([128, 384], BF16) for _ in range(2)]
        pS = [p.ps() for _ in range(3)]
        pO = [p.ps() for _ in range(2)]
        pY = p.ps()
        pC = p.ps()
        s.dma("sp", bft[:, :], self.bf[e_i], writes=["bft"])
        s.dma("sp", lf[:, :, :], self.FG.rearrange("(j p) h -> p j h", p=128), writes=["lf"])
        s.add("dve", lambda e: e.tensor_tensor(out=lf[:, :, :], in0=lf[:, :, :], in1=bft[:, :].unsqueeze(1).to_broadcast([128, NKT, NH]), op=ALU.add),
              reads=["lf", "bft"], writes=["lf"])
        s.add("act", lambda e: e.activation(out=lf[:, :, :], in_=lf[:, :, :], func=AF.Exp, scale=-1.0), reads=["lf"], writes=["lf"])
        s.add("act", lambda e: e.activation(out=lf[:, :, :], in_=lf[:, :, :], func=AF.Ln, bias=1.0), reads=["lf"], writes=["lf"])
        s.add("dve", lambda e: e.tensor_single_scalar(out=lf[:, :, :], in_=lf[:, :, :], scalar=-1.0, op=ALU.mult), reads=["lf"], writes=["lf"])
        s.add("dve", lambda e: e.memset(tot[:, :], 0.0), writes=["tot"])
        for j in range(NKT):
            s.add("pe", lambda e, j=j: e.matmul(pC[:, 0:NH], lhsT=tri, rhs=lf[:, j, :], start=True, stop=True), reads=["lf", ("lfj", j), "cs"], writes=["pC"])
            s.add("pe", lambda e, j=j: e.matmul(pC[:, NH:2 * NH], lhsT=onesf, rhs=lf[:, j, :], start=True, stop=True), reads=["lf", ("lfj", j), "cs"], writes=["pC2"])
            s.add("dve", lambda e, j=j: e.tensor_tensor(out=lf[:, j, :], in0=pC[:, 0:NH], in1=tot[:, :], op=ALU.add), reads=["pC", "tot"], writes=[("lfj", j)])
            s.add("dve", lambda e: e.tensor_tensor(out=tot[:, :], in0=pC[:, NH:2 * NH], in1=tot[:, :], op=ALU.add), reads=["pC2", "tot"], writes=["tot"])
        ckeys = [("lfj", j) for j in range(NKT)]
        for hp in range(c.AC):
            b = hp % 2
            s.dma("sp", qt[b][:, :], self.QT[hp * 128:(hp + 1) * 128, :], writes=[("qt", b)])
            s.dma("sp", kt_[b][:, :], self.KT[hp * 128:(hp + 1) * 128, :], writes=[("kt", b)])
            if hp < 2:
                s.add("dve", lambda e, b=b: e.memset(va[b][:, :, 64:128], 0.0), writes=[("vaE", b)])
                s.add("dve", lambda e, b=b: e.memset(va[b][:, :, 64:65], 1.0), reads=[("vaE", b)], writes=[("vaE", b)])
            vsrc = self.VT.rearrange("(j p) w -> p j w", p=128)
            s.dma("sp", va[b][:, :, 0:64], vsrc[:, :, hp * 128:hp * 128 + 64], writes=[("vaA", b)])
            s.dma("sp", va[b][:, :, 128:192], vsrc[:, :, hp * 128 + 64:hp * 128 + 128], writes=[("vaB", b)])
            for qb in range(NQB):
                q0 = qb * 384
                nkt = 3 * qb + 3
                bb = bias[qb % 2]
                for j2 in range(3):
                    nk2 = 3 * qb + j2 + 1
                    s.add("pe", lambda e, qb=qb, j2=j2: e.matmul(pC[:, (2 + j2) * NH:(3 + j2) * NH], lhsT=sel127, rhs=lf[:, 3 * qb + j2, :], start=True, stop=True),
                          reads=ckeys + ["cs"], writes=[("pC3", j2)])
                    s.add("dve", lambda e, j2=j2: e.tensor_copy(out=cref[:, j2, :], in_=pC[:, (2 + j2) * NH:(3 + j2) * NH]), reads=[("pC3", j2)], writes=[("cref", j2)])
                    s.add("dve", lambda e, bb=bb, j2=j2, nk2=nk2: e.tensor_tensor(out=bb[j2][:, 0:nk2, :], in0=cref[:, j2, :].unsqueeze(1).to_broadcast([128, nk2, NH]),
                                                                            in1=lf[:, 0:nk2, :], op=ALU.subtract),
                          reads=[("cref", j2)] + ckeys, writes=[("bias", qb % 2, j2)])
                for hh in range(2):
                    h = hp * 2 + hh
                    r0 = hh * 64
                    lcols = slice(0, 128) if hh == 0 else slice(64, 192)
                    LA = 2

                    def emit_scores(kt, r0=r0, b=b, q0=q0):
                        sp_ = pS[kt % 3]
                        s.add("pe", lambda e, sp_=sp_, kt=kt: e.matmul(
                            sp_[:, 0:384], lhsT=kt_[b][r0:r0 + 64, kt * 128:(kt + 1) * 128], rhs=qt[b][r0:r0 + 64, q0:q0 + 384], start=True, stop=True),
                            reads=[("qt", b), ("kt", b)], writes=[("pS", kt % 3)])

                    def emit_softmax_pv(kt, hh=hh, h=h, b=b, bb=bb, qb=qb, nkt=nkt, lcols=lcols):
                        sp_ = pS[kt % 3]
                        pb = pt[kt % 4]
                        jd = kt - 3 * qb
                        if jd > 0:
                            s.add("dve", lambda e, pb=pb, jd=jd: e.memset(pb[:, 0:128 * jd], 0.0), reads=[], writes=[("pt", kt % 4)])
                        for j2 in range(max(jd, 0), 3):
                            s.add("act", lambda e, sp_=sp_, pb=pb, kt=kt, j2=j2: e.activation(
                                out=pb[:, 128 * j2:128 * (j2 + 1)], in_=sp_[:, 128 * j2:128 * (j2 + 1)], func=AF.Exp, bias=bb[j2][:, kt, h:h + 1], scale=0.125),
                                reads=[("pS", kt % 3), ("bias", qb % 2, j2)], writes=[("pt", kt % 4)])
                        if kt >= 3 * qb:
                            s.add("dve", lambda e, pb=pb, kt=kt: e.tensor_tensor(out=pb[:, :], in0=pb[:, :], in1=msk[:, kt - 3 * qb, :], op=ALU.mult),
                                  reads=[("pt", kt % 4), "msk"], writes=[("pt", kt % 4)])
                        s.add("pe", lambda e, pb=pb, kt=kt: e.matmul(
                            pO[hh][:, 0:384], lhsT=va[b][:, kt, lcols], rhs=pb[:, :], start=(kt == 0), stop=(kt == nkt - 1)),
                            reads=[("pt", kt % 4), ("vaA", b), ("vaB", b), ("vaE", b)], writes=[("pO", hh)])

                    for i in range(nkt + LA):
                        if i < nkt:
                            emit_scores(i)
                        if i - LA >= 0:
                            emit_softmax_pv(i - LA)
                s.add("dve", lambda e: e.reciprocal(out=rec[64:65, :], in_=pO[0][64:65, 0:384]), reads=[("pO", 0)], writes=["recA"])
                s.add("dve", lambda e: e.reciprocal(out=rec[0:1, :], in_=pO[1][0:1, 0:384]), reads=[("pO", 1)], writes=["recB"])
                s.add("pe", lambda e: e.matmul(pY[:, 0:384], lhsT=bsel[64:65, :], rhs=rec[64:65, :], start=True, stop=False),
                      reads=["recA", "cs"], writes=["pY"])
                s.add("pe", lambda e: e.matmul(pY[:, 0:384], lhsT=bsel[0:1, :], rhs=rec[0:1, :], start=False, stop=True),
                      reads=["recB", "cs"], writes=["pY"])
                s.add("act", lambda e: e.activation(out=recb[:, :], in_=pY[:, 0:384], func=AF.Copy), reads=["pY"], writes=["recb"])
                a_ = ao[qb % 2]
                s.add("dve", lambda e, a_=a_: e.tensor_tensor(out=a_[0:64, :], in0=pO[0][0:64, 0:384], in1=recb[0:64, :], op=ALU.mult),
                      reads=[("pO", 0), "recb"], writes=[("ao", qb % 2, 0)])
                s.add("dve", lambda e, a_=a_: e.tensor_tensor(out=a_[64:128, :], in0=pO[1][64:128, 0:384], in1=recb[64:128, :], op=ALU.mult),
                      reads=[("pO", 1), "recb"], writes=[("ao", qb % 2, 1)])
                s.dma("sp", self.CAT[hp * 128:(hp + 1) * 128, q0:q0 + 384], a_[:, :], reads=[("ao", qb % 2, 0), ("ao", qb % 2, 1)],
                      writes=[("o", hp, qb)], semkey=("ao", qb % 2))
        p.close()

    def outproj_phase(self, layer, t0, wmat, src):
        c = self.c
        p = Phase(self.nc)
        s = p.s
        self.norm_bufs(p)
        xin = p.sb([128, c.KC, c.TB], BF16)
        wP = [p.sb([128, c.KC, 128], BF16) for _ in range(2)]
        p.pA = [p.ps() for _ in range(3)]
        p.pB = [p.ps() for _ in range(3)]
        for k in range(c.KC):
            s.dma("sp", xin[:, k, :], src[k * 128:(k + 1) * 128, t0:t0 + c.TB], writes=[("xin", k)], semkey=("xin", k % 4))
        xk = [("xin", k) for k in range(c.KC)]

        def epi(m, pp, pk):
            slot = m % 4
            for n in range(3):
                s.add("act", lambda e, n=n: e.activation(out=p.ch[slot][:, n * c.NT:(n + 1) * c.NT], in_=pp[n][:, 0:c.NT], func=AF.Copy),
                      reads=[(pk, n)], writes=[("ch", slot)])
            s.dma("sp", self.Fs[m * 128:(m + 1) * 128, t0:t0 + c.TB], p.ch[slot][:, :], reads=[("ch", slot)],
                  writes=[self.dkey("Fs", m, t0)], semkey=("st", slot))
        self.linear_fm(p, wmat.rearrange("(k p) m -> p k m", p=128), 0, c.KC, xin, xk, c.KC, wP, "wP", epi)
        self.norm_residual(p, self.Fs, "Fs", self.H, "H", t0, layer, 1)
        p.close()


    def s5_norm_phase(self, layer, t0):
        c = self.c
        p = Phase(self.nc)
        s = p.s
        self.norm_bufs(p)
        p.xa = p.sb([128, c.KC, c.TB], BF16)
        xs = [p.sb([128, 8, c.TBC], BF16) for _ in range(2)]
        self.norm_to_xa(p, self.H, "H", t0, layer, 0)
        n0 = t0 // 8
        for k in range(c.KC):
            b = k % 2
            s.dma("sp", self.UT[k * 128:(k + 1) * 128, t0:t0 + c.TB], p.xa[:, k, :], reads=[("xa", k)], writes=[("o", "ut", k)], semkey=("ut", b))
            s.add("pool", lambda e, k=k, b=b: e.tensor_copy(out=xs[b][:, :, :], in_=p.xa[:, k, :].rearrange("p (n s) -> p s n", s=8)),
                  reads=[("xa", k)], writes=[("xs", b)])
            s.dma("sp", self.U8D[k, :, :, n0:n0 + c.TBC].rearrange("s p n -> p s n"), xs[b][:, :, :], reads=[("xs", b)], writes=[("o", "u8", k)], semkey=("xs", b))
        p.close()

    def s5_core_phase(self, layer, gb):
        c = self.c
        o_i = layer // 2
        GB, NCH = c.GB, c.NCH
        NH2 = NCH // 2
        g0 = gb * GB
        p = Phase(self.nc)
        s = p.s
        cnt = [0]

        def op(eng, fn, reads, writes):
            s.add(eng, fn, reads=reads, writes=writes)

        def T(shape, dtype=F32):
            return p.sb(shape, dtype)

        def tt(out, a, b, alu, eng="dve"):
            op(eng, lambda e: e.tensor_tensor(out=out[0], in0=a[0], in1=b[0], op=alu), [a[1], b[1]], [out[1]])

        def key(t, name):
            return (t, name)

        def new(name, shape=None, dtype=F32):
            t = T([64] + list(shape or [GB]), dtype)
            return (t[:], name)

        prm = T([64, 3, c.G])
        bt = T([64, 2, GB, 16])
        ct = T([64, 2, GB, 16])
        c2 = T([128, 256])
        tmask = T([128, 128], BF16)
        ident = T([128, 128], BF16)
        s.dma("sp", prm[:, :, :], self.s5p[o_i], writes=["prm"])
        s.dma("sp", bt[:, :, :, :], self.s5b[o_i, :, :, g0:g0 + GB, :], writes=["bt"])
        s.dma("sp", ct[:, :, :, :], self.s5c[o_i, :, :, g0:g0 + GB, :], writes=["ct"])
        s.dma("sp", c2[:, :], self.cst2, writes=["c2"])
        op("dve", lambda e: e.tensor_copy(out=tmask[:, :], in_=c2[:, 0:128]), ["c2"], ["tmask"])
        op("dve", lambda e: e.tensor_copy(out=ident[:, :], in_=c2[:, 128:256]), ["c2"], ["ident"])
        are = (prm[:, 0, g0:g0 + GB], "prm")
        aim = (prm[:, 1, g0:g0 + GB], "prm")
        lst = (prm[:, 2, g0:g0 + GB], "prm")
        lre, dl, ex, mag, ang = new("lre"), new("dl"), new("ex"), new("mag"), new("ang")
        op("dve", lambda e: e.tensor_scalar_min(out=lre[0], in0=are[0], scalar1=-1e-4), ["prm"], ["lre"])
        op("act", lambda e: e.activation(out=dl[0], in_=lst[0], func=AF.Exp), ["prm"], ["dl"])
        tt(ex, lre, dl, ALU.mult)
        op("act", lambda e: e.activation(out=mag[0], in_=ex[0], func=AF.Exp), ["ex"], ["mag"])
        tt(ang, aim, dl, ALU.mult)

        def sin_of(name, src, shift):
            r, m = new(name + "r"), new(name + "m")
            op("dve", lambda e: e.tensor_scalar_add(out=r[0], in0=src[0], scalar1=shift), [src[1]], [r[1]])
            for _ in range(4):
                op("dve", lambda e: e.tensor_single_scalar(out=m[0], in_=r[0], scalar=math.pi, op=ALU.is_gt), [r[1]], [m[1]])
                op("dve", lambda e: e.scalar_tensor_tensor(out=r[0], in0=m[0], scalar=-2.0 * math.pi, in1=r[0], op0=ALU.mult, op1=ALU.add), [m[1], r[1]], [r[1]])
            o = new(name)
            op("act", lambda e: e.activation(out=o[0], in_=r[0], func=AF.Sin), [r[1]], [o[1]])
            return o
        sn = sin_of("sn", ang, 0.0)
        cs_ = sin_of("cs", ang, 0.5 * math.pi)
        lbr, lbi = new("lbr"), new("lbi")
        tt(lbr, mag, cs_, ALU.mult)
        tt(lbi, mag, sn, ALU.mult)
        den, t1, t2, nr, cr, ci = new("den"), new("t1"), new("t2"), new("nr"), new("cr"), new("ci")
        tt(den, lre, lre, ALU.mult)
        tt(t1, aim, aim, ALU.mult)
        tt(den, den, t1, ALU.add)
        op("dve", lambda e: e.reciprocal(out=den[0], in_=den[0]), ["den"], ["den"])
        op("dve", lambda e: e.tensor_scalar_add(out=nr[0], in0=lbr[0], scalar1=-1.0), ["lbr"], ["nr"])
        tt(t1, nr, lre, ALU.mult)
        tt(t2, lbi, aim, ALU.mult)
        tt(cr, t1, t2, ALU.add)
        tt(cr, cr, den, ALU.mult)
        tt(t1, lbi, lre, ALU.mult)
        tt(t2, nr, aim, ALU.mult)
        tt(ci, t1, t2, ALU.subtract)
        tt(ci, ci, den, ALU.mult)
        bbr, bbi, u1, u2 = new("bbr", [GB, 16]), new("bbi", [GB, 16]), new("u1", [GB, 16]), new("u2", [GB, 16])
        bc16 = lambda t: (t[0].unsqueeze(2).to_broadcast([64, GB, 16]), t[1])
        b_re, b_im = (bt[:, 0, :, :], "bt"), (bt[:, 1, :, :], "bt")
        c_re, c_im = (ct[:, 0, :, :], "ct"), (ct[:, 1, :, :], "ct")

        def cmul(outr, outi, ar, ai, br, bi, neg_im=False):
            tt(u1, ar, br, ALU.mult)
            tt(u2, ai, bi, ALU.mult)
            tt(outr, u1, u2, ALU.subtract)
            tt(u1, ar, bi, ALU.mult)
            tt(u2, ai, br, ALU.mult)
            if neg_im:
                tt(outi, u1, u2, ALU.add)
                op("dve", lambda e: e.tensor_single_scalar(out=outi[0], in_=outi[0], scalar=-1.0, op=ALU.mult), [outi[1]], [outi[1]])
            else:
                tt(outi, u1, u2, ALU.add)
        cmul(bbr, bbi, bc16(cr), bc16(ci), b_re, b_im)
        v1, v2 = new("v1"), new("v2")
        Pw = [(new("p0r"), new("p0i"))]
        op("dve", lambda e: e.memset(Pw[0][0][0], 1.0), [], ["p0r"])
        op("dve", lambda e: e.memset(Pw[0][1][0], 0.0), [], ["p0i"])
        for k in range(1, 9):
            pr, pi = new("p%dr" % k), new("p%di" % k)
            a_r, a_i = Pw[k - 1]
            tt(v1, a_r, lbr, ALU.mult); tt(v2, a_i, lbi, ALU.mult); tt(pr, v1, v2, ALU.subtract)
            tt(v1, a_r, lbi, ALU.mult); tt(v2, a_i, lbr, ALU.mult); tt(pi, v1, v2, ALU.add)
            Pw.append((pr, pi))
        m8, i8r, i8i = new("m8"), new("i8r"), new("i8i")
        tt(m8, Pw[8][0], Pw[8][0], ALU.mult); tt(v1, Pw[8][1], Pw[8][1], ALU.mult); tt(m8, m8, v1, ALU.add)
        op("dve", lambda e: e.reciprocal(out=m8[0], in_=m8[0]), ["m8"], ["m8"])
        tt(i8r, Pw[8][0], m8, ALU.mult)
        tt(i8i, Pw[8][1], m8, ALU.mult)
        op("dve", lambda e: e.tensor_single_scalar(out=i8i[0], in_=i8i[0], scalar=-1.0, op=ALU.mult), ["i8i"], ["i8i"])
        Rj = []
        for j in range(8):
            rr, ri = new("r%dr" % j), new("r%di" % j)
            a_r, a_i = Pw[j + 1]
            tt(v1, a_r, i8r, ALU.mult); tt(v2, a_i, i8i, ALU.mult); tt(rr, v1, v2, ALU.subtract)
            tt(v1, a_r, i8i, ALU.mult); tt(v2, a_i, i8r, ALU.mult); tt(ri, v1, v2, ALU.add)
            Rj.append((rr, ri))
        A1r, A1i = T([64, GB, 8, 16], BF16), T([64, GB, 8, 16], BF16)
        W2r, W2n = T([64, GB, 8, 16], BF16), T([64, GB, 8, 16], BF16)
        Wsr, Wsn = T([64, GB, 8, 16], BF16), T([64, GB, 8, 16], BF16)
        fr, fi = new("fr", [GB, 16]), new("fi", [GB, 16])
        for q in range(8):
            cmul(fr, fi, bc16(Pw[7 - q][0]), bc16(Pw[7 - q][1]), bbr, bbi)
            op("act", lambda e, q=q: e.activation(out=A1r[:, :, q, :], in_=fr[0], func=AF.Copy), ["fr"], [("A1r", q)])
            op("act", lambda e, q=q: e.activation(out=A1i[:, :, q, :], in_=fi[0], func=AF.Copy), ["fi"], [("A1i", q)])
            cmul(fr, fi, bc16(Pw[q + 1][0]), bc16(Pw[q + 1][1]), c_re, c_im, neg_im=True)
            op("act", lambda e, q=q: e.activation(out=W2r[:, :, q, :], in_=fr[0], func=AF.Copy), ["fr"], [("W2r", q)])
            op("act", lambda e, q=q: e.activation(out=W2n[:, :, q, :], in_=fi[0], func=AF.Copy), ["fi"], [("W2n", q)])
            cmul(fr, fi, bc16(Rj[q][0]), bc16(Rj[q][1]), c_re, c_im, neg_im=True)
            op("act", lambda e, q=q: e.activation(out=Wsr[:, :, q, :], in_=fr[0], func=AF.Copy), ["fr"], [("Wsr", q)])
            op("act", lambda e, q=q: e.activation(out=Wsn[:, :, q, :], in_=fi[0], func=AF.Copy), ["fi"], [("Wsn", q)])
        allk = lambda nm: [(nm, q) for q in range(8)]
        flat = lambda t, gi: t[:, gi, :, :].rearrange("p s h -> p (s h)")
        Tb = T([128, GB, 128], BF16)
        W1b = T([128, GB, 128], BF16)
        pT = [p.ps() for _ in range(2)]
        pW = [p.ps([128, 128], BF16) for _ in range(2)]
        for gi in range(GB):
            b = gi % 2
            op("pe", lambda e, gi=gi, b=b: e.matmul(pT[b][:, 0:128], lhsT=flat(A1r, gi), rhs=flat(Wsr, gi), start=True, stop=False),
               allk("A1r") + allk("Wsr"), [("pT", b)])
            op("pe", lambda e, gi=gi, b=b: e.matmul(pT[b][:, 0:128], lhsT=flat(A1i, gi), rhs=flat(Wsn, gi), start=False, stop=True),
               allk("A1i") + allk("Wsn"), [("pT", b)])
            op("dve", lambda e, gi=gi, b=b: e.tensor_tensor(out=Tb[:, gi, :], in0=pT[b][:, 0:128], in1=tmask[:, :], op=ALU.mult),
               [("pT", b), "tmask"], [("Tb", gi)])
            op("pe", lambda e, gi=gi, b=b: e.transpose(pW[b][:, 0:64], flat(A1r, gi), ident[0:64, 0:64]), allk("A1r") + ["ident"], [("pW", b, 0)])
            op("pe", lambda e, gi=gi, b=b: e.transpose(pW[b][:, 64:128], flat(A1i, gi), ident[0:64, 0:64]), allk("A1i") + ["ident"], [("pW", b, 1)])
            op("act", lambda e, gi=gi, b=b: e.activation(out=W1b[:, gi, :], in_=pW[b][:, :], func=AF.Copy), [("pW", b, 0), ("pW", b, 1)], [("W1b", gi)])
        u8 = T([128, GB, NCH], BF16)
        for kl in range(GB // 8):
            kc = g0 // 8 + kl
            for q in range(8):
                s.dma("sp", u8[q * 16:(q + 1) * 16, kl * 8:(kl + 1) * 8, :], self.U8D[kc, q, :, :].rearrange("(gl h) n -> h gl n", h=16),
                      writes=[("u8", kl, q)], semkey=("u8", q % 4))
        u8k = [("u8", kl, q) for kl in range(GB // 8) for q in range(8)]
        Vb = T([64, 2, GB, NCH], BF16)
        Xb = T([64, 2, GB, NCH + 1], BF16)
        pV = [p.ps() for _ in range(2)]
        for gi in range(GB):
            for hf in range(2):
                for cc in range(2):
                    b = cc
                    op("pe", lambda e, gi=gi, hf=hf, cc=cc, b=b: e.matmul(pV[b][0:64, 0:NH2], lhsT=W1b[:, gi, cc * 64:(cc + 1) * 64],
                                                                       rhs=u8[:, gi, hf * NH2:(hf + 1) * NH2], start=True, stop=True),
                       [("W1b", gi)] + u8k, [("pV", b)])
                    op("act", lambda e, gi=gi, hf=hf, cc=cc, b=b: e.activation(out=Vb[:, cc, gi, hf * NH2:(hf + 1) * NH2], in_=pV[b][0:64, 0:NH2], func=AF.Copy),
                       [("pV", b)], [("Vb", gi)])
        st, w1, w2, cb = T([64, 2, GB]), T([64, 2, GB]), T([64, 2, GB]), T([64, 2, GB])
        A8r, A8i = Pw[8]
        a8b = A8r[0].unsqueeze(1).to_broadcast([64, 2, GB])
        op("dve", lambda e: e.tensor_single_scalar(out=cb[:, 0, :], in_=A8i[0], scalar=-1.0, op=ALU.mult), ["p8i"], ["cb0"])
        op("dve", lambda e: e.tensor_copy(out=cb[:, 1, :], in_=A8i[0]), ["p8i"], ["cb1"])
        op("dve", lambda e: e.memset(st[:, :, :], 0.0), [], ["st"])
        op("dve", lambda e: e.memset(Xb[:, :, :, 0:1], 0.0), [], [("Xb", 0)])
        vk = [("Vb", gi) for gi in range(GB)]
        for n in range(NCH):
            op("dve", lambda e: e.tensor_tensor(out=w1[:, :, :], in0=st[:, :, :], in1=a8b, op=ALU.mult), ["st", "p8r"], ["w1"])
            op("dve", lambda e: e.tensor_tensor(out=w2[:, :, :], in0=st[:, ::-1, :], in1=cb[:, :, :], op=ALU.mult), ["st", "cb0", "cb1"], ["w2"])
            op("dve", lambda e: e.tensor_tensor(out=w1[:, :, :], in0=w1[:, :, :], in1=w2[:, :, :], op=ALU.add), ["w1", "w2"], ["w1"])
            op("dve", lambda e, n=n: e.tensor_tensor(out=st[:, :, :], in0=w1[:, :, :], in1=Vb[:, :, :, n], op=ALU.add), ["w1"] + (vk if n == 0 else []), ["st"])
            op("dve", lambda e, n=n: e.tensor_copy(out=Xb[:, :, :, n + 1], in_=st[:, :, :]), ["st"], [("Xb", n + 1)])
        xk = [("Xb", NCH)]
        y8 = T([128, GB, NCH], BF16)
        pY = [p.ps() for _ in range(2)]
        for gi in range(GB):
            for hf in range(2):
                b = hf
                sl = slice(hf * NH2, (hf + 1) * NH2)
                op("pe", lambda e, gi=gi, sl=sl, b=b: e.matmul(pY[b][:, 0:NH2], lhsT=Tb[:, gi, :], rhs=u8[:, gi, sl], start=True, stop=False),
                   [("Tb", gi)] + u8k, [("pY", b)])
                op("pe", lambda e, gi=gi, sl=sl, b=b: e.matmul(pY[b][:, 0:NH2], lhsT=flat(W2r, gi), rhs=Xb[:, 0, gi, sl], start=False, stop=False),
                   allk("W2r") + xk, [("pY", b)])
                op("pe", lambda e, gi=gi, sl=sl, b=b: e.matmul(pY[b][:, 0:NH2], lhsT=flat(W2n, gi), rhs=Xb[:, 1, gi, sl], start=False, stop=True),
                   allk("W2n") + xk, [("pY", b)])
                op("act", lambda e, gi=gi, sl=sl, b=b: e.activation(out=y8[:, gi, sl], in_=pY[b][:, 0:NH2], func=AF.Copy), [("pY", b)], [("y8", gi)])
        yk = [("y8", gi) for gi in range(GB)]
        for kl in range(GB // 8):
            kc = g0 // 8 + kl
            for q in range(8):
                s.dma("sp", self.Y8D[kc, q, :, :].rearrange("(gl h) n -> h gl n", h=16), y8[q * 16:(q + 1) * 16, kl * 8:(kl + 1) * 8, :],
                      reads=yk, writes=[("o", "y8", kl, q)], semkey=("y8o", q % 4))
        p.close()

    def s5_out_phase(self, layer, t0):
        c = self.c
        o_i = layer // 2
        p = Phase(self.nc)
        s = p.s
        self.norm_bufs(p)
        ga = p.sb([128, c.KC, c.TB], BF16)
        yt = [p.sb([128, 8, c.TBC], BF16) for _ in range(2)]
        ut = [p.sb([128, c.TB], BF16) for _ in range(2)]
        yv = [p.sb([128, c.TB], F32) for _ in range(2)]
        w_ = [p.sb([128, c.TB], F32) for _ in range(2)]
        wA = [p.sb([128, c.KC, 128], BF16) for _ in range(2)]
        wB = [p.sb([128, c.KC, 128], BF16) for _ in range(2)]
        sg = [p.sb([128, c.TB], BF16) for _ in range(2)]
        p.pA = [p.ps() for _ in range(3)]
        p.pB = [p.ps() for _ in range(3)]
        n0 = t0 // 8
        v3 = lambda ap: ap.rearrange("p (n j) -> p n j", j=8)
        for k in range(c.KC):
            b = k % 2
            s.dma("sp", yt[b][:, :, :], self.Y8D[k, :, :, n0:n0 + c.TBC].rearrange("j p n -> p j n"), writes=[("yt", b)])
            s.dma("sp", ut[b][:, :], self.UT[k * 128:(k + 1) * 128, t0:t0 + c.TB], writes=[("ut", b)])
            dcol = self.ds[:, o_i * c.KC + k:o_i * c.KC + k + 1]
            s.add("dve", lambda e, b=b, dcol=dcol: e.scalar_tensor_tensor(out=v3(yv[b][:, :]), in0=v3(ut[b][:, :]), scalar=dcol,
                                                                       in1=yt[b][:, :, :].rearrange("p j n -> p n j"), op0=ALU.mult, op1=ALU.add),
                  reads=[("yt", b), ("ut", b)], writes=[("yv", b)])
            s.add("dve", lambda e, b=b: e.tensor_tensor(out=w_[b][:, :], in0=yv[b][:, :], in1=yv[b][:, :], op=ALU.mult), reads=[("yv", b)], writes=[("w", b)])
            s.add("dve", lambda e, b=b: e.tensor_scalar(out=w_[b][:, :], in0=w_[b][:, :], scalar1=0.044715, scalar2=1.0, op0=ALU.mult, op1=ALU.add),
                  reads=[("w", b)], writes=[("w", b)])
            s.add("dve", lambda e, b=b: e.tensor_tensor(out=w_[b][:, :], in0=w_[b][:, :], in1=yv[b][:, :], op=ALU.mult), reads=[("w", b), ("yv", b)], writes=[("w", b)])
            s.add("act", lambda e, b=b: e.activation(out=w_[b][:, :], in_=w_[b][:, :], func=AF.Sigmoid, scale=1.5957691216057308), reads=[("w", b)], writes=[("w", b)])
            s.add("dve", lambda e, b=b, k=k: e.tensor_tensor(out=ga[:, k, :], in0=w_[b][:, :], in1=yv[b][:, :], op=ALU.mult), reads=[("w", b), ("yv", b)], writes=[("ga", k)])
        gk = [("ga", k) for k in range(c.KC)]
        w1v = self.wq1[o_i].rearrange("(k p) m -> p k m", p=128)
        w2v = self.wq2[o_i].rearrange("(k p) m -> p k m", p=128)
        for m in range(c.KC):
            sl = m % 2
            s.dma("pool", wA[sl][:, :, :], w2v[:, :, m * 128:(m + 1) * 128], writes=[("wA", sl)])
            s.dma("pool", wB[sl][:, :, :], w1v[:, :, m * 128:(m + 1) * 128], writes=[("wB", sl)])
            for wbuf, wkey, pp, pk in ((wA, "wA", p.pA, "pA"), (wB, "wB", p.pB, "pB")):
                for k in range(c.KC):
                    for n in range(3):
                        s.add("pe", lambda e, wbuf=wbuf, pp=pp, k=k, n=n, sl=sl: e.matmul(
                            pp[n][:, 0:c.NT], lhsT=wbuf[sl][:, k, :], rhs=ga[:, k, n * c.NT:(n + 1) * c.NT], start=(k == 0), stop=(k == c.KC - 1)),
                            reads=[(wkey, sl)] + gk, writes=[(pk, n)])
                slot = m % 4
                for n in range(3):
                    if wkey == "wA":
                        s.add("act", lambda e, n=n, m=m: e.activation(out=sg[m % 2][:, n * c.NT:(n + 1) * c.NT], in_=p.pA[n][:, 0:c.NT], func=AF.Sigmoid),
                              reads=[("pA", n)], writes=[("sg", m % 2, n)])
                    else:
                        s.add("dve", lambda e, n=n, m=m, slot=slot: e.tensor_tensor(out=p.ch[slot][:, n * c.NT:(n + 1) * c.NT], in0=p.pB[n][:, 0:c.NT],
                                                                                in1=sg[m % 2][:, n * c.NT:(n + 1) * c.NT], op=ALU.mult),
                              reads=[("pB", n), ("sg", m % 2, n)], writes=[("ch", slot)])
            s.dma("sp", self.Fs[m * 128:(m + 1) * 128, t0:t0 + c.TB], p.ch[m % 4][:, :], reads=[("ch", m % 4)],
                  writes=[self.dkey("Fs", m, t0)], semkey=("st", m % 4))
        self.norm_residual(p, self.Fs, "Fs", self.H, "H", t0, layer, 1)
        p.close()

    def build(self):
        c = self.c
        for q in range(c.NSEQ):
            p = Phase(self.nc)
            for k in range(c.KC):
                p.s.dma("sp", self.H[k * 128:(k + 1) * 128, :], self.xT[q, k * 128:(k + 1) * 128, :], writes=[("h", k)], semkey=("hinit",))
            p.close()
            for layer in range(c.DEPTH):
                if layer % 2 == 0 and "even" in self.stages:
                    for b in range(c.NB):
                        self.inproj_phase(layer, b * c.TB)
                    self.conv_phase(layer)
                    self.attn_phase(layer)
                    if self.dbg and layer == 0:
                        p = Phase(self.nc)
                        for i, (n, o) in enumerate(self.dbg_out.items()):
                            p.s.dma("sp", o, getattr(self, n), writes=[("dbg", i)], semkey=("dbg", i))
                        p.close()
                    for b in range(c.NB):
                        self.outproj_phase(layer, b * c.TB, self.wo[layer // 2], self.CAT)
                if layer % 2 == 1 and "s5" in self.stages:
                    for b in range(c.NB):
                        self.s5_norm_phase(layer, b * c.TB)
                    for gb in range(c.G // c.GB):
                        self.s5_core_phase(layer, gb)
                    if self.dbg and layer == 1:
                        p = Phase(self.nc)
                        for i, (n, o) in enumerate(self.dbg_out.items()):
                            p.s.dma("sp", o, getattr(self, n), writes=[("dbg", i)], semkey=("dbg", i))
                        p.close()
                    for b in range(c.NB):
                        self.s5_out_phase(layer, b * c.TB)
                if "ffn" in self.stages:
                    for b in range(c.NB):
                        self.ffn_phase(layer, b * c.TB)
                if self.dbg:
                    p = Phase(self.nc)
                    for k in range(c.KC):
                        p.s.dma("sp", self.hdump[q, layer, k * 128:(k + 1) * 128, :], self.H[k * 128:(k + 1) * 128, :], writes=[("hd", k)], semkey=("hd",))
                    p.close()
            p = Phase(self.nc)
            for k in range(c.KC):
                p.s.dma("sp", self.out[q, k * 128:(k + 1) * 128, :], self.H[k * 128:(k + 1) * 128, :], writes=[("o", k)], semkey=("hout",))
            p.close()
        return self.nc


N_META = 16
_DEBUG_DUMPS = False
_LAST = {}


def kernel(x, meta_tokens, norm_g, ab_w_in, ab_b_f, ab_conv_w, ab_conv_b, ab_w_o,
           s5_a_re, s5_a_im, s5_log_step, s5_b_re, s5_b_im, s5_c_re, s5_c_im,
           s5_d, s5_w_glu1, s5_w_glu2, ffn_w_gate, ffn_w_up, ffn_w_down):
    f = np.float32
    x = np.asarray(x, f)
    B, S, D = x.shape
    depth = norm_g.shape[0]
    cfg = Cfg(D=D, FFN=ffn_w_gate.shape[2], LP=4224, NT=352, NSEQ=1, DEPTH=depth)
    L = N_META + S
    assert L <= cfg.LP and cfg.PW == ab_w_in.shape[2]
    xT = np.zeros((B, D, cfg.LP), f)
    xT[:, :, :N_META] = np.asarray(meta_tokens, f).T[None]
    xT[:, :, N_META:L] = x.transpose(0, 2, 1)
    ngl = np.ascontiguousarray(np.asarray(norm_g, f).reshape(depth * 4, cfg.KC, 128).transpose(2, 0, 1).reshape(128, -1))
    NE = ab_w_in.shape[0]
    bfl = np.ascontiguousarray(np.broadcast_to(np.asarray(ab_b_f, f)[:, None, :], (NE, 128, cfg.NH)))
    cwl = np.zeros((NE, 128, cfg.CC * 4), f)
    cw3 = np.asarray(ab_conv_w, f).reshape(NE, 3, cfg.CC, 128)
    cwl.reshape(NE, 128, cfg.CC, 4)[:, :, :, :3] = cw3.transpose(0, 3, 2, 1)
    cwl.reshape(NE, 128, cfg.CC, 4)[:, :, :, 3] = np.asarray(ab_conv_b, f).reshape(NE, cfg.CC, 128).transpose(0, 2, 1)
    NO = s5_a_re.shape[0]
    G = cfg.G
    s5p = np.ascontiguousarray(np.stack([np.asarray(s5_a_re, f).transpose(0, 2, 1), np.asarray(s5_a_im, f).transpose(0, 2, 1),
                                         np.broadcast_to(np.asarray(s5_log_step, f)[:, None, :], (NO, 64, G))], 2))
    s5b = np.ascontiguousarray(np.stack([np.asarray(s5_b_re, f).transpose(0, 2, 1, 3), np.asarray(s5_b_im, f).transpose(0, 2, 1, 3)], 2))
    s5c = np.ascontiguousarray(np.stack([np.asarray(s5_c_re, f).transpose(0, 3, 1, 2), np.asarray(s5_c_im, f).transpose(0, 3, 1, 2)], 2))
    s5dl = np.ascontiguousarray(np.asarray(s5_d, f).reshape(NO, cfg.KC, 128).transpose(2, 0, 1).reshape(128, -1))
    ins = {"xT": xT, "ng": ngl, "wg": np.ascontiguousarray(ffn_w_gate, f), "wu": np.ascontiguousarray(ffn_w_up, f),
           "wd": np.ascontiguousarray(ffn_w_down, f), "win": np.ascontiguousarray(ab_w_in, f), "wo": np.ascontiguousarray(ab_w_o, f),
           "bf": bfl, "cw": cwl, "cst": host_constants(),
           "s5p": s5p, "s5b": s5b, "s5c": s5c, "s5d": s5dl, "wq1": np.ascontiguousarray(s5_w_glu1, f), "wq2": np.ascontiguousarray(s5_w_glu2, f),
           "cst2": host_constants2()}
    nc = Builder(cfg, stages=IMPLEMENTED_STAGES, dbg=_DEBUG_DUMPS).build()
    shared = {k: v for k, v in ins.items() if k != "xT"}
    in_maps = [dict(shared, xT=np.ascontiguousarray(xT[b:b + 1])) for b in range(B)]
    res = run_bass_kernel_spmd(nc, in_maps, core_ids=list(range(B)))
    out = np.concatenate([res.results[b]["out"] for b in range(B)], axis=0)
    if _DEBUG_DUMPS:
        _LAST.clear()
        _LAST.update(res.results[0])
    return np.ascontiguousarray(out[:, :, N_META:L].transpose(0, 2, 1)).astype(f)
```
